# Optimizing a Trainium2 kernel written in Bass

```python
import jax, jax.numpy as jnp
from jax import lax
import numpy as np

D_MODEL = 2048
BATCH = 1
SEQ = 16384
DEPTH = 1
DEC_BATCH = 4
DEC_SEQ = 4096
PAST_LEN = 128

N_META = 16
GRID_W = 64
NA_HEADS = 8
NA_HEAD_DIM = 128
NA_WIN_ROWS = 8
NA_WIN_COLS = 16
SW_Q_HEADS = 8
SW_KV_HEADS = 2
SW_HEAD_DIM = 128
SW_WINDOW = 128
SW_BLOCK = 128
D_FF = 5632
CONV_W = 3
LN_EPS = 1e-5
NEG_INF = -1e30
DEEPNORM_ALPHA = (2 * DEPTH) ** 0.25
DEEPNORM_BETA = (8 * DEPTH) ** -0.25
NA_WIDTH = NA_HEADS * NA_HEAD_DIM
SW_WIDTH = SW_Q_HEADS * SW_HEAD_DIM
SW_KV_WIDTH = SW_KV_HEADS * SW_HEAD_DIM
IN_SPLITS = (NA_WIDTH, NA_WIDTH, NA_WIDTH, SW_WIDTH, SW_KV_WIDTH, SW_KV_WIDTH, D_MODEL, D_MODEL)
IN_COLS = sum(IN_SPLITS)

kernel_name = 'hybrid_natten_swa_deepnorm_encoder'


def layer_norm(x, g, b):
    xf = x.astype(jnp.float32)
    mu = jnp.mean(xf, axis=-1, keepdims=True)
    var = jnp.mean(jnp.square(xf - mu), axis=-1, keepdims=True)
    y = (xf - mu) * lax.rsqrt(var + LN_EPS) * g.astype(jnp.float32) + b.astype(jnp.float32)
    return y.astype(x.dtype)


def neighbourhood_attention(q, k, v, rpb):
    B, L, H, Dh = q.shape
    T = L - N_META
    rows = T // GRID_W
    kh = min(NA_WIN_ROWS, rows)
    scale = Dh ** -0.5
    qm, km, vm = q[:, :N_META], k[:, :N_META], v[:, :N_META]
    qg = q[:, N_META:].reshape(B, rows, GRID_W, H, Dh)
    kg = k[:, N_META:].reshape(B, rows, GRID_W, H, Dh)
    vg = v[:, N_META:].reshape(B, rows, GRID_W, H, Dh)
    r = jnp.arange(rows)
    row_start = jnp.clip(r - kh // 2, 0, rows - kh)
    key_rows = row_start[:, None] + jnp.arange(kh)[None, :]
    kb = kg[:, key_rows]
    vb = vg[:, key_rows]
    c = jnp.arange(GRID_W)
    col_start = jnp.clip(c - NA_WIN_COLS // 2, 0, GRID_W - NA_WIN_COLS)
    col_in = (c[None, :] >= col_start[:, None]) & (c[None, :] < col_start[:, None] + NA_WIN_COLS)
    dr = key_rows - r[:, None] + (NA_WIN_ROWS - 1)
    dc = jnp.clip(c[None, :] - c[:, None], -(NA_WIN_COLS - 1), NA_WIN_COLS - 1) + (NA_WIN_COLS - 1)
    bias = rpb.astype(jnp.float32)[:, dr[:, None, :, None], dc[None, :, None, :]]
    bias = jnp.moveaxis(bias, 0, 1)
    bias = jnp.where(col_in[None, None, :, None, :], bias, NEG_INF)
    s_loc = jnp.einsum('brqhd,brkwhd->brhqkw', qg, kb, preferred_element_type=jnp.float32) * scale + bias[None]
    s_loc = s_loc.reshape(B, rows, H, GRID_W, kh * GRID_W)
    s_met = jnp.einsum('brqhd,bmhd->brhqm', qg, km, preferred_element_type=jnp.float32) * scale
    p = jax.nn.softmax(jnp.concatenate([s_loc, s_met], axis=-1), axis=-1).astype(v.dtype)
    p_loc = p[..., :kh * GRID_W].reshape(B, rows, H, GRID_W, kh, GRID_W)
    p_met = p[..., kh * GRID_W:]
    out = jnp.einsum('brhqkw,brkwhd->brqhd', p_loc, vb) + jnp.einsum('brhqm,bmhd->brqhd', p_met, vm)
    out = out.reshape(B, T, H, Dh)
    s_mm = jnp.einsum('bqhd,bmhd->bhqm', qm, km, preferred_element_type=jnp.float32) * scale
    p_mm = jax.nn.softmax(s_mm, axis=-1).astype(v.dtype)
    out_m = jnp.einsum('bhqm,bmhd->bqhd', p_mm, vm)
    return jnp.concatenate([out_m, out], axis=1)


def sliding_window_attention(q, k, v, sink):
    B, L, HQ, Dh = q.shape
    G = k.shape[2]
    R = HQ // G
    T = L - N_META
    nb = T // SW_BLOCK
    scale = Dh ** -0.5
    slopes = jnp.power(2.0, -8.0 * jnp.arange(1, HQ + 1, dtype=jnp.float32) / HQ).reshape(G, R)
    sink = sink.astype(jnp.float32).reshape(G, R)
    qm = q[:, :N_META].reshape(B, N_META, G, R, Dh)
    km, vm = k[:, :N_META], v[:, :N_META]
    qr = q[:, N_META:].reshape(B, nb, SW_BLOCK, G, R, Dh)
    kr = k[:, N_META:].reshape(B, nb, SW_BLOCK, G, Dh)
    vr = v[:, N_META:].reshape(B, nb, SW_BLOCK, G, Dh)
    pad = ((0, 0), (1, 1), (0, 0), (0, 0), (0, 0))
    kp, vp = jnp.pad(kr, pad), jnp.pad(vr, pad)
    kband = jnp.concatenate([kp[:, :-2], kp[:, 1:-1], kp[:, 2:]], axis=2)
    vband = jnp.concatenate([vp[:, :-2], vp[:, 1:-1], vp[:, 2:]], axis=2)
    qpos = jnp.arange(nb)[:, None] * SW_BLOCK + jnp.arange(SW_BLOCK)[None, :]
    kpos = (jnp.arange(nb)[:, None] - 1) * SW_BLOCK + jnp.arange(3 * SW_BLOCK)[None, :]
    dist = jnp.abs(qpos[:, :, None] - kpos[:, None, :])
    valid = (dist <= SW_WINDOW) & (kpos[:, None, :] >= 0) & (kpos[:, None, :] < T)
    penalty = dist[:, None, None] * slopes[None, :, :, None, None]
    s_loc = jnp.einsum('bnqgrd,bnkgd->bngrqk', qr, kband, preferred_element_type=jnp.float32) * scale
    s_loc = jnp.where(valid[:, None, None], s_loc - penalty, NEG_INF)
    s_met = jnp.einsum('bnqgrd,bmgd->bngrqm', qr, km, preferred_element_type=jnp.float32) * scale
    s_snk = jnp.broadcast_to(sink[None, None, :, :, None, None], s_loc.shape[:-1] + (1,))
    p = jax.nn.softmax(jnp.concatenate([s_loc, s_met, s_snk], axis=-1), axis=-1).astype(v.dtype)
    nk = 3 * SW_BLOCK
    out = (jnp.einsum('bngrqk,bnkgd->bnqgrd', p[..., :nk], vband)
           + jnp.einsum('bngrqm,bmgd->bnqgrd', p[..., nk:nk + N_META], vm))
    out = out.reshape(B, T, HQ, Dh)
    dist_m = (N_META + jnp.arange(SW_BLOCK))[None, :] - jnp.arange(N_META)[:, None]
    s_mm = jnp.einsum('bqgrd,bmgd->bgrqm', qm, km, preferred_element_type=jnp.float32) * scale
    s_mr = jnp.einsum('bqgrd,bkgd->bgrqk', qm, kr[:, 0], preferred_element_type=jnp.float32) * scale
    s_mr = jnp.where(dist_m <= SW_WINDOW, s_mr - slopes[:, :, None, None] * dist_m, NEG_INF)
    s_ms = jnp.broadcast_to(sink[None, :, :, None, None], s_mm.shape[:-1] + (1,))
    p_m = jax.nn.softmax(jnp.concatenate([s_mm, s_mr, s_ms], axis=-1), axis=-1).astype(v.dtype)
    out_m = (jnp.einsum('bgrqm,bmgd->bqgrd', p_m[..., :N_META], vm)
             + jnp.einsum('bgrqk,bkgd->bqgrd', p_m[..., N_META:N_META + SW_BLOCK], vr[:, 0]))
    out_m = out_m.reshape(B, N_META, HQ, Dh)
    return jnp.concatenate([out_m, out], axis=1)


def depthwise_conv_centred(u, w, b):
    c = u.shape[-1]
    y = lax.conv_general_dilated(u, w[:, None, :].astype(u.dtype), window_strides=(1,),
                                 padding=((CONV_W // 2, CONV_W // 2),),
                                 dimension_numbers=('NWC', 'WIO', 'NWC'), feature_group_count=c)
    return y + b.astype(u.dtype)


def encoder_layer(x, w_in, na_rpb, sw_sink, w_proj_na, w_proj_sw, w_out, ln1_g, ln1_b,
                  w_ffn_in, ffn_conv_w, ffn_conv_b, w_ffn_down, ln2_g, ln2_b):
    B, L, _ = x.shape
    proj = x @ w_in
    qa, ka, va, qb, kb, vb, ga, gb = jnp.split(proj, np.cumsum(IN_SPLITS)[:-1].tolist(), axis=-1)
    qa = qa.reshape(B, L, NA_HEADS, NA_HEAD_DIM)
    ka = ka.reshape(B, L, NA_HEADS, NA_HEAD_DIM)
    va = va.reshape(B, L, NA_HEADS, NA_HEAD_DIM)
    qb = qb.reshape(B, L, SW_Q_HEADS, SW_HEAD_DIM)
    kb = kb.reshape(B, L, SW_KV_HEADS, SW_HEAD_DIM)
    vb = vb.reshape(B, L, SW_KV_HEADS, SW_HEAD_DIM)
    oa = neighbourhood_attention(qa, ka, va, na_rpb).reshape(B, L, NA_WIDTH)
    ob = sliding_window_attention(qb, kb, vb, sw_sink).reshape(B, L, SW_WIDTH)
    merged = jax.nn.sigmoid(ga) * (oa @ w_proj_na) + jax.nn.sigmoid(gb) * (ob @ w_proj_sw)
    h = layer_norm(DEEPNORM_ALPHA * x + merged @ w_out, ln1_g, ln1_b)
    gate, val = jnp.split(h @ w_ffn_in, 2, axis=-1)
    gate = depthwise_conv_centred(gate, ffn_conv_w, ffn_conv_b)
    f = (jax.nn.gelu(gate) * val) @ w_ffn_down
    return layer_norm(DEEPNORM_ALPHA * h + f, ln2_g, ln2_b)


def encoder_trunk(x, meta_tokens, ln_emb_g, ln_emb_b, w_in, na_rpb, sw_sink, w_proj_na, w_proj_sw, w_out,
                  ln1_g, ln1_b, w_ffn_in, ffn_conv_w, ffn_conv_b, w_ffn_down, ln2_g, ln2_b):
    B = x.shape[0]
    meta = jnp.broadcast_to(meta_tokens.astype(x.dtype)[None], (B, N_META, D_MODEL))
    h = layer_norm(jnp.concatenate([meta, x], axis=1), ln_emb_g, ln_emb_b)
    for i in range(DEPTH):
        h = encoder_layer(h, w_in[i], na_rpb[i], sw_sink[i], w_proj_na[i], w_proj_sw[i], w_out[i],
                          ln1_g[i], ln1_b[i], w_ffn_in[i], ffn_conv_w[i], ffn_conv_b[i], w_ffn_down[i],
                          ln2_g[i], ln2_b[i])
    return h[:, N_META:]


def setup_inputs(seed: int = 0) -> dict:
    key = jax.random.key(seed)
    ks = jax.random.split(key, 20)
    nrm = jax.random.normal
    f32 = jnp.float32
    return {
        'x_prompt': nrm(ks[0], (BATCH, SEQ, D_MODEL), f32),
        'x_sample': nrm(ks[1], (DEC_BATCH, DEC_SEQ, D_MODEL), f32),
        'meta_tokens': nrm(ks[2], (N_META, D_MODEL), f32),
        'ln_emb_g': 1.0 + 0.01 * nrm(ks[3], (D_MODEL,), f32),
        'ln_emb_b': 0.01 * nrm(ks[4], (D_MODEL,), f32),
        'w_in': nrm(ks[5], (DEPTH, D_MODEL, IN_COLS), f32) * D_MODEL ** -0.5,
        'na_rpb': 0.1 * nrm(ks[6], (DEPTH, NA_HEADS, 2 * NA_WIN_ROWS - 1, 2 * NA_WIN_COLS - 1), f32),
        'sw_sink': 0.5 * nrm(ks[7], (DEPTH, SW_Q_HEADS), f32),
        'w_proj_na': nrm(ks[8], (DEPTH, NA_WIDTH, D_MODEL), f32) * NA_WIDTH ** -0.5,
        'w_proj_sw': nrm(ks[9], (DEPTH, SW_WIDTH, D_MODEL), f32) * SW_WIDTH ** -0.5,
        'w_out': nrm(ks[10], (DEPTH, D_MODEL, D_MODEL), f32) * (D_MODEL ** -0.5 * DEEPNORM_BETA),
        'ln1_g': 1.0 + 0.01 * nrm(ks[11], (DEPTH, D_MODEL), f32),
        'ln1_b': 0.01 * nrm(ks[12], (DEPTH, D_MODEL), f32),
        'w_ffn_in': nrm(ks[13], (DEPTH, D_MODEL, 2 * D_FF), f32) * D_MODEL ** -0.5,
        'ffn_conv_w': nrm(ks[14], (DEPTH, CONV_W, D_FF), f32) * CONV_W ** -0.5,
        'ffn_conv_b': 0.01 * nrm(ks[15], (DEPTH, D_FF), f32),
        'w_ffn_down': nrm(ks[16], (DEPTH, D_FF, D_MODEL), f32) * (D_FF ** -0.5 * DEEPNORM_BETA),
        'ln2_g': 1.0 + 0.01 * nrm(ks[17], (DEPTH, D_MODEL), f32),
        'ln2_b': 0.01 * nrm(ks[18], (DEPTH, D_MODEL), f32),
    }


def reference(x_prompt, x_sample, meta_tokens, ln_emb_g, ln_emb_b, w_in, na_rpb, sw_sink, w_proj_na, w_proj_sw,
              w_out, ln1_g, ln1_b, w_ffn_in, ffn_conv_w, ffn_conv_b, w_ffn_down, ln2_g, ln2_b):
    y_prompt = encoder_trunk(x_prompt, meta_tokens, ln_emb_g, ln_emb_b, w_in, na_rpb, sw_sink, w_proj_na,
                             w_proj_sw, w_out, ln1_g, ln1_b, w_ffn_in, ffn_conv_w, ffn_conv_b, w_ffn_down,
                             ln2_g, ln2_b)
    y_sample = encoder_trunk(x_sample, meta_tokens, ln_emb_g, ln_emb_b, w_in, na_rpb, sw_sink, w_proj_na,
                             w_proj_sw, w_out, ln1_g, ln1_b, w_ffn_in, ffn_conv_w, ffn_conv_b, w_ffn_down,
                             ln2_g, ln2_b)
    return (y_prompt, y_sample)
```

```python
import contextlib
import numpy as np
import concourse.bass as bass
import concourse.mybir as mybir
from concourse.bass_utils import run_bass_kernel_spmd

F32 = mybir.dt.float32
BF16 = mybir.dt.bfloat16
AF = mybir.ActivationFunctionType
ALU = mybir.AluOpType

D = 2048
NMETA = 16
DFF = 5632
NCH = DFF // 128
INC = 8704
TOK = 4096
TT = 512
NTILES = TOK // TT
NB = 384
NW = 1152
NWT = NW + NMETA
NQ = 514
Q0 = NB - 1
XE_ROWS = NB + TOK + 256
ALPHA = 2.0 ** 0.25
EPS = 1e-5
NEG = -30000.0
SCALE = 128.0 ** -0.5
NFILL = 0


class Res:
    __slots__ = ("name", "writers", "readers", "prev")

    def __init__(self, name):
        self.name = name
        self.writers = []
        self.readers = []
        self.prev = []


class Op:
    __slots__ = ("eng", "fn", "deps", "dma", "stream", "signal", "semval", "idx", "n_dma")

    def __init__(self, eng, fn, dma=False, stream=None, n_dma=1):
        self.eng = eng
        self.fn = fn
        self.deps = []
        self.dma = dma
        self.stream = stream
        self.signal = False
        self.semval = None
        self.idx = None
        self.n_dma = n_dma


class Sched:
    ENGS = ("pe", "act", "dve", "pool", "sp")

    def __init__(self, nc):
        self.nc = nc
        self.ops = []

    def op(self, eng, fn, reads=(), writes=(), dma=False, stream=None, n_dma=1, cowrites=()):
        o = Op(eng, fn, dma=dma, stream=stream, n_dma=n_dma)
        o.idx = len(self.ops)
        deps = set()
        writes = list(writes)
        co = []
        for r in cowrites:
            if r.readers or not r.writers:
                writes.append(r)
            else:
                co.append(r)
        for r in reads:
            deps.update(r.writers)
        for r in writes:
            deps.update(r.writers)
            deps.update(r.readers)
        for r in co:
            deps.update(r.prev)
        o.deps = sorted(deps, key=lambda d: d.idx)
        for r in reads:
            r.readers.append(o)
        for r in writes:
            r.prev = list(r.writers) + list(r.readers)
            r.writers = [o]
            r.readers = []
        for r in co:
            r.writers.append(o)
        self.ops.append(o)
        return o

    def dma(self, eng, stream, fn, reads=(), writes=(), n_dma=1):
        return self.op(eng, fn, reads=reads, writes=writes, dma=True, stream=stream, n_dma=n_dma)

    def emit(self):
        nc = self.nc
        for o in self.ops:
            if o.dma:
                o.signal = True
            for d in o.deps:
                if d.dma:
                    continue
                if d.eng == o.eng and d.eng == "pe" and not o.dma:
                    continue
                d.signal = True
        cnt = {e: 0 for e in self.ENGS}
        scnt = {}
        for o in self.ops:
            if o.dma:
                scnt[o.stream] = scnt.get(o.stream, 0) + 16 * o.n_dma
                o.semval = scnt[o.stream]
            elif o.signal:
                cnt[o.eng] += 1
                o.semval = cnt[o.eng]
        stream_names = sorted(scnt.keys())
        with contextlib.ExitStack() as es:
            esem = {e: es.enter_context(nc.semaphore("s_" + e)) for e in self.ENGS}
            ssem = {s: es.enter_context(nc.semaphore("d_" + s)) for s in stream_names}
            block = es.enter_context(nc.Block())
            by_eng = {e: [o for o in self.ops if o.eng == e] for e in self.ENGS}

            def run(eng_name, eng):
                waited = {}
                for o in by_eng[eng_name]:
                    need = {}
                    for d in o.deps:
                        if d.dma:
                            key = ("s", d.stream)
                        else:
                            if d.eng == eng_name and eng_name == "pe" and not o.dma:
                                continue
                            key = ("e", d.eng)
                        if d.semval > need.get(key, 0):
                            need[key] = d.semval
                    for key, v in need.items():
                        if waited.get(key, 0) >= v:
                            continue
                        waited[key] = v
                        sem = ssem[key[1]] if key[0] == "s" else esem[key[1]]
                        eng.wait_ge(sem, v)
                    ins = o.fn(eng)
                    if o.dma:
                        if not isinstance(ins, (list, tuple)):
                            ins = [ins]
                        assert len(ins) == o.n_dma, (len(ins), o.n_dma)
                        for i in ins:
                            i.then_inc(ssem[o.stream], 16)
                    elif o.signal:
                        ins.then_inc(esem[eng_name], 1)
                if eng_name == "sp":
                    for s in stream_names:
                        if waited.get(("s", s), 0) < scnt[s]:
                            eng.wait_ge(ssem[s], scnt[s])
                    for e in self.ENGS:
                        if e != "sp" and cnt[e] > 0:
                            eng.wait_ge(esem[e], cnt[e])

            @block.tensor
            def _(eng):
                run("pe", eng)

            @block.scalar
            def _(eng):
                run("act", eng)

            @block.vector
            def _(eng):
                run("dve", eng)

            @block.gpsimd
            def _(eng):
                run("pool", eng)

            @block.sync
            def _(eng):
                run("sp", eng)


def build_nc(ntiles=NTILES, dbg=False):
    nc = bass.Bass("TRN2", target_bir_lowering=False)
    DBG = {}
    if dbg:
        DBG['h0'] = nc.dram_tensor('d_h0', [512, D], F32, kind='ExternalOutput').ap()
        DBG['h1'] = nc.dram_tensor('d_h1', [512, D], F32, kind='ExternalOutput').ap()
        DBG['oa'] = nc.dram_tensor('d_oa', [128, 8 * NQ], BF16, kind='ExternalOutput').ap()
        DBG['ob'] = nc.dram_tensor('d_ob', [128, 8 * NQ], BF16, kind='ExternalOutput').ap()
        DBG['mg'] = nc.dram_tensor('d_mg', [128, 16 * NQ], BF16, kind='ExternalOutput').ap()
        DBG['uT'] = nc.dram_tensor('d_uT', [128, NCH * NQ], BF16, kind='ExternalOutput').ap()
        DBG['qa'] = nc.dram_tensor('d_qa', [128, 8 * NQ], BF16, kind='ExternalOutput').ap()
        DBG['ka'] = nc.dram_tensor('d_ka', [128, 8 * NWT], BF16, kind='ExternalOutput').ap()
        DBG['va'] = nc.dram_tensor('d_va', [128, 10 * 1024], BF16, kind='ExternalOutput').ap()
        DBG['h0T'] = nc.dram_tensor('d_h0T', [128, 16 * NWT], BF16, kind='ExternalOutput').ap()

    def din(name, shape):
        return nc.dram_tensor(name, list(shape), F32, kind="ExternalInput").ap()

    xe = din("xe", [XE_ROWS, D])
    xh = din("xh", [NTILES, 128, 32])
    meta = din("meta", [NMETA, D])
    lnv = din("lnv", [6, D])
    lnT_d = din("lnT", [128, 64])
    convw_d = din("convw", [128, 4 * NCH])
    sink_d = din("sink", [128, 8])
    nabg_d = din("nab_g", [128, 7168])
    nabm_d = din("nab_m", [128, 7168])
    swb_d = din("swb", [128, 3072])
    lrow_d = din("lrow", [18, NW])
    rna_d = din("rna", [NTILES, 18, NQ])
    fsw_d = din("fsw", [128, NTILES * 9])
    flag_d = din("flag", [128, NTILES])
    w_in = din("w_in", [INC // 256, 128, 4096])
    w_pa = din("w_pa", [8, 128, 2048])
    w_pb = din("w_pb", [8, 128, 2048])
    w_out = din("w_out", [8, 128, 4096])
    w_fi = din("w_fi", [2 * DFF // 256, 128, 4096])
    w_fd = din("w_fd", [24, 128, 4096])
    y = nc.dram_tensor("y", [TOK, D], F32, kind="ExternalOutput").ap()

    es = contextlib.ExitStack()
    with es:
        def sb(name, shape, dt):
            return es.enter_context(nc.sbuf_tensor("sb_" + name, list(shape), dt))

        h0 = sb("h0", [128, 4, D], F32)
        h0T = sb("h0T", [128, 16, NWT], BF16)
        R1 = sb("R1", [128, 23808], BF16)
        R2 = sb("R2", [128, 8224], BF16)
        wbuf = [sb("wb%d" % i, [128, 4096], BF16) for i in range(3)]
        LNp = sb("LNp", [128, 2, D], F32)
        Btab = sb("Btab", [128, 7168], BF16)
        BWtab = sb("BWtab", [128, 3072], BF16)
        lrow = sb("lrow", [18, NW], BF16)
        rna = [sb("rna0", [18, NQ], BF16)] * 2
        ident = sb("ident", [128, 128], BF16)
        ones_bf = sb("ones_bf", [128, 128], BF16)
        ones_f = sb("ones_f", [128, 128], F32)
        SCR = sb("SCR", [128, 2048], F32)
        PT = [sb("PT%d" % i, [128, NQ], BF16) for i in range(2)]
        lnT = sb("lnT", [128, 64], F32)
        convw = sb("convw", [128, 4 * NCH], F32)
        esink = sb("esink", [128, 8], F32)
        fsw = sb("fsw", [128, NTILES * 9], F32)
        flag = sb("flag", [128, NTILES], F32)
        small = sb("small", [128, 64], F32)
        xhs = sb("xhs", [128, 16, 2], F32)
        h0h = sb("h0h", [128, 16, 2], F32)
        zh = sb("zh", [128, 16, 2], F32)
        zh2 = sb("zh2", [128, 16, 2], F32)
        bnst = sb("bnst", [128, 48], F32)

        KT_A = R1[:, 0:9344].rearrange("p (h n) -> p h n", h=8)
        V_A = R1[:, 9344:19584].rearrange("p (t c) -> p t c", t=10)
        QT_A = R1[:, 19584:23696].rearrange("p (h n) -> p h n", h=8)
        KT_B = R1[:, 0:2336].rearrange("p (h n) -> p h n", h=2)
        V_B = R1[:, 2336:4896].rearrange("p (t c) -> p t c", t=10)
        QT_B = R1[:, 4896:9008].rearrange("p (h n) -> p h n", h=8)
        mergedT = R1[:, 0:8224].rearrange("p (c n) -> p c n", c=16)
        uT = R1[:, 0:22616].rearrange("p (c n) -> p c n", c=NCH)
        xs = R2[:, 0:4096].bitcast(F32)
        hn = R2[:, 4096:6144]
        hnbufs = [R2[:, 4096:6144], R2[:, 6144:8192]]
        xsbufs = [R2[:, 0:4096].bitcast(F32), SCR[:, 0:2048]]
        oaT = R2[:, 0:4112].rearrange("p (h n) -> p h n", h=8)
        obT = R2[:, 4112:8224].rearrange("p (h n) -> p h n", h=8)
        sga = SCR[:, 0:NQ]
        sgb = SCR[:, 514:514 + NQ]
        rD = SCR[:, 1028:1028 + NQ]
        Araw = SCR[:, 0:516]
        cbuf = SCR[:, 516:1028]
        gbuf = SCR[:, 1028:1540]

        banks = [es.enter_context(nc.psum_tensor("pb%d" % i, [128, 512], F32)) for i in range(8)]
        rb = [Res("bank%d" % i) for i in range(8)]

        S = Sched(nc)
        r_h0 = [Res("h0_%d" % j) for j in range(4)]
        r_hTl, r_hTq, r_hTh = Res("h0T_lo"), Res("h0T_q"), Res("h0T_hi")
        r_hTall = [r_hTl, r_hTq, r_hTh]
        r_R1 = Res("R1guard")
        r_R2 = Res("R2guard")
        r_xs, r_hn = Res("xs"), Res("hn")
        r_hnb = [Res("hnA"), Res("hnB")]
        r_bnb = [Res("bn0"), Res("bn1")]
        r_smb = [Res("sm0"), Res("sm1")]
        r_wb = [Res("wb%d" % i) for i in range(3)]
        r_LNp = Res("LNp")
        r_const = Res("const")
        r_rna = [Res("rna0")] * 2
        r_SCR = Res("SCR")
        r_Oc, r_rD = Res("Oc"), Res("rD")
        r_xsb = [Res("xsA"), r_SCR]
        r_PT = [Res("PT0"), Res("PT1")]
        r_small = Res("small")
        r_xhs, r_h0h, r_zh = Res("xhs"), Res("h0h"), Res("zh")
        r_bn = Res("bnst")
        r_KTA = [Res("KTA%d" % h) for h in range(8)]
        r_VA = [Res("VA%d" % t) for t in range(10)]
        r_QTA = [Res("QTA%d" % h) for h in range(8)]
        r_KTB = [Res("KTB%d" % h) for h in range(2)]
        r_VB = [Res("VB%d" % t) for t in range(10)]
        r_QTB = [Res("QTB%d" % h) for h in range(8)]
        r_oa = [Res("oa%d" % h) for h in range(8)]
        r_ob = [Res("ob%d" % h) for h in range(8)]
        r_mg = [Res("mg%d" % c) for c in range(16)]
        r_uT = [Res("uT%d" % c) for c in range(NCH)]
        r_y = Res("y")
        r_yj = [Res("y%d" % j) for j in range(4)]

        wstate = {"n": 0}

        def wload(src2, kk, cols):
            i = wstate["n"] % 3
            wstate["n"] += 1
            flat = wbuf[i][:, 0:kk * cols]
            view = flat.rearrange("p (k c) -> p k c", k=kk)
            S.dma("pool", "wb%d" % i, lambda e, flat=flat, src2=src2: e.dma_start(out=flat, in_=src2),
                  writes=[r_wb[i]])
            return view, r_wb[i]

        def wcols(w, c0, ncols, k0=0, nk=None):
            v = w.rearrange("(k p) c -> p k c", p=128)
            if nk is None:
                nk = v.shape[1] - k0
            return v[:, k0:k0 + nk, c0:c0 + ncols]

        def setup():
            S.dma("sp", "c0", lambda e: [e.dma_start(out=lnT[:], in_=lnT_d[:, :]),
                                         e.dma_start(out=convw[:], in_=convw_d[:, :]),
                                         e.dma_start(out=esink[:], in_=sink_d[:, :]),
                                         e.dma_start(out=fsw[:], in_=fsw_d[:, :]),
                                         e.dma_start(out=flag[:], in_=flag_d[:, :])],
                  writes=[r_const], n_dma=5)
            S.dma("pool", "c1", lambda e: [e.dma_start(out=BWtab[:], in_=swb_d[:, :]),
                                           e.dma_start(out=lrow[:], in_=lrow_d[:, :])],
                  writes=[r_const], n_dma=2)
            stg_g = h0[:, :, :].rearrange("p a f -> p (a f)")[:, 0:7168]
            stg_m = R1[:, 0:14336].bitcast(F32)
            S.dma("sp", "c2", lambda e: [e.dma_start(out=stg_g, in_=nabg_d[:, :]),
                                         e.dma_start(out=stg_m, in_=nabm_d[:, :])],
                  writes=[r_h0[0], r_R1], n_dma=2)
            S.op("dve", lambda e: e.tensor_tensor(out=Btab[:], in0=stg_g, in1=stg_m, op=ALU.add),
                 reads=[r_h0[0], r_R1], writes=[r_const])

            def mk_ones(e):
                e.memset(ones_f[:], 1.0)
                return e.memset(SCR[:, 0:128], 1.0)
            S.op("pool", mk_ones, writes=[r_SCR, r_const])
            S.op("pool", lambda e: e.affine_select(out=SCR[:, 0:128], in_=SCR[:, 0:128], pattern=[[-1, 128]],
                                                   compare_op=ALU.is_equal, fill=0.0, base=0, channel_multiplier=1),
                 reads=[r_SCR], writes=[r_SCR])

            def mk_consts(e):
                e.tensor_copy(out=ident[:], in_=SCR[:, 0:128])
                e.tensor_copy(out=ones_bf[:], in_=ones_f[:])
                e.memset(small[:, 0:1], EPS)
                return e.memset(small[:, 1:2], 0.0)
            S.op("dve", mk_consts, reads=[r_SCR], writes=[r_const, r_small])
            S.op("act", lambda e: e.activation(out=esink[:], in_=esink[:], func=AF.Exp),
                 reads=[r_const], writes=[r_const])

        eps_ap = small[:, 0:1]

        lnstate = {"n": 0}

        def ln_tok(src, np_, gi, out_f32=None, out_bf=None):
            ss = lnstate["n"] % 2
            lnstate["n"] += 1
            sc = 8 + 16 * ss
            bo = 24 * ss
            rsm, rbn = r_smb[ss], r_bnb[ss]
            extra = gi.get("extra", [])

            def stats(e):
                last = None
                for c in range(4):
                    last = e.bn_stats(out=bnst[0:np_, bo + 6 * c:bo + 6 * c + 6], in_=src[:, c * 512:(c + 1) * 512])
                return last
            S.op("dve", stats, reads=gi["r_src"], writes=[rbn])
            tick()
            S.op("dve", lambda e: e.bn_aggr(out=small[0:np_, sc:sc + 2], in_=bnst[0:np_, bo:bo + 24]), reads=[rbn], writes=[rsm])
            tick()
            S.op("act", lambda e: e.activation(out=small[0:np_, sc + 2:sc + 3], in_=small[0:np_, sc + 1:sc + 2], func=AF.Ln,
                                               bias=eps_ap[0:np_, :], scale=1.0), reads=[rsm, r_const], writes=[rsm])
            tick()
            S.op("act", lambda e: e.activation(out=small[0:np_, sc + 2:sc + 3], in_=small[0:np_, sc + 2:sc + 3], func=AF.Exp, scale=-0.5),
                 reads=[rsm], writes=[rsm])
            tick()
            if gi.get("feat"):
                S.op("dve", lambda e: e.scalar_tensor_tensor(out=small[0:np_, sc + 3:sc + 4], in0=small[0:np_, sc:sc + 1], scalar=-1.0,
                                                             in1=small[0:np_, sc + 2:sc + 3], op0=ALU.mult, op1=ALU.mult),
                     reads=[rsm], writes=[rsm])
                tick()
                S.op("act", lambda e: e.activation(out=out_bf, in_=src, func=AF.Identity, bias=small[0:np_, sc + 3:sc + 4],
                                                   scale=small[0:np_, sc + 2:sc + 3]),
                     reads=[rsm] + gi["r_src"] + extra, writes=gi["r_bf"])
                tick()
                return
            S.op("dve", lambda e: e.scalar_tensor_tensor(out=src, in0=src, scalar=small[0:np_, sc:sc + 1], in1=LNp[0:np_, 0, :],
                                                         op0=ALU.subtract, op1=ALU.mult),
                 reads=[rsm, r_LNp] + gi["r_src"], writes=gi["r_src"])
            tick()
            if out_f32 is not None:
                S.op("dve", lambda e: e.scalar_tensor_tensor(out=out_f32, in0=src, scalar=small[0:np_, sc + 2:sc + 3], in1=LNp[0:np_, 1, :],
                                                             op0=ALU.mult, op1=ALU.add),
                     reads=[rsm, r_LNp] + gi["r_src"], writes=gi["r_dst"])
                tick()
                if out_bf is not None:
                    S.op("act", lambda e: e.copy(out=out_bf, in_=out_f32), reads=gi["r_dst"] + extra, writes=gi["r_bf"])
                    tick()
            else:
                S.op("dve", lambda e: e.scalar_tensor_tensor(out=out_bf, in0=src, scalar=small[0:np_, sc + 2:sc + 3], in1=LNp[0:np_, 1, :],
                                                             op0=ALU.mult, op1=ALU.add),
                     reads=[rsm, r_LNp] + gi["r_src"] + extra, writes=gi["r_bf"])
                tick()

        def load_lnp(i):
            S.dma("sp", "lnp", lambda e: [e.dma_start(out=LNp[:, 0:1, :], in_=lnv[2 * i:2 * i + 1, :].partition_broadcast(128)),
                                          e.dma_start(out=LNp[:, 1:2, :], in_=lnv[2 * i + 1:2 * i + 2, :].partition_broadcast(128))],
                  writes=[r_LNp], n_dma=2)

        tstate = {"n": 0}

        def transpose_to(np_, dst_fn, reads, writes, hnb=None, r_hnx=None, affine=None):
            for half in range(2):
                bi = 6 + (tstate["n"] % 2)
                tstate["n"] += 1
                pT = banks[bi][:, :].bitcast(BF16)

                def tr(e, half=half, pT=pT):
                    last = None
                    for j in range(8):
                        k = half * 8 + j
                        last = e.transpose(out=pT[:, j * 128:j * 128 + np_], in_=(hn if hnb is None else hnb)[0:np_, k * 128:(k + 1) * 128],
                                           identity=ident[0:np_, 0:np_])
                    return last
                S.op("pe", tr, reads=[r_hn if r_hnx is None else r_hnx, r_const], writes=[rb[bi]])
                if affine is None:
                    src = pT[:, :].rearrange("p (a b) -> p a b", a=8)[:, :, 0:np_]
                    S.op("act", lambda e, half=half, src=src: e.copy(out=dst_fn(half * 8, 8), in_=src),
                         reads=[rb[bi]] + reads, writes=writes)
                else:
                    g0, b0 = affine

                    def ev_act(e, half=half, pT=pT):
                        last = None
                        for j in range(8):
                            k = half * 8 + j
                            last = e.activation(out=dst_fn(k, 1)[:, 0, :], in_=pT[:, j * 128:j * 128 + np_], func=AF.Identity,
                                                bias=lnT[:, b0 + k:b0 + k + 1], scale=lnT[:, g0 + k:g0 + k + 1])
                        return last

                    def ev_dve(e, half=half, pT=pT):
                        last = None
                        for j in range(8):
                            k = half * 8 + j
                            last = e.tensor_scalar(out=dst_fn(k, 1)[:, 0, :], in0=pT[:, j * 128:j * 128 + np_],
                                                   scalar1=lnT[:, g0 + k:g0 + k + 1], scalar2=lnT[:, b0 + k:b0 + k + 1],
                                                   op0=ALU.mult, op1=ALU.add)
                        return last
                    if half == 0:
                        S.op("act", ev_act, reads=[rb[bi], r_const] + reads, writes=writes)
                    else:
                        S.op("dve", ev_dve, reads=[rb[bi], r_const] + reads, cowrites=writes)

        def ln_feat_gen(x3, gcol, bcol, out3, r_in, r_out, hb_bank):
            S.op("dve", lambda e: e.tensor_tensor(out=zh2[:], in0=x3, in1=x3, op=ALU.mult), reads=r_in, writes=[r_zh])
            yield
            ps = banks[hb_bank]

            def mm(e):
                last = None
                for k in range(16):
                    last = e.matmul(ps[:, 0:2], lhsT=ones_f[:], rhs=x3[:, k, :], start=(k == 0), stop=False)
                for k in range(16):
                    last = e.matmul(ps[:, 2:4], lhsT=ones_f[:], rhs=zh2[:, k, :], start=False, stop=(k == 15))
                return last
            S.op("pe", mm, reads=r_in + [r_zh, r_const], writes=[rb[hb_bank]])
            yield
            S.op("dve", lambda e: e.tensor_scalar(out=small[:, 16:20], in0=ps[:, 0:4], scalar1=1.0 / D, scalar2=None, op0=ALU.mult),
                 reads=[rb[hb_bank]], writes=[r_small])
            yield
            S.op("dve", lambda e: e.tensor_tensor(out=small[:, 20:22], in0=small[:, 16:18], in1=small[:, 16:18], op=ALU.mult),
                 reads=[r_small], writes=[r_small])
            yield
            S.op("dve", lambda e: e.tensor_tensor(out=small[:, 18:20], in0=small[:, 18:20], in1=small[:, 20:22], op=ALU.subtract),
                 reads=[r_small], writes=[r_small])
            yield
            S.op("act", lambda e: e.activation(out=small[:, 18:20], in_=small[:, 18:20], func=AF.Ln, bias=eps_ap, scale=1.0),
                 reads=[r_small, r_const], writes=[r_small])
            yield
            S.op("act", lambda e: e.activation(out=small[:, 18:20], in_=small[:, 18:20], func=AF.Exp, scale=-0.5),
                 reads=[r_small], writes=[r_small])
            yield

            def n1(e):
                e.tensor_scalar(out=zh2[:, :, 0], in0=x3[:, :, 0], scalar1=small[:, 16:17], scalar2=small[:, 18:19], op0=ALU.subtract, op1=ALU.mult)
                return e.tensor_scalar(out=zh2[:, :, 1], in0=x3[:, :, 1], scalar1=small[:, 17:18], scalar2=small[:, 19:20], op0=ALU.subtract, op1=ALU.mult)
            S.op("dve", n1, reads=r_in + [r_small, r_zh], writes=[r_zh])
            yield

            def n2(e):
                e.tensor_tensor(out=zh2[:, :, 0], in0=zh2[:, :, 0], in1=gcol, op=ALU.mult)
                return e.tensor_tensor(out=zh2[:, :, 1], in0=zh2[:, :, 1], in1=gcol, op=ALU.mult)
            S.op("dve", n2, reads=[r_zh, r_const], writes=[r_zh])
            yield

            def n3(e):
                e.tensor_tensor(out=out3[:, :, 0], in0=zh2[:, :, 0], in1=bcol, op=ALU.add)
                return e.tensor_tensor(out=out3[:, :, 1], in0=zh2[:, :, 1], in1=bcol, op=ALU.add)
            S.op("dve", n3, reads=[r_zh, r_const], writes=r_out + [r_zh])
            yield

        def ln_feat(*args):
            for _ in ln_feat_gen(*args):
                pass

        tickstate = {"g": None}

        def tick(n=1):
            for _ in range(n):
                g = tickstate["g"]
                if g is None:
                    return
                try:
                    next(g)
                except StopIteration:
                    tickstate["g"] = None

        def drain():
            while tickstate["g"] is not None:
                tick()

        fm_state = {"main": 0, "tail": 0}

        def fm_proj(lhs_fn, nk, rhs_fn, ncols, reads, consume, main_banks=None, tail_banks=None):
            if ncols == NQ:
                pieces = [(0, 257), (257, NQ)]
            else:
                pieces = []
                c = 0
                while c < ncols:
                    n = min(512, ncols - c)
                    pieces.append((c, c + n))
                    c += n
            for (c0, c1) in pieces:
                n = c1 - c0
                bi = fm_state["main"] % 6
                fm_state["main"] += 1
                ps = banks[bi][:, 0:n]

                def mm(e, ps=ps, c0=c0, c1=c1):
                    last = None
                    for k in range(nk):
                        last = e.matmul(ps, lhsT=lhs_fn(k), rhs=rhs_fn(k, c0, c1), start=(k == 0), stop=(k == nk - 1))
                    return last
                S.op("pe", mm, reads=reads, writes=[rb[bi]])
                consume(ps, c0, c1, rb[bi])

        def attn_phase(heads):
            Om, Dm = banks[0], banks[1]
            Ot = banks[2][:, 0:2]
            Dt = banks[7][:, 0:2]
            Oc = SCR[:, 0:NQ]
            steps = []
            for H in heads:
                items = [("meta", 0, NQ, None, None, None)] + list(H["key_tiles"])
                for ii, itm in enumerate(items):
                    steps.append((H, itm, ii == 0, ii == len(items) - 1))

            def segs_of(a, b):
                sg = []
                if a < 512:
                    sg.append((a, min(b, 512), False))
                if b > 512:
                    sg.append((max(a, 512), b, True))
                return sg

            def emit_S(si):
                H, (kt, a, b, tabf, mask, bias_ap), _, _ = steps[si]
                segs = segs_of(a, b)
                Sm, St = banks[3 + si % 2], banks[5 + si % 2]
                res_w = [rb[3 + si % 2], rb[5 + si % 2]]
                qT, kt_fn, meta_k = H["qT"], H["kt_fn"], H["meta_k"]

                def smm(e):
                    last = None
                    for (c0, c1, tail) in segs:
                        npo = NMETA if kt == "meta" else 128
                        dst = St[0:npo, c0 - 512:c1 - 512] if tail else Sm[0:npo, c0:c1]
                        if kt == "meta":
                            last = e.matmul(dst, lhsT=meta_k, rhs=qT[:, c0:c1], start=True, stop=True)
                        else:
                            nm = 1 + (tabf is not None) + (mask is not None)
                            j = 0
                            last = e.matmul(dst, lhsT=kt_fn(kt), rhs=qT[:, c0:c1], start=True, stop=(j == nm - 1))
                            if tabf is not None:
                                j += 1
                                last = e.matmul(dst, lhsT=ident[:], rhs=tabf(c0, c1), start=False, stop=(j == nm - 1))
                            if mask is not None:
                                j += 1
                                last = e.matmul(dst, lhsT=mask[0], rhs=mask[1](c0, c1), start=False, stop=(j == nm - 1))
                    return last
                S.op("pe", smm, reads=H["r_reads"], writes=res_w)

            def emit_EXP(si):
                H, (kt, a, b, tabf, mask, bias_ap), _, _ = steps[si]
                segs = segs_of(a, b)
                Sm, St = banks[3 + si % 2], banks[5 + si % 2]
                nkp = NMETA if kt == "meta" else 128
                pt_i = si % 2

                def ex(e):
                    last = None
                    for (c0, c1, tail) in segs:
                        src = St[0:nkp, c0 - 512:c1 - 512] if tail else Sm[0:nkp, c0:c1]
                        if bias_ap is None:
                            last = e.activation(out=PT[pt_i][0:nkp, c0:c1], in_=src, func=AF.Exp)
                        else:
                            last = e.activation(out=PT[pt_i][0:nkp, c0:c1], in_=src, func=AF.Exp, bias=bias_ap[0:nkp, :], scale=1.0)
                    return last
                S.op("act", ex, reads=[rb[3 + si % 2], rb[5 + si % 2], r_const], writes=[r_PT[pt_i]])

            def emit_PV(si):
                H, (kt, a, b, tabf, mask, bias_ap), first, last_item = steps[si]
                segs = segs_of(a, b)
                nkp = NMETA if kt == "meta" else 128
                pt_i = si % 2
                vv = H["meta_v"] if kt == "meta" else H["v_fn"](kt)

                if NFILL:
                    def filler(e):
                        last = None
                        for _ in range(NFILL):
                            last = e.matmul(banks[7][:, :], lhsT=ident[:], rhs=Btab[:, 0:512], start=True, stop=True, skip_group_check=True)
                        return last
                    S.op("pe", filler, reads=[r_const], writes=[rb[7]])

                def pv(e):
                    last = None
                    for (c0, c1, tail) in segs:
                        od = Ot[:, c0 - 512:c1 - 512] if tail else Om[:, c0:c1]
                        dd = Dt[:, c0 - 512:c1 - 512] if tail else Dm[:, c0:c1]
                        e.matmul(od, lhsT=vv, rhs=PT[pt_i][0:nkp, c0:c1], start=first, stop=False, skip_group_check=True)
                        last = e.matmul(dd, lhsT=ones_bf[0:nkp, :], rhs=PT[pt_i][0:nkp, c0:c1], start=first,
                                        stop=False, skip_group_check=True)
                    return last
                S.op("pe", pv, reads=[r_PT[pt_i], r_const] + H["r_reads"], writes=[rb[0], rb[1], rb[2], rb[7]])
                if last_item:
                    sink_ap, out_ap = H["sink_ap"], H["out_ap"]

                    def oc(e):
                        e.copy(out=Oc[:, 0:512], in_=Om[:, :])
                        return e.copy(out=Oc[:, 512:514], in_=Ot)
                    S.op("act", oc, reads=[rb[0], rb[2], r_SCR], writes=[r_Oc])
                    if sink_ap is not None:
                        def f1(e):
                            e.tensor_scalar(out=rD[:, 0:512], in0=Dm[:, :], scalar1=sink_ap, scalar2=None, op0=ALU.add)
                            return e.tensor_scalar(out=rD[:, 512:514], in0=Dt, scalar1=sink_ap, scalar2=None, op0=ALU.add)
                        S.op("dve", f1, reads=[rb[1], rb[7], r_const, r_SCR], writes=[r_rD])
                        S.op("dve", lambda e: e.reciprocal(out=rD[:, :], in_=rD[:, :]), reads=[r_rD], writes=[r_rD])
                    else:
                        def f1(e):
                            e.reciprocal(out=rD[:, 0:512], in_=Dm[:, :])
                            return e.reciprocal(out=rD[:, 512:514], in_=Dt)
                        S.op("dve", f1, reads=[rb[1], rb[7], r_const, r_SCR], writes=[r_rD])
                    S.op("dve", lambda e: e.tensor_tensor(out=out_ap[:, :], in0=Oc[:, :], in1=rD[:, :], op=ALU.mult),
                         reads=[r_Oc, r_rD, r_SCR], writes=H["r_out"])

            emit_S(0)
            for si in range(len(steps)):
                if si + 1 < len(steps):
                    emit_S(si + 1)
                emit_EXP(si)
                emit_PV(si)

        def guard(res, eng="dve"):
            if eng == "act":
                S.op("act", lambda e: e.copy(out=small[:, 3:4], in_=small[:, 1:2]), reads=[r_const], writes=[res])
            else:
                S.op("dve", lambda e: e.memset(small[:, 2:3], 0.0), writes=[res])

        s1state = {"h": 0, "x": 0}

        s1buf = {}

        def s1_ln(i, w, only_a=False):
            row0 = TT * i
            np_ = 128 if w < 9 else NMETA
            src_rows = xe[row0 + 128 * w: row0 + 128 * w + 128, :] if w < 9 else meta[:, :]
            hi = s1state["h"] % 2
            s1state["h"] += 1
            hnb, r_hnx = hnbufs[hi], r_hnb[hi]
            xi = 0 if only_a else (s1state["x"] % 2)
            s1state["x"] += 1
            xsb, r_x = xsbufs[xi], r_xsb[xi]
            S.dma("sp", "xs%d" % xi, lambda e, np_=np_, src_rows=src_rows, xsb=xsb: e.dma_start(out=xsb[0:np_, :], in_=src_rows),
                  reads=[r_R2], writes=[r_x])
            gi = {"r_src": [r_x], "r_dst": None, "r_bf": [r_hnx], "extra": [r_R2], "feat": True}
            ln_tok(xsb[0:np_, :], np_, gi, out_bf=hnb[0:np_, :])
            s1buf[(i, w)] = (hnb, r_hnx, np_)

        def s1_tr(i, w):
            hnb, r_hnx, np_ = s1buf.pop((i, w))
            c0 = 128 * w
            dstf = (lambda k0, nk, c0=c0, np_=np_: h0T[:, k0:k0 + nk, c0:c0 + np_])
            wr = {0: [r_hTl], 1: [r_hTl], 2: [r_hTl, r_hTq], 3: [r_hTq], 4: [r_hTq], 5: [r_hTq], 6: [r_hTq],
                  7: [r_hTq, r_hTh], 8: [r_hTh], 9: [r_hTh]}[w]
            transpose_to(np_, dstf, [r_R2], wr, hnb=hnb, r_hnx=r_hnx, affine=(0, 16))

        def s1_subtile(i, w):
            s1_ln(i, w)
            s1_tr(i, w)

        def s1_resid(i, j):
            row0 = TT * i + 128 * (3 + j)
            S.dma("sp", "xh0_%d" % j, lambda e: e.dma_start(out=h0[:, j, :], in_=xe[row0:row0 + 128, :]), writes=[r_h0[j]])
            gi = {"r_src": [r_h0[j]], "r_dst": [r_h0[j]], "r_bf": None}
            ln_tok(h0[:, j, :], 128, gi, out_f32=h0[:, j, :])

        mov_groups = [(0, 512), (512, 1024), (1024, NWT)]

        def stage_KA(hp):
            wv, rw = wload(w_in[4 + hp, :, :], 16, 256)
            for hh in range(2):
                h = 2 * hp + hh
                for (m0, m1) in mov_groups:
                    def cons(ps, c0, c1, rbk, h=h, m0=m0):
                        S.op("act", lambda e: e.copy(out=KT_A[:, h, m0 + c0:m0 + c1], in_=ps), reads=[rbk, r_R1], writes=[r_KTA[h]])
                    fm_proj(lambda k, wv=wv, hh=hh: wv[:, k, hh * 128:(hh + 1) * 128], 16,
                            lambda k, c0, c1, m0=m0: h0T[:, k, m0 + c0:m0 + c1], m1 - m0, r_hTall + [rw], cons)


        def stage_ln2(i, j):
            gi = {"r_src": [r_h0[j]], "r_dst": [r_h0[j]], "r_bf": None}
            ln_tok(h0[:, j, :], 128, gi, out_f32=h0[:, j, :])
            r0 = TT * i + 128 * j
            S.dma("sp", "y%d" % j, lambda e, j=j, r0=r0: e.dma_start(out=y[r0:r0 + 128, :], in_=h0[:, j, :]), reads=[r_h0[j]], writes=[r_yj[j]])

        def tile(i):
            row0 = TT * i
            if i == 0:
                guard(r_R2)
                for w in range(10):
                    s1_subtile(0, w)
            load_lnp(0)
            S.dma("sp", "xh", lambda e: e.dma_start(out=xhs[:].rearrange("p k t -> p (k t)"), in_=xh[i, :, :]), writes=[r_xhs])
            S.dma("pool", "rna", lambda e: e.dma_start(out=rna[0][:], in_=rna_d[i, :, :]), writes=[r_rna[0]])

            if dbg and i == 0:
                S.dma('sp', 'dbg', lambda e: [e.dma_start(out=DBG['h0'][128 * j:128 * j + 128, :], in_=h0[:, j, :]) for j in range(4)], reads=r_h0, n_dma=4)
                S.dma('sp', 'dbg', lambda e: e.dma_start(out=DBG['h0T'][:, :], in_=h0T[:, :, :].rearrange('p a b -> p (a b)')), reads=r_hTall)
            if i == 0:
                guard(r_R1)
            if i == 0:
                for hp in range(4):
                    stage_KA(hp)
            for pc in range(4):
                s1_resid(i, pc)
                wv, rw = wload(w_in[8 + pc, :, :], 16, 256)
                for t in range(10):
                    npk = 128 if t < 9 else NMETA
                    bi = fm_state["main"] % 6
                    fm_state["main"] += 1
                    ps = banks[bi][0:npk, 0:256]

                    def mm(e, ps=ps, t=t, npk=npk, wv=wv):
                        last = None
                        for k in range(16):
                            last = e.matmul(ps, lhsT=h0T[:, k, 128 * t:128 * t + npk], rhs=wv[:, k, :], start=(k == 0), stop=(k == 15))
                        return last
                    S.op("pe", mm, reads=r_hTall + [rw], writes=[rb[bi]])
                    S.op("act", lambda e, ps=ps, t=t, npk=npk, pc=pc: e.copy(out=V_A[0:npk, t, 256 * pc:256 * pc + 256], in_=ps),
                         reads=[rb[bi], r_R1], writes=[r_VA[t]])
            tickstate["g"] = ln_feat_gen(xhs[:], lnT[:, 0:16], lnT[:, 16:32], h0h[:], [r_xhs], [r_h0h], 7)
            for hp in range(4):
                wv, rw = wload(w_in[hp, :, :], 16, 256)
                for hh in range(2):
                    h = 2 * hp + hh
                    tick(2)

                    def cons(ps, c0, c1, rbk, h=h):
                        S.op("act", lambda e: e.mul(out=QT_A[:, h, c0:c1], in_=ps, mul=SCALE), reads=[rbk, r_R1], writes=[r_QTA[h]])
                    fm_proj(lambda k, wv=wv, hh=hh: wv[:, k, hh * 128:(hh + 1) * 128], 16,
                            lambda k, c0, c1: h0T[:, k, Q0 + c0:Q0 + c1], NQ, [r_hTq, rw], cons)

            if dbg and i == 0:
                S.dma('sp', 'dbg', lambda e: [e.dma_start(out=DBG['qa'][:, :], in_=R1[:, 19584:23696]), e.dma_start(out=DBG['ka'][:, :], in_=R1[:, 0:9344]), e.dma_start(out=DBG['va'][:, :], in_=R1[:, 9344:19584])], reads=r_QTA + r_KTA + r_VA, n_dma=3)
            drain()
            guard(r_R2, "act")
            dlo, dhi = (0, 5) if i == 0 else ((-1, 4) if i == NTILES - 1 else (0, 4))
            rn = rna[i % 2]
            heads = []
            for h in range(8):
                kts = []
                for kt in range(9):
                    plo, phi = kt - 1 - dhi, kt - 1 - dlo
                    plo, phi = max(plo, -1), min(phi, 4)
                    if plo > phi:
                        continue
                    a = 0 if plo == -1 else 1 + 128 * plo
                    b = NQ if phi == 4 else 1 + 128 * (phi + 1)
                    t0 = h * 896 + (6 - kt) * 128 - 1
                    kts.append((kt, a, b,
                                (lambda c0, c1, t0=t0: Btab[:, t0 + c0:t0 + c1]),
                                (lrow[:, 128 * kt:128 * kt + 128], (lambda c0, c1, rn=rn: rn[:, c0:c1])),
                                None))
                heads.append(dict(kt_fn=(lambda kt, h=h: KT_A[:, h, 128 * kt:128 * kt + 128]),
                                  v_fn=(lambda kt, h=h: V_A[:, kt, 128 * h:128 * h + 128]),
                                  qT=QT_A[:, h, :], key_tiles=kts, meta_k=KT_A[:, h, NW:NWT],
                                  meta_v=V_A[0:NMETA, 9, 128 * h:128 * h + 128], out_ap=oaT[:, h, :],
                                  r_reads=[r_KTA[h], r_QTA[h], r_const, r_rna[i % 2], r_R1] + r_VA,
                                  r_out=[r_oa[h], r_R2], sink_ap=None))
            attn_phase(heads)

            guard(r_R1)
            wv, rw = wload(w_in[16, :, :], 16, 256)
            for g in range(2):
                for (m0, m1) in mov_groups:
                    def cons(ps, c0, c1, rbk, g=g, m0=m0):
                        S.op("act", lambda e: e.copy(out=KT_B[:, g, m0 + c0:m0 + c1], in_=ps), reads=[rbk, r_R1], writes=[r_KTB[g]])
                    fm_proj(lambda k, wv=wv, g=g: wv[:, k, g * 128:(g + 1) * 128], 16,
                            lambda k, c0, c1, m0=m0: h0T[:, k, m0 + c0:m0 + c1], m1 - m0, r_hTall + [rw], cons)
            wv, rw = wload(w_in[17, :, :], 16, 256)
            for t in range(10):
                npk = 128 if t < 9 else NMETA
                bi = fm_state["main"] % 6
                fm_state["main"] += 1
                ps = banks[bi][0:npk, 0:256]

                def mm(e, ps=ps, t=t, npk=npk, wv=wv):
                    last = None
                    for k in range(16):
                        last = e.matmul(ps, lhsT=h0T[:, k, 128 * t:128 * t + npk], rhs=wv[:, k, :], start=(k == 0), stop=(k == 15))
                    return last
                S.op("pe", mm, reads=r_hTall + [rw], writes=[rb[bi]])
                S.op("act", lambda e, ps=ps, t=t, npk=npk: e.copy(out=V_B[0:npk, t, :], in_=ps),
                     reads=[rb[bi], r_R1], writes=[r_VB[t]])
            for hp in range(4):
                wv, rw = wload(w_in[12 + hp, :, :], 16, 256)
                for hh in range(2):
                    h = 2 * hp + hh

                    def cons(ps, c0, c1, rbk, h=h):
                        S.op("act", lambda e: e.mul(out=QT_B[:, h, c0:c1], in_=ps, mul=SCALE), reads=[rbk, r_R1], writes=[r_QTB[h]])
                    fm_proj(lambda k, wv=wv, hh=hh: wv[:, k, hh * 128:(hh + 1) * 128], 16,
                            lambda k, c0, c1: h0T[:, k, Q0 + c0:Q0 + c1], NQ, [r_hTq, rw], cons)

            heads = []
            for hq in range(8):
                g = hq // 4
                kts = []
                for kb in range(1, 9):
                    a = max(0, 1 + 128 * (kb - 4))
                    b = min(NQ, 1 + 128 * (kb - 1))
                    if a >= b:
                        continue
                    t0 = hq * 384 + (4 - kb) * 128 - 1
                    kts.append((kb, a, b, (lambda c0, c1, t0=t0: BWtab[:, t0 + c0:t0 + c1]), None,
                                fsw[:, i * 9 + kb:i * 9 + kb + 1]))
                heads.append(dict(kt_fn=(lambda kb, g=g: KT_B[:, g, 128 * kb:128 * kb + 128]),
                                  v_fn=(lambda kb, g=g: V_B[:, kb, 128 * g:128 * g + 128]),
                                  qT=QT_B[:, hq, :], key_tiles=kts, meta_k=KT_B[:, g, NW:NWT],
                                  meta_v=V_B[0:NMETA, 9, 128 * g:128 * g + 128], out_ap=obT[:, hq, :],
                                  r_reads=[r_KTB[g], r_QTB[hq], r_const, r_R1] + r_VB,
                                  r_out=[r_ob[hq], r_R2], sink_ap=esink[:, hq:hq + 1]))
            attn_phase(heads)

            if dbg and i == 0:
                S.dma('sp', 'dbg', lambda e: [e.dma_start(out=DBG['oa'][:, :], in_=R2[:, 0:4112]), e.dma_start(out=DBG['ob'][:, :], in_=R2[:, 4112:8224])], reads=r_oa + r_ob, n_dma=2)
            guard(r_R1)
            SCRb = SCR[:, :].bitcast(BF16)
            sgt = {("ga", 0): SCRb[:, 0:514], ("ga", 1): SCRb[:, 514:1028], ("gb", 0): SCRb[:, 1028:1542], ("gb", 1): SCRb[:, 1542:2056]}
            r_sg = {k: Res("sg%s%d" % k) for k in sgt}
            t1 = SCR[:, 1032:1546]
            r_t1 = Res("t1")
            guard(r_SCR)
            for cp in range(8):
                for nm, wc0 in (("ga", 4608), ("gb", 6656)):
                    wv, rw = wload(w_in[wc0 // 256 + cp, :, :], 16, 256)
                    for cc in range(2):
                        dst, rd = sgt[(nm, cc)], r_sg[(nm, cc)]

                        def cons(ps, c0, c1, rbk, dst=dst, rd=rd):
                            S.op("act", lambda e: e.activation(out=dst[:, c0:c1], in_=ps, func=AF.Sigmoid), reads=[rbk, r_SCR], writes=[rd])
                        fm_proj(lambda k, wv=wv, cc=cc: wv[:, k, 128 * cc:128 * cc + 128], 16,
                                lambda k, c0, c1: h0T[:, k, Q0 + c0:Q0 + c1], NQ, [r_hTq, rw], cons, main_banks=(0, 1))
                wva, rwa = wload(w_pa[cp, :, :], 8, 256)
                wvb, rwb = wload(w_pb[cp, :, :], 8, 256)
                for cc in range(2):
                    c = 2 * cp + cc
                    sa, ra = sgt[("ga", cc)], r_sg[("ga", cc)]
                    sb_, rb_ = sgt[("gb", cc)], r_sg[("gb", cc)]

                    def consa(ps, c0, c1, rbk, sa=sa, ra=ra):
                        S.op("dve", lambda e: e.tensor_tensor(out=t1[:, c0:c1], in0=ps, in1=sa[:, c0:c1], op=ALU.mult),
                             reads=[rbk, ra, r_SCR], writes=[r_t1])
                    fm_proj(lambda k, wva=wva, cc=cc: wva[:, k, 128 * cc:128 * cc + 128], 8,
                            lambda k, c0, c1: oaT[:, k, c0:c1], NQ, r_oa + [rwa, r_R2], consa, main_banks=(2,), tail_banks=(4,))

                    def consb(ps, c0, c1, rbk, c=c, sb_=sb_, rb_=rb_):
                        S.op("dve", lambda e: e.tensor_tensor(out=ps, in0=ps, in1=sb_[:, c0:c1], op=ALU.mult),
                             reads=[rb_, r_SCR], writes=[rbk])
                        S.op("dve", lambda e: e.tensor_tensor(out=mergedT[:, c, c0:c1], in0=ps, in1=t1[:, c0:c1], op=ALU.add),
                             reads=[rbk, r_t1, r_R1, r_SCR], writes=[r_mg[c]])
                    fm_proj(lambda k, wvb=wvb, cc=cc: wvb[:, k, 128 * cc:128 * cc + 128], 8,
                            lambda k, c0, c1: obT[:, k, c0:c1], NQ, r_ob + [rwb, r_R2], consb, main_banks=(3,), tail_banks=(5,))
            guard(r_SCR)

            if dbg and i == 0:
                S.dma('sp', 'dbg', lambda e: e.dma_start(out=DBG['mg'][:, :], in_=R1[:, 0:8224]), reads=r_mg + [r_R1])
            load_lnp(1)
            guard(r_R2)
            for cb in range(4):
                pcs = []
                for q in range(2):
                    pcs.append(wload(w_out[2 * cb + q, :, :], 8, 512))
                    wv, rw = pcs[q]

                    def mm(e, q=q, wv=wv):
                        last = None
                        for j in range(4):
                            for kk in range(8):
                                k = 8 * q + kk
                                last = e.matmul(banks[j][:, :], lhsT=mergedT[:, k, 1 + 128 * j:129 + 128 * j], rhs=wv[:, kk, :],
                                                start=(k == 0), stop=(k == 15), skip_group_check=True)
                        return last
                    S.op("pe", mm, reads=r_mg + [rw, r_R1], writes=[rb[0], rb[1], rb[2], rb[3]])
                for j in range(4):
                    S.op("dve", lambda e, j=j, cb=cb: e.scalar_tensor_tensor(
                        out=h0[:, j, 512 * cb:512 * cb + 512], in0=h0[:, j, 512 * cb:512 * cb + 512], scalar=ALPHA,
                        in1=banks[j][:, :], op0=ALU.mult, op1=ALU.add), reads=[rb[j]], writes=[r_h0[j]])
                hps = banks[5]

                def hmm(e, cb=cb, pcs=pcs):
                    last = None
                    for cc in range(4):
                        for k in range(16):
                            wv = pcs[k // 8][0]
                            last = e.matmul(hps[:, 2 * cc:2 * cc + 2], lhsT=wv[:, k % 8, 128 * cc:128 * cc + 128],
                                            rhs=mergedT[:, k, 0:NQ:NQ - 1], start=(k == 0 and cc == 0), stop=(k == 15),
                                            skip_group_check=True)
                    return last
                S.op("pe", hmm, reads=r_mg + [pcs[0][1], pcs[1][1], r_R1], writes=[rb[5]])
                S.op("dve", lambda e, cb=cb: e.scalar_tensor_tensor(
                    out=zh[:, 4 * cb:4 * cb + 4, :], in0=h0h[:, 4 * cb:4 * cb + 4, :], scalar=ALPHA,
                    in1=hps[:, 0:8].rearrange("p (c t) -> p c t", c=4), op0=ALU.mult, op1=ALU.add),
                    reads=[rb[5], r_h0h], writes=[r_zh])
            tickstate["g"] = ln_feat_gen(zh[:], lnT[:, 32:48], lnT[:, 48:64], zh[:], [r_zh], [r_zh], 5)
            for j in range(4):
                hi = lnstate["n"] % 2
                hnb, r_hnx = hnbufs[hi], r_hnb[hi]
                gi = {"r_src": [r_h0[j]], "r_dst": [r_h0[j]], "r_bf": [r_hnx], "extra": [r_R2]}
                ln_tok(h0[:, j, :], 128, gi, out_f32=h0[:, j, :], out_bf=hnb[:, :])
                c0 = Q0 + 1 + 128 * j
                transpose_to(128, lambda k0, nk, c0=c0: h0T[:, k0:k0 + nk, c0:c0 + 128], [r_R2], [r_hTq], hnb=hnb, r_hnx=r_hnx)
            drain()

            def hcols(e):
                e.tensor_copy(out=h0T[:, :, Q0], in_=zh[:, :, 0])
                return e.tensor_scalar(out=h0T[:, :, Q0 + NQ - 1], in0=zh[:, :, 1], scalar1=flag[:, i:i + 1], scalar2=None, op0=ALU.mult)
            S.op("dve", hcols, reads=[r_zh, r_const], writes=[r_hTq])

            if dbg and i == 0:
                S.dma('sp', 'dbg', lambda e: [e.dma_start(out=DBG['h1'][128 * j:128 * j + 128, :], in_=h0[:, j, :]) for j in range(4)], reads=r_h0, n_dma=4)
            guard(r_R1)
            Araw2 = [SCR[:, 0:514], SCR[:, 514:1028]]
            cbuf6 = SCR[:, 1028:1540]
            vbuf6 = PT[0][:, 0:512]
            gbuf6 = PT[1][:, 0:512]
            r_Araw = [Res("Araw0"), Res("Araw1")]
            r_cbuf = Res("cbuf")
            guard(r_SCR)
            sched6 = {1: [("ln", 0)], 4: [("tr", 0)], 5: [("ln", 1)], 8: [("tr", 1)], 9: [("ln", 8)], 12: [("tr", 8)],
                      13: [("ln", 9)], 16: [("tr", 9)]}
            if i + 1 < ntiles:
                guard(r_R2)
            gpiece, vpiece = {}, {}

            def emit_gate(c):
                cp, cc = c // 2, c % 2
                if cc == 0:
                    gpiece[cp] = wload(w_fi[cp, :, :], 16, 256)
                wg, rwg = gpiece[cp]
                Ar, rA = Araw2[c % 2], r_Araw[c % 2]

                def consg(ps, c0, c1, rbk):
                    if c0 == 0:
                        S.op("act", lambda e: e.copy(out=Ar[:, c0:c1], in_=ps), reads=[rbk, r_SCR], writes=[rA])
                    else:
                        S.op("act", lambda e: e.copy(out=Ar[:, c0:c1], in_=ps), reads=[rbk, r_SCR], cowrites=[rA])
                fm_proj(lambda k: wg[:, k, cc * 128:cc * 128 + 128], 16,
                        lambda k, c0, c1: h0T[:, k, Q0 + c0:Q0 + c1], NQ, [r_hTq, rwg], consg)

            def emit_val(c):
                cp, cc = c // 2, c % 2
                wvv, rwv = vpiece[cp]

                def consv(ps, c0, c1, rbk):
                    lo, hi = max(c0, 1), min(c1, 513)
                    if c0 == 0:
                        S.op("act", lambda e: e.copy(out=vbuf6[:, lo - 1:hi - 1], in_=ps[:, lo - c0:hi - c0]), reads=[rbk], writes=[r_PT[0]])
                    else:
                        S.op("act", lambda e: e.copy(out=vbuf6[:, lo - 1:hi - 1], in_=ps[:, lo - c0:hi - c0]), reads=[rbk], cowrites=[r_PT[0]])
                fm_proj(lambda k: wvv[:, k, cc * 128:cc * 128 + 128], 16,
                        lambda k, c0, c1: h0T[:, k, Q0 + c0:Q0 + c1], NQ, [r_hTq, rwv], consv)

            emit_gate(0)
            for c in range(NCH):
                cp = c // 2
                if c % 2 == 0:
                    if i + 1 < ntiles:
                        for (act_, w) in sched6.get(cp, []):
                            if act_ == "ln":
                                s1_ln(i + 1, w, only_a=True)
                            else:
                                s1_tr(i + 1, w)
                    vpiece[cp] = wload(w_fi[NCH // 2 + cp, :, :], 16, 256)
                w0 = convw[:, 0 * NCH + c:0 * NCH + c + 1]
                w1 = convw[:, 1 * NCH + c:1 * NCH + c + 1]
                w2 = convw[:, 2 * NCH + c:2 * NCH + c + 1]
                bb = convw[:, 3 * NCH + c:3 * NCH + c + 1]
                Ar, rA = Araw2[c % 2], r_Araw[c % 2]
                S.op("act", lambda e, w1=w1, bb=bb, Ar=Ar: e.activation(out=cbuf6[:, :], in_=Ar[:, 1:513], func=AF.Identity, bias=bb, scale=w1),
                     reads=[rA, r_const, r_SCR], writes=[r_cbuf])
                S.op("dve", lambda e, w0=w0, Ar=Ar: e.scalar_tensor_tensor(out=cbuf6[:, :], in0=Ar[:, 0:512], scalar=w0, in1=cbuf6[:, :], op0=ALU.mult, op1=ALU.add),
                     reads=[rA, r_const, r_SCR], writes=[r_cbuf])
                S.op("dve", lambda e, w2=w2, Ar=Ar: e.scalar_tensor_tensor(out=cbuf6[:, :], in0=Ar[:, 2:514], scalar=w2, in1=cbuf6[:, :], op0=ALU.mult, op1=ALU.add),
                     reads=[rA, r_const, r_SCR], writes=[r_cbuf])
                emit_val(c)
                if c + 1 < NCH:
                    emit_gate(c + 1)
                S.op("act", lambda e: e.activation(out=gbuf6[:, :], in_=cbuf6[:, :], func=AF.Gelu_apprx_tanh), reads=[r_cbuf], writes=[r_PT[1]])
                S.op("dve", lambda e, c=c: e.tensor_tensor(out=uT[:, c, 1:513], in0=gbuf6[:, :], in1=vbuf6[:, :], op=ALU.mult),
                     reads=[r_PT[0], r_PT[1], r_R1], writes=[r_uT[c]])

            if dbg and i == 0:
                S.dma('sp', 'dbg', lambda e: e.dma_start(out=DBG['uT'][:, :], in_=R1[:, 0:22616]), reads=r_uT + [r_R1])
            load_lnp(2)
            sched7 = {0: [("ln", 2)], 1: [("ln", 7)], 3: [("tr", 2)], 4: [("ln", 3)], 6: [("tr", 7)], 7: [("ln", 4)],
                      9: [("tr", 3)], 10: [("ln", 5)], 12: [("tr", 4)], 13: [("ln", 6)], 15: [("tr", 5)], 18: [("tr", 6)]}
            for cb in range(4):
                ksz = [8, 8, 8, 8, 8, 4]
                for q in range(6):
                    if i + 1 < ntiles:
                        for (act_, w) in sched7.get(6 * cb + q, []):
                            if act_ == "ln":
                                s1_ln(i + 1, w)
                            else:
                                s1_tr(i + 1, w)
                    wv, rw = wload(w_fd[6 * cb + q, :, 0:ksz[q] * 512], ksz[q], 512)

                    def mm(e, q=q, wv=wv):
                        last = None
                        for j in range(4):
                            for kk in range(ksz[q]):
                                k = 8 * q + kk
                                last = e.matmul(banks[j][:, :], lhsT=uT[:, k, 1 + 128 * j:129 + 128 * j], rhs=wv[:, kk, :],
                                                start=(k == 0), stop=(k == NCH - 1), skip_group_check=True)
                        return last
                    S.op("pe", mm, reads=r_uT + [rw, r_R1], writes=[rb[0], rb[1], rb[2], rb[3]])
                for j in range(4):
                    S.op("dve", lambda e, j=j, cb=cb: e.scalar_tensor_tensor(
                        out=h0[:, j, 512 * cb:512 * cb + 512], in0=h0[:, j, 512 * cb:512 * cb + 512], scalar=ALPHA,
                        in1=banks[j][:, :], op0=ALU.mult, op1=ALU.add), reads=[rb[j]], writes=[r_h0[j]])
            if i + 1 < ntiles:
                guard(r_R1, "act")
                for hp in range(4):
                    stage_KA(hp)
                    stage_ln2(i, hp)
            else:
                for j in range(4):
                    stage_ln2(i, j)

        setup()
        for i in range(ntiles):
            tile(i)
        S.emit()
    return nc


def _static_tables():
    kp = np.arange(128) // 64
    kc = np.arange(128) % 64
    qp = kp.copy()
    qc = kc.copy()
    col_start = np.clip(qc - 8, 0, 48)
    col_ok = (kc[:, None] >= col_start[None, :]) & (kc[:, None] < col_start[None, :] + 16)
    dc = np.clip(kc[:, None] - qc[None, :], -15, 15) + 15
    dr_idx = np.zeros((7, 128, 128), np.int64)
    nab_m = np.zeros((7, 128, 128), np.float32)
    for b in range(7):
        d = 5 - b
        dr = 2 * (d - 2) + kp[:, None] - qp[None, :]
        ok = (np.abs(dr) <= 7) & col_ok
        dr_idx[b] = np.clip(dr, -7, 7) + 7
        nab_m[b] = np.where(ok, 0.0, NEG)
    slopes = 2.0 ** (-(np.arange(1, 9, dtype=np.float64)))
    swb = np.zeros((8, 3, 128, 128), np.float32)
    k = np.arange(128)
    for b in range(3):
        dd = 1 - b
        dist = np.abs(128 * dd + k[:, None] - k[None, :])
        for h in range(8):
            swb[h, b] = np.where(dist <= 128, -slopes[h] * dist, NEG)
    lrow = (np.arange(NW)[None, :] // 64 == np.arange(18)[:, None]).astype(np.float32)
    return dr_idx, dc, nab_m, swb, lrow


def _core_geom(c):
    if c < 4:
        return ("p", 0, TOK * c, 16384)
    return ("s", c - 4, 0, 4096)


def _core_tables(s, tseq):
    rows_seq = tseq // 64
    rna = np.full((NTILES, 18, NQ), NEG, np.float32)
    fsw = np.zeros((128, NTILES * 9), np.float32)
    flag = np.zeros((128, NTILES), np.float32)
    for i in range(NTILES):
        t0 = s + TT * i
        R0 = t0 // 64
        for qcol in range(NQ):
            if qcol == 0:
                rq = R0 - 1
            elif qcol == NQ - 1:
                rq = R0 + 8
            else:
                rq = R0 + (qcol - 1) // 64
            if rq < 0 or rq >= rows_seq:
                continue
            rs = min(max(rq - 4, 0), rows_seq - 8)
            for rr in range(18):
                rk = R0 + rr - 6
                if rs <= rk < rs + 8:
                    rna[i, rr, qcol] = 0.0
        for kb in range(9):
            blk = t0 // 128 + kb - 3
            if blk < 0 or blk >= tseq // 128:
                fsw[:, i * 9 + kb] = NEG
        flag[:, i] = 1.0 if t0 + TT < tseq else 0.0
    return rna, fsw, flag


def _pieces(w, nk, cols):
    C = w.shape[1]
    return np.ascontiguousarray(w.reshape(nk, 128, C // cols, cols).transpose(2, 1, 0, 3).reshape(C // cols, 128, nk * cols))


def _kpieces(w, ksz):
    out = np.zeros((4 * len(ksz), 128, 8 * 512), np.float32)
    for cb in range(4):
        k0 = 0
        for q, n in enumerate(ksz):
            blk = w[k0 * 128:(k0 + n) * 128, 512 * cb:512 * cb + 512].reshape(n, 128, 512).transpose(1, 0, 2)
            out[cb * len(ksz) + q, :, :n * 512] = blk.reshape(128, n * 512)
            k0 += n
    return out


def _prep(inputs):
    f = lambda a: np.ascontiguousarray(np.asarray(a, dtype=np.float32))
    xp = f(inputs["x_prompt"])[0]
    xsm = f(inputs["x_sample"])
    meta = f(inputs["meta_tokens"])
    dr_idx, dc, nab_m, swb, lrow = _static_tables()
    rpb = f(inputs["na_rpb"])[0]
    nab_g = rpb[:, dr_idx, dc[None, :, :]]
    nab_g = np.ascontiguousarray(nab_g.transpose(2, 0, 1, 3).reshape(128, 7168))
    nab_mm = np.ascontiguousarray(np.broadcast_to(nab_m[None], (8, 7, 128, 128)).transpose(2, 0, 1, 3).reshape(128, 7168))
    swb2 = np.ascontiguousarray(swb.transpose(2, 0, 1, 3).reshape(128, 3072))
    lnv = np.stack([f(inputs["ln_emb_g"]), f(inputs["ln_emb_b"]), f(inputs["ln1_g"])[0], f(inputs["ln1_b"])[0],
                    f(inputs["ln2_g"])[0], f(inputs["ln2_b"])[0]])
    lnT = np.ascontiguousarray(np.concatenate([v.reshape(16, 128).T for v in lnv[:4]], axis=1))
    cw = f(inputs["ffn_conv_w"])[0]
    cbias = f(inputs["ffn_conv_b"])[0]
    convw = np.ascontiguousarray(np.concatenate([cw[j].reshape(NCH, 128).T for j in range(3)] + [cbias.reshape(NCH, 128).T], axis=1))
    sink = np.ascontiguousarray(np.broadcast_to(f(inputs["sw_sink"])[0][None, :], (128, 8)))
    shared = {
        "meta": meta, "lnv": np.ascontiguousarray(lnv), "lnT": lnT, "convw": convw, "sink": sink,
        "nab_g": nab_g, "nab_m": nab_mm, "swb": swb2, "lrow": np.ascontiguousarray(lrow),
        "w_in": _pieces(f(inputs["w_in"])[0], 16, 256), "w_pa": _pieces(f(inputs["w_proj_na"])[0], 8, 256),
        "w_pb": _pieces(f(inputs["w_proj_sw"])[0], 8, 256), "w_out": _kpieces(f(inputs["w_out"])[0], [8, 8]),
        "w_fi": _pieces(f(inputs["w_ffn_in"])[0], 16, 256), "w_fd": _kpieces(f(inputs["w_ffn_down"])[0], [8, 8, 8, 8, 8, 4]),
    }
    in_maps = []
    for c in range(8):
        kind, b, s, tseq = _core_geom(c)
        xsrc = xp if kind == "p" else xsm[b]
        xe = np.zeros((XE_ROWS, D), np.float32)
        lo, hi = s - NB, s - NB + XE_ROWS
        a0, a1 = max(lo, 0), min(hi, tseq)
        xe[a0 - lo:a1 - lo] = xsrc[a0:a1]
        if s == 0:
            xe[NB - 1] = meta[NMETA - 1]
        xh = np.zeros((NTILES, 128, 16, 2), np.float32)
        for i in range(NTILES):
            for t, row in enumerate((NB + TT * i - 1, NB + TT * i + TT)):
                xh[i, :, :, t] = xe[row].reshape(16, 128).T
        rna, fsw, flag = _core_tables(s, tseq)
        m = dict(shared)
        m.update({"xe": xe, "xh": np.ascontiguousarray(xh.reshape(NTILES, 128, 32)), "rna": rna, "fsw": fsw, "flag": flag})
        in_maps.append(m)
    return in_maps


_NC_CACHE = {}


def kernel(**inputs):
    in_maps = _prep(inputs)
    if "nc" not in _NC_CACHE:
        _NC_CACHE["nc"] = build_nc()
    res = run_bass_kernel_spmd(_NC_CACHE["nc"], in_maps, core_ids=list(range(8)))
    ys = [np.asarray(r["y"], dtype=np.float32) for r in res.results]
    y_prompt = np.concatenate(ys[0:4], axis=0)[None]
    y_sample = np.stack(ys[4:8], axis=0)
    return (y_prompt, y_sample)
```

```python
import contextlib
import numpy as np
import concourse.bass as bass
import concourse.mybir as mybir
from concourse.bass_utils import run_bass_kernel_spmd

F32 = mybir.dt.float32
BF16 = mybir.dt.bfloat16
AF = mybir.ActivationFunctionType
ALU = mybir.AluOpType

D = 2048
NMETA = 16
DFF = 5632
NCH = DFF // 128
INC = 8704
TOK = 4096
TT = 512
NTILES = TOK // TT
NB = 384
NW = 1152
NWT = NW + NMETA
NQ = 514
Q0 = NB - 1
XE_ROWS = NB + TOK + 256
ALPHA = 2.0 ** 0.25
EPS = 1e-5
NEG = -30000.0
SCALE = 128.0 ** -0.5
NFILL = 0


class Res:
    __slots__ = ("name", "writers", "readers", "prev")

    def __init__(self, name):
        self.name = name
        self.writers = []
        self.readers = []
        self.prev = []


class Op:
    __slots__ = ("eng", "fn", "deps", "dma", "stream", "signal", "semval", "idx", "n_dma")

    def __init__(self, eng, fn, dma=False, stream=None, n_dma=1):
        self.eng = eng
        self.fn = fn
        self.deps = []
        self.dma = dma
        self.stream = stream
        self.signal = False
        self.semval = None
        self.idx = None
        self.n_dma = n_dma


class Sched:
    ENGS = ("pe", "act", "dve", "pool", "sp")

    def __init__(self, nc):
        self.nc = nc
        self.ops = []

    def op(self, eng, fn, reads=(), writes=(), dma=False, stream=None, n_dma=1, cowrites=()):
        o = Op(eng, fn, dma=dma, stream=stream, n_dma=n_dma)
        o.idx = len(self.ops)
        deps = set()
        writes = list(writes)
        co = []
        for r in cowrites:
            if r.readers or not r.writers:
                writes.append(r)
            else:
                co.append(r)
        for r in reads:
            deps.update(r.writers)
        for r in writes:
            deps.update(r.writers)
            deps.update(r.readers)
        for r in co:
            deps.update(r.prev)
        o.deps = sorted(deps, key=lambda d: d.idx)
        for r in reads:
            r.readers.append(o)
        for r in writes:
            r.prev = list(r.writers) + list(r.readers)
            r.writers = [o]
            r.readers = []
        for r in co:
            r.writers.append(o)
        self.ops.append(o)
        return o

    def dma(self, eng, stream, fn, reads=(), writes=(), n_dma=1):
        return self.op(eng, fn, reads=reads, writes=writes, dma=True, stream=stream, n_dma=n_dma)

    def emit(self):
        nc = self.nc
        for o in self.ops:
            if o.dma:
                o.signal = True
            for d in o.deps:
                if d.dma:
                    continue
                if d.eng == o.eng and d.eng == "pe" and not o.dma:
                    continue
                d.signal = True
        cnt = {e: 0 for e in self.ENGS}
        scnt = {}
        for o in self.ops:
            if o.dma:
                scnt[o.stream] = scnt.get(o.stream, 0) + 16 * o.n_dma
                o.semval = scnt[o.stream]
            elif o.signal:
                cnt[o.eng] += 1
                o.semval = cnt[o.eng]
        stream_names = sorted(scnt.keys())
        with contextlib.ExitStack() as es:
            esem = {e: es.enter_context(nc.semaphore("s_" + e)) for e in self.ENGS}
            ssem = {s: es.enter_context(nc.semaphore("d_" + s)) for s in stream_names}
            block = es.enter_context(nc.Block())
            by_eng = {e: [o for o in self.ops if o.eng == e] for e in self.ENGS}

            def run(eng_name, eng):
                waited = {}
                for o in by_eng[eng_name]:
                    need = {}
                    for d in o.deps:
                        if d.dma:
                            key = ("s", d.stream)
                        else:
                            if d.eng == eng_name and eng_name == "pe" and not o.dma:
                                continue
                            key = ("e", d.eng)
                        if d.semval > need.get(key, 0):
                            need[key] = d.semval
                    for key, v in need.items():
                        if waited.get(key, 0) >= v:
                            continue
                        waited[key] = v
                        sem = ssem[key[1]] if key[0] == "s" else esem[key[1]]
                        eng.wait_ge(sem, v)
                    ins = o.fn(eng)
                    if o.dma:
                        if not isinstance(ins, (list, tuple)):
                            ins = [ins]
                        assert len(ins) == o.n_dma, (len(ins), o.n_dma)
                        for i in ins:
                            i.then_inc(ssem[o.stream], 16)
                    elif o.signal:
                        ins.then_inc(esem[eng_name], 1)
                if eng_name == "sp":
                    for s in stream_names:
                        if waited.get(("s", s), 0) < scnt[s]:
                            eng.wait_ge(ssem[s], scnt[s])
                    for e in self.ENGS:
                        if e != "sp" and cnt[e] > 0:
                            eng.wait_ge(esem[e], cnt[e])

            @block.tensor
            def _(eng):
                run("pe", eng)

            @block.scalar
            def _(eng):
                run("act", eng)

            @block.vector
            def _(eng):
                run("dve", eng)

            @block.gpsimd
            def _(eng):
                run("pool", eng)

            @block.sync
            def _(eng):
                run("sp", eng)


def build_nc(ntiles=NTILES, dbg=False):
    nc = bass.Bass("TRN2", target_bir_lowering=False)
    DBG = {}
    if dbg:
        DBG['h0'] = nc.dram_tensor('d_h0', [512, D], F32, kind='ExternalOutput').ap()
        DBG['h1'] = nc.dram_tensor('d_h1', [512, D], F32, kind='ExternalOutput').ap()
        DBG['oa'] = nc.dram_tensor('d_oa', [128, 8 * NQ], BF16, kind='ExternalOutput').ap()
        DBG['ob'] = nc.dram_tensor('d_ob', [128, 8 * NQ], BF16, kind='ExternalOutput').ap()
        DBG['mg'] = nc.dram_tensor('d_mg', [128, 16 * NQ], BF16, kind='ExternalOutput').ap()
        DBG['uT'] = nc.dram_tensor('d_uT', [128, NCH * NQ], BF16, kind='ExternalOutput').ap()
        DBG['qa'] = nc.dram_tensor('d_qa', [128, 8 * NQ], BF16, kind='ExternalOutput').ap()
        DBG['ka'] = nc.dram_tensor('d_ka', [128, 8 * NWT], BF16, kind='ExternalOutput').ap()
        DBG['va'] = nc.dram_tensor('d_va', [128, 10 * 1024], BF16, kind='ExternalOutput').ap()
        DBG['h0T'] = nc.dram_tensor('d_h0T', [128, 16 * NWT], BF16, kind='ExternalOutput').ap()

    def din(name, shape):
        return nc.dram_tensor(name, list(shape), F32, kind="ExternalInput").ap()

    xe = din("xe", [XE_ROWS, D])
    xh = din("xh", [NTILES, 128, 32])
    meta = din("meta", [NMETA, D])
    lnv = din("lnv", [6, D])
    lnT_d = din("lnT", [128, 64])
    convw_d = din("convw", [128, 4 * NCH])
    sink_d = din("sink", [128, 8])
    nabg_d = din("nab_g", [128, 7168])
    nabm_d = din("nab_m", [128, 7168])
    swb_d = din("swb", [128, 3072])
    lrow_d = din("lrow", [18, NW])
    rna_d = din("rna", [NTILES, 18, NQ])
    fsw_d = din("fsw", [128, NTILES * 9])
    flag_d = din("flag", [128, NTILES])
    w_in = din("w_in", [INC // 256, 128, 4096])
    w_pa = din("w_pa", [8, 128, 2048])
    w_pb = din("w_pb", [8, 128, 2048])
    w_out = din("w_out", [8, 128, 4096])
    w_fi = din("w_fi", [2 * DFF // 256, 128, 4096])
    w_fd = din("w_fd", [24, 128, 4096])
    y = nc.dram_tensor("y", [TOK, D], F32, kind="ExternalOutput").ap()

    es = contextlib.ExitStack()
    with es:
        def sb(name, shape, dt):
            return es.enter_context(nc.sbuf_tensor("sb_" + name, list(shape), dt))

        h0 = sb("h0", [128, 4, D], F32)
        h0T = sb("h0T", [128, 16, NWT], BF16)
        R1 = sb("R1", [128, 23808], BF16)
        R2 = sb("R2", [128, 8224], BF16)
        wbuf = [sb("wb%d" % i, [128, 4096], BF16) for i in range(3)]
        LNp = sb("LNp", [128, 2, D], F32)
        Btab = sb("Btab", [128, 7168], BF16)
        BWtab = sb("BWtab", [128, 3072], BF16)
        lrow = sb("lrow", [18, NW], BF16)
        rna = [sb("rna0", [18, NQ], BF16)] * 2
        ident = sb("ident", [128, 128], BF16)
        ones_bf = sb("ones_bf", [128, 128], BF16)
        ones_f = sb("ones_f", [128, 128], F32)
        SCR = sb("SCR", [128, 2048], F32)
        PT = [sb("PT%d" % i, [128, NQ], BF16) for i in range(2)]
        lnT = sb("lnT", [128, 64], F32)
        convw = sb("convw", [128, 4 * NCH], F32)
        esink = sb("esink", [128, 8], F32)
        fsw = sb("fsw", [128, NTILES * 9], F32)
        flag = sb("flag", [128, NTILES], F32)
        small = sb("small", [128, 64], F32)
        xhs = sb("xhs", [128, 16, 2], F32)
        h0h = sb("h0h", [128, 16, 2], F32)
        zh = sb("zh", [128, 16, 2], F32)
        zh2 = sb("zh2", [128, 16, 2], F32)
        bnst = sb("bnst", [128, 48], F32)

        KT_A = R1[:, 0:9344].rearrange("p (h n) -> p h n", h=8)
        V_A = R1[:, 9344:19584].rearrange("p (t c) -> p t c", t=10)
        QT_A = R1[:, 19584:23696].rearrange("p (h n) -> p h n", h=8)
        KT_B = R1[:, 0:2336].rearrange("p (h n) -> p h n", h=2)
        V_B = R1[:, 2336:4896].rearrange("p (t c) -> p t c", t=10)
        QT_B = R1[:, 4896:9008].rearrange("p (h n) -> p h n", h=8)
        mergedT = R1[:, 0:8224].rearrange("p (c n) -> p c n", c=16)
        uT = R1[:, 0:22616].rearrange("p (c n) -> p c n", c=NCH)
        xs = R2[:, 0:4096].bitcast(F32)
        hn = R2[:, 4096:6144]
        hnbufs = [R2[:, 4096:6144], R2[:, 6144:8192]]
        xsbufs = [R2[:, 0:4096].bitcast(F32), SCR[:, 0:2048]]
        oaT = R2[:, 0:4112].rearrange("p (h n) -> p h n", h=8)
        obT = R2[:, 4112:8224].rearrange("p (h n) -> p h n", h=8)
        sga = SCR[:, 0:NQ]
        sgb = SCR[:, 514:514 + NQ]
        rD = SCR[:, 1028:1028 + NQ]
        Araw = SCR[:, 0:516]
        cbuf = SCR[:, 516:1028]
        gbuf = SCR[:, 1028:1540]

        banks = [es.enter_context(nc.psum_tensor("pb%d" % i, [128, 512], F32)) for i in range(8)]
        rb = [Res("bank%d" % i) for i in range(8)]

        S = Sched(nc)
        r_h0 = [Res("h0_%d" % j) for j in range(4)]
        r_hTl, r_hTq, r_hTh = Res("h0T_lo"), Res("h0T_q"), Res("h0T_hi")
        r_hTall = [r_hTl, r_hTq, r_hTh]
        r_R1 = Res("R1guard")
        r_R2 = Res("R2guard")
        r_xs, r_hn = Res("xs"), Res("hn")
        r_hnb = [Res("hnA"), Res("hnB")]
        r_bnb = [Res("bn0"), Res("bn1")]
        r_smb = [Res("sm0"), Res("sm1")]
        r_wb = [Res("wb%d" % i) for i in range(3)]
        r_LNp = Res("LNp")
        r_const = Res("const")
        r_rna = [Res("rna0")] * 2
        r_SCR = Res("SCR")
        r_Oc, r_rD = Res("Oc"), Res("rD")
        r_xsb = [Res("xsA"), r_SCR]
        r_PT = [Res("PT0"), Res("PT1")]
        r_small = Res("small")
        r_xhs, r_h0h, r_zh = Res("xhs"), Res("h0h"), Res("zh")
        r_bn = Res("bnst")
        r_KTA = [Res("KTA%d" % h) for h in range(8)]
        r_VA = [Res("VA%d" % t) for t in range(10)]
        r_QTA = [Res("QTA%d" % h) for h in range(8)]
        r_KTB = [Res("KTB%d" % h) for h in range(2)]
        r_VB = [Res("VB%d" % t) for t in range(10)]
        r_QTB = [Res("QTB%d" % h) for h in range(8)]
        r_oa = [Res("oa%d" % h) for h in range(8)]
        r_ob = [Res("ob%d" % h) for h in range(8)]
        r_mg = [Res("mg%d" % c) for c in range(16)]
        r_uT = [Res("uT%d" % c) for c in range(NCH)]
        r_y = Res("y")
        r_yj = [Res("y%d" % j) for j in range(4)]

        wstate = {"n": 0}

        def wload(src2, kk, cols):
            i = wstate["n"] % 3
            wstate["n"] += 1
            flat = wbuf[i][:, 0:kk * cols]
            view = flat.rearrange("p (k c) -> p k c", k=kk)
            S.dma("pool", "wb%d" % i, lambda e, flat=flat, src2=src2: e.dma_start(out=flat, in_=src2),
                  writes=[r_wb[i]])
            return view, r_wb[i]

        def wcols(w, c0, ncols, k0=0, nk=None):
            v = w.rearrange("(k p) c -> p k c", p=128)
            if nk is None:
                nk = v.shape[1] - k0
            return v[:, k0:k0 + nk, c0:c0 + ncols]

        def setup():
            S.dma("sp", "c0", lambda e: [e.dma_start(out=lnT[:], in_=lnT_d[:, :]),
                                         e.dma_start(out=convw[:], in_=convw_d[:, :]),
                                         e.dma_start(out=esink[:], in_=sink_d[:, :]),
                                         e.dma_start(out=fsw[:], in_=fsw_d[:, :]),
                                         e.dma_start(out=flag[:], in_=flag_d[:, :])],
                  writes=[r_const], n_dma=5)
            S.dma("pool", "c1", lambda e: [e.dma_start(out=BWtab[:], in_=swb_d[:, :]),
                                           e.dma_start(out=lrow[:], in_=lrow_d[:, :])],
                  writes=[r_const], n_dma=2)
            stg_g = h0[:, :, :].rearrange("p a f -> p (a f)")[:, 0:7168]
            stg_m = R1[:, 0:14336].bitcast(F32)
            S.dma("sp", "c2", lambda e: [e.dma_start(out=stg_g, in_=nabg_d[:, :]),
                                         e.dma_start(out=stg_m, in_=nabm_d[:, :])],
                  writes=[r_h0[0], r_R1], n_dma=2)
            S.op("dve", lambda e: e.tensor_tensor(out=Btab[:], in0=stg_g, in1=stg_m, op=ALU.add),
                 reads=[r_h0[0], r_R1], writes=[r_const])

            def mk_ones(e):
                e.memset(ones_f[:], 1.0)
                return e.memset(SCR[:, 0:128], 1.0)
            S.op("pool", mk_ones, writes=[r_SCR, r_const])
            S.op("pool", lambda e: e.affine_select(out=SCR[:, 0:128], in_=SCR[:, 0:128], pattern=[[-1, 128]],
                                                   compare_op=ALU.is_equal, fill=0.0, base=0, channel_multiplier=1),
                 reads=[r_SCR], writes=[r_SCR])

            def mk_consts(e):
                e.tensor_copy(out=ident[:], in_=SCR[:, 0:128])
                e.tensor_copy(out=ones_bf[:], in_=ones_f[:])
                e.memset(small[:, 0:1], EPS)
                return e.memset(small[:, 1:2], 0.0)
            S.op("dve", mk_consts, reads=[r_SCR], writes=[r_const, r_small])
            S.op("act", lambda e: e.activation(out=esink[:], in_=esink[:], func=AF.Exp),
                 reads=[r_const], writes=[r_const])

        eps_ap = small[:, 0:1]

        lnstate = {"n": 0}

        def ln_tok(src, np_, gi, out_f32=None, out_bf=None):
            ss = lnstate["n"] % 2
            lnstate["n"] += 1
            sc = 8 + 16 * ss
            bo = 24 * ss
            rsm, rbn = r_smb[ss], r_bnb[ss]
            extra = gi.get("extra", [])

            def stats(e):
                last = None
                for c in range(4):
                    last = e.bn_stats(out=bnst[0:np_, bo + 6 * c:bo + 6 * c + 6], in_=src[:, c * 512:(c + 1) * 512])
                return last
            S.op("dve", stats, reads=gi["r_src"], writes=[rbn])
            tick()
            S.op("dve", lambda e: e.bn_aggr(out=small[0:np_, sc:sc + 2], in_=bnst[0:np_, bo:bo + 24]), reads=[rbn], writes=[rsm])
            tick()
            S.op("act", lambda e: e.activation(out=small[0:np_, sc + 2:sc + 3], in_=small[0:np_, sc + 1:sc + 2], func=AF.Ln,
                                               bias=eps_ap[0:np_, :], scale=1.0), reads=[rsm, r_const], writes=[rsm])
            tick()
            S.op("act", lambda e: e.activation(out=small[0:np_, sc + 2:sc + 3], in_=small[0:np_, sc + 2:sc + 3], func=AF.Exp, scale=-0.5),
                 reads=[rsm], writes=[rsm])
            tick()
            if gi.get("feat"):
                S.op("dve", lambda e: e.scalar_tensor_tensor(out=small[0:np_, sc + 3:sc + 4], in0=small[0:np_, sc:sc + 1], scalar=-1.0,
                                                             in1=small[0:np_, sc + 2:sc + 3], op0=ALU.mult, op1=ALU.mult),
                     reads=[rsm], writes=[rsm])
                tick()
                S.op("act", lambda e: e.activation(out=out_bf, in_=src, func=AF.Identity, bias=small[0:np_, sc + 3:sc + 4],
                                                   scale=small[0:np_, sc + 2:sc + 3]),
                     reads=[rsm] + gi["r_src"] + extra, writes=gi["r_bf"])
                tick()
                return
            S.op("dve", lambda e: e.scalar_tensor_tensor(out=src, in0=src, scalar=small[0:np_, sc:sc + 1], in1=LNp[0:np_, 0, :],
                                                         op0=ALU.subtract, op1=ALU.mult),
                 reads=[rsm, r_LNp] + gi["r_src"], writes=gi["r_src"])
            tick()
            if out_f32 is not None:
                S.op("dve", lambda e: e.scalar_tensor_tensor(out=out_f32, in0=src, scalar=small[0:np_, sc + 2:sc + 3], in1=LNp[0:np_, 1, :],
                                                             op0=ALU.mult, op1=ALU.add),
                     reads=[rsm, r_LNp] + gi["r_src"], writes=gi["r_dst"])
                tick()
                if out_bf is not None:
                    S.op("act", lambda e: e.copy(out=out_bf, in_=out_f32), reads=gi["r_dst"] + extra, writes=gi["r_bf"])
                    tick()
            else:
                S.op("dve", lambda e: e.scalar_tensor_tensor(out=out_bf, in0=src, scalar=small[0:np_, sc + 2:sc + 3], in1=LNp[0:np_, 1, :],
                                                             op0=ALU.mult, op1=ALU.add),
                     reads=[rsm, r_LNp] + gi["r_src"] + extra, writes=gi["r_bf"])
                tick()

        def load_lnp(i):
            S.dma("sp", "lnp", lambda e: [e.dma_start(out=LNp[:, 0:1, :], in_=lnv[2 * i:2 * i + 1, :].partition_broadcast(128)),
                                          e.dma_start(out=LNp[:, 1:2, :], in_=lnv[2 * i + 1:2 * i + 2, :].partition_broadcast(128))],
                  writes=[r_LNp], n_dma=2)

        tstate = {"n": 0}

        def transpose_to(np_, dst_fn, reads, writes, hnb=None, r_hnx=None, affine=None):
            for half in range(2):
                bi = 6 + (tstate["n"] % 2)
                tstate["n"] += 1
                pT = banks[bi][:, :].bitcast(BF16)

                def tr(e, half=half, pT=pT):
                    last = None
                    for j in range(8):
                        k = half * 8 + j
                        last = e.transpose(out=pT[:, j * 128:j * 128 + np_], in_=(hn if hnb is None else hnb)[0:np_, k * 128:(k + 1) * 128],
                                           identity=ident[0:np_, 0:np_])
                    return last
                S.op("pe", tr, reads=[r_hn if r_hnx is None else r_hnx, r_const], writes=[rb[bi]])
                if affine is None:
                    src = pT[:, :].rearrange("p (a b) -> p a b", a=8)[:, :, 0:np_]
                    S.op("act", lambda e, half=half, src=src: e.copy(out=dst_fn(half * 8, 8), in_=src),
                         reads=[rb[bi]] + reads, writes=writes)
                else:
                    g0, b0 = affine

                    def ev_act(e, half=half, pT=pT):
                        last = None
                        for j in range(8):
                            k = half * 8 + j
                            last = e.activation(out=dst_fn(k, 1)[:, 0, :], in_=pT[:, j * 128:j * 128 + np_], func=AF.Identity,
                                                bias=lnT[:, b0 + k:b0 + k + 1], scale=lnT[:, g0 + k:g0 + k + 1])
                        return last

                    def ev_dve(e, half=half, pT=pT):
                        last = None
                        for j in range(8):
                            k = half * 8 + j
                            last = e.tensor_scalar(out=dst_fn(k, 1)[:, 0, :], in0=pT[:, j * 128:j * 128 + np_],
                                                   scalar1=lnT[:, g0 + k:g0 + k + 1], scalar2=lnT[:, b0 + k:b0 + k + 1],
                                                   op0=ALU.mult, op1=ALU.add)
                        return last
                    if half == 0:
                        S.op("act", ev_act, reads=[rb[bi], r_const] + reads, writes=writes)
                    else:
                        S.op("dve", ev_dve, reads=[rb[bi], r_const] + reads, cowrites=writes)

        def ln_feat_gen(x3, gcol, bcol, out3, r_in, r_out, hb_bank):
            S.op("dve", lambda e: e.tensor_tensor(out=zh2[:], in0=x3, in1=x3, op=ALU.mult), reads=r_in, writes=[r_zh])
            yield
            ps = banks[hb_bank]

            def mm(e):
                last = None
                for k in range(16):
                    last = e.matmul(ps[:, 0:2], lhsT=ones_f[:], rhs=x3[:, k, :], start=(k == 0), stop=False)
                for k in range(16):
                    last = e.matmul(ps[:, 2:4], lhsT=ones_f[:], rhs=zh2[:, k, :], start=False, stop=(k == 15))
                return last
            S.op("pe", mm, reads=r_in + [r_zh, r_const], writes=[rb[hb_bank]])
            yield
            S.op("dve", lambda e: e.tensor_scalar(out=small[:, 16:20], in0=ps[:, 0:4], scalar1=1.0 / D, scalar2=None, op0=ALU.mult),
                 reads=[rb[hb_bank]], writes=[r_small])
            yield
            S.op("dve", lambda e: e.tensor_tensor(out=small[:, 20:22], in0=small[:, 16:18], in1=small[:, 16:18], op=ALU.mult),
                 reads=[r_small], writes=[r_small])
            yield
            S.op("dve", lambda e: e.tensor_tensor(out=small[:, 18:20], in0=small[:, 18:20], in1=small[:, 20:22], op=ALU.subtract),
                 reads=[r_small], writes=[r_small])
            yield
            S.op("act", lambda e: e.activation(out=small[:, 18:20], in_=small[:, 18:20], func=AF.Ln, bias=eps_ap, scale=1.0),
                 reads=[r_small, r_const], writes=[r_small])
            yield
            S.op("act", lambda e: e.activation(out=small[:, 18:20], in_=small[:, 18:20], func=AF.Exp, scale=-0.5),
                 reads=[r_small], writes=[r_small])
            yield

            def n1(e):
                e.tensor_scalar(out=zh2[:, :, 0], in0=x3[:, :, 0], scalar1=small[:, 16:17], scalar2=small[:, 18:19], op0=ALU.subtract, op1=ALU.mult)
                return e.tensor_scalar(out=zh2[:, :, 1], in0=x3[:, :, 1], scalar1=small[:, 17:18], scalar2=small[:, 19:20], op0=ALU.subtract, op1=ALU.mult)
            S.op("dve", n1, reads=r_in + [r_small, r_zh], writes=[r_zh])
            yield

            def n2(e):
                e.tensor_tensor(out=zh2[:, :, 0], in0=zh2[:, :, 0], in1=gcol, op=ALU.mult)
                return e.tensor_tensor(out=zh2[:, :, 1], in0=zh2[:, :, 1], in1=gcol, op=ALU.mult)
            S.op("dve", n2, reads=[r_zh, r_const], writes=[r_zh])
            yield

            def n3(e):
                e.tensor_tensor(out=out3[:, :, 0], in0=zh2[:, :, 0], in1=bcol, op=ALU.add)
                return e.tensor_tensor(out=out3[:, :, 1], in0=zh2[:, :, 1], in1=bcol, op=ALU.add)
            S.op("dve", n3, reads=[r_zh, r_const], writes=r_out + [r_zh])
            yield

        def ln_feat(*args):
            for _ in ln_feat_gen(*args):
                pass

        tickstate = {"g": None}

        def tick(n=1):
            for _ in range(n):
                g = tickstate["g"]
                if g is None:
                    return
                try:
                    next(g)
                except StopIteration:
                    tickstate["g"] = None

        def drain():
            while tickstate["g"] is not None:
                tick()

        fm_state = {"main": 0, "tail": 0}

        def fm_proj(lhs_fn, nk, rhs_fn, ncols, reads, consume, main_banks=None, tail_banks=None):
            if ncols == NQ:
                pieces = [(0, 257), (257, NQ)]
            else:
                pieces = []
                c = 0
                while c < ncols:
                    n = min(512, ncols - c)
                    pieces.append((c, c + n))
                    c += n
            for (c0, c1) in pieces:
                n = c1 - c0
                bi = fm_state["main"] % 6
                fm_state["main"] += 1
                ps = banks[bi][:, 0:n]

                def mm(e, ps=ps, c0=c0, c1=c1):
                    last = None
                    for k in range(nk):
                        last = e.matmul(ps, lhsT=lhs_fn(k), rhs=rhs_fn(k, c0, c1), start=(k == 0), stop=(k == nk - 1))
                    return last
                S.op("pe", mm, reads=reads, writes=[rb[bi]])
                consume(ps, c0, c1, rb[bi])

        def attn_phase(heads):
            Om, Dm = banks[0], banks[1]
            Ot = banks[2][:, 0:2]
            Dt = banks[7][:, 0:2]
            Oc = SCR[:, 0:NQ]
            steps = []
            for H in heads:
                items = [("meta", 0, NQ, None, None, None)] + list(H["key_tiles"])
                for ii, itm in enumerate(items):
                    steps.append((H, itm, ii == 0, ii == len(items) - 1))

            def segs_of(a, b):
                sg = []
                if a < 512:
                    sg.append((a, min(b, 512), False))
                if b > 512:
                    sg.append((max(a, 512), b, True))
                return sg

            def emit_S(si):
                H, (kt, a, b, tabf, mask, bias_ap), _, _ = steps[si]
                segs = segs_of(a, b)
                Sm, St = banks[3 + si % 2], banks[5 + si % 2]
                res_w = [rb[3 + si % 2], rb[5 + si % 2]]
                qT, kt_fn, meta_k = H["qT"], H["kt_fn"], H["meta_k"]

                def smm(e):
                    last = None
                    for (c0, c1, tail) in segs:
                        npo = NMETA if kt == "meta" else 128
                        dst = St[0:npo, c0 - 512:c1 - 512] if tail else Sm[0:npo, c0:c1]
                        if kt == "meta":
                            last = e.matmul(dst, lhsT=meta_k, rhs=qT[:, c0:c1], start=True, stop=True)
                        else:
                            nm = 1 + (tabf is not None) + (mask is not None)
                            j = 0
                            last = e.matmul(dst, lhsT=kt_fn(kt), rhs=qT[:, c0:c1], start=True, stop=(j == nm - 1))
                            if tabf is not None:
                                j += 1
                                last = e.matmul(dst, lhsT=ident[:], rhs=tabf(c0, c1), start=False, stop=(j == nm - 1))
                            if mask is not None:
                                j += 1
                                last = e.matmul(dst, lhsT=mask[0], rhs=mask[1](c0, c1), start=False, stop=(j == nm - 1))
                    return last
                S.op("pe", smm, reads=H["r_reads"], writes=res_w)

            def emit_EXP(si):
                H, (kt, a, b, tabf, mask, bias_ap), _, _ = steps[si]
                segs = segs_of(a, b)
                Sm, St = banks[3 + si % 2], banks[5 + si % 2]
                nkp = NMETA if kt == "meta" else 128
                pt_i = si % 2

                def ex(e):
                    last = None
                    for (c0, c1, tail) in segs:
                        src = St[0:nkp, c0 - 512:c1 - 512] if tail else Sm[0:nkp, c0:c1]
                        if bias_ap is None:
                            last = e.activation(out=PT[pt_i][0:nkp, c0:c1], in_=src, func=AF.Exp)
                        else:
                            last = e.activation(out=PT[pt_i][0:nkp, c0:c1], in_=src, func=AF.Exp, bias=bias_ap[0:nkp, :], scale=1.0)
                    return last
                S.op("act", ex, reads=[rb[3 + si % 2], rb[5 + si % 2], r_const], writes=[r_PT[pt_i]])

            def emit_PV(si):
                H, (kt, a, b, tabf, mask, bias_ap), first, last_item = steps[si]
                segs = segs_of(a, b)
                nkp = NMETA if kt == "meta" else 128
                pt_i = si % 2
                vv = H["meta_v"] if kt == "meta" else H["v_fn"](kt)

                if NFILL:
                    def filler(e):
                        last = None
                        for _ in range(NFILL):
                            last = e.matmul(banks[7][:, :], lhsT=ident[:], rhs=Btab[:, 0:512], start=True, stop=True, skip_group_check=True)
                        return last
                    S.op("pe", filler, reads=[r_const], writes=[rb[7]])

                def pv(e):
                    last = None
                    for (c0, c1, tail) in segs:
                        od = Ot[:, c0 - 512:c1 - 512] if tail else Om[:, c0:c1]
                        dd = Dt[:, c0 - 512:c1 - 512] if tail else Dm[:, c0:c1]
                        e.matmul(od, lhsT=vv, rhs=PT[pt_i][0:nkp, c0:c1], start=first, stop=False, skip_group_check=True)
                        last = e.matmul(dd, lhsT=ones_bf[0:nkp, :], rhs=PT[pt_i][0:nkp, c0:c1], start=first,
                                        stop=False, skip_group_check=True)
                    return last
                S.op("pe", pv, reads=[r_PT[pt_i], r_const] + H["r_reads"], writes=[rb[0], rb[1], rb[2], rb[7]])
                if last_item:
                    sink_ap, out_ap = H["sink_ap"], H["out_ap"]

                    def oc(e):
                        e.copy(out=Oc[:, 0:512], in_=Om[:, :])
                        return e.copy(out=Oc[:, 512:514], in_=Ot)
                    S.op("act", oc, reads=[rb[0], rb[2], r_SCR], writes=[r_Oc])
                    if sink_ap is not None:
                        def f1(e):
                            e.tensor_scalar(out=rD[:, 0:512], in0=Dm[:, :], scalar1=sink_ap, scalar2=None, op0=ALU.add)
                            return e.tensor_scalar(out=rD[:, 512:514], in0=Dt, scalar1=sink_ap, scalar2=None, op0=ALU.add)
                        S.op("dve", f1, reads=[rb[1], rb[7], r_const, r_SCR], writes=[r_rD])
                        S.op("dve", lambda e: e.reciprocal(out=rD[:, :], in_=rD[:, :]), reads=[r_rD], writes=[r_rD])
                    else:
                        def f1(e):
                            e.reciprocal(out=rD[:, 0:512], in_=Dm[:, :])
                            return e.reciprocal(out=rD[:, 512:514], in_=Dt)
                        S.op("dve", f1, reads=[rb[1], rb[7], r_const, r_SCR], writes=[r_rD])
                    S.op("dve", lambda e: e.tensor_tensor(out=out_ap[:, :], in0=Oc[:, :], in1=rD[:, :], op=ALU.mult),
                         reads=[r_Oc, r_rD, r_SCR], writes=H["r_out"])

            s_done = set()

            def ensure_S(k):
                if k < len(steps) and k not in s_done:
                    s_done.add(k)
                    emit_S(k)

            ensure_S(0)
            for si in range(len(steps)):
                ensure_S(si + 1)
                emit_EXP(si)
                if steps[si][2]:
                    ensure_S(si + 2)
                emit_PV(si)

        def guard(res, eng="dve"):
            if eng == "act":
                S.op("act", lambda e: e.copy(out=small[:, 3:4], in_=small[:, 1:2]), reads=[r_const], writes=[res])
            else:
                S.op("dve", lambda e: e.memset(small[:, 2:3], 0.0), writes=[res])

        s1state = {"h": 0, "x": 0}

        s1buf = {}

        def s1_ln(i, w, only_a=False):
            row0 = TT * i
            np_ = 128 if w < 9 else NMETA
            src_rows = xe[row0 + 128 * w: row0 + 128 * w + 128, :] if w < 9 else meta[:, :]
            hi = s1state["h"] % 2
            s1state["h"] += 1
            hnb, r_hnx = hnbufs[hi], r_hnb[hi]
            xi = 0 if only_a else (s1state["x"] % 2)
            s1state["x"] += 1
            xsb, r_x = xsbufs[xi], r_xsb[xi]
            S.dma("sp", "xs%d" % xi, lambda e, np_=np_, src_rows=src_rows, xsb=xsb: e.dma_start(out=xsb[0:np_, :], in_=src_rows),
                  reads=[r_R2], writes=[r_x])
            gi = {"r_src": [r_x], "r_dst": None, "r_bf": [r_hnx], "extra": [r_R2], "feat": True}
            ln_tok(xsb[0:np_, :], np_, gi, out_bf=hnb[0:np_, :])
            s1buf[(i, w)] = (hnb, r_hnx, np_)

        def s1_tr(i, w):
            hnb, r_hnx, np_ = s1buf.pop((i, w))
            c0 = 128 * w
            dstf = (lambda k0, nk, c0=c0, np_=np_: h0T[:, k0:k0 + nk, c0:c0 + np_])
            wr = {0: [r_hTl], 1: [r_hTl], 2: [r_hTl, r_hTq], 3: [r_hTq], 4: [r_hTq], 5: [r_hTq], 6: [r_hTq],
                  7: [r_hTq, r_hTh], 8: [r_hTh], 9: [r_hTh]}[w]
            transpose_to(np_, dstf, [r_R2], wr, hnb=hnb, r_hnx=r_hnx, affine=(0, 16))

        def s1_subtile(i, w):
            s1_ln(i, w)
            s1_tr(i, w)

        def s1_resid(i, j):
            row0 = TT * i + 128 * (3 + j)
            S.dma("sp", "xh0_%d" % j, lambda e: e.dma_start(out=h0[:, j, :], in_=xe[row0:row0 + 128, :]), writes=[r_h0[j]])
            gi = {"r_src": [r_h0[j]], "r_dst": [r_h0[j]], "r_bf": None}
            ln_tok(h0[:, j, :], 128, gi, out_f32=h0[:, j, :])

        mov_groups = [(0, 512), (512, 1024), (1024, NWT)]

        def stage_KA(hp):
            wv, rw = wload(w_in[4 + hp, :, :], 16, 256)
            for hh in range(2):
                h = 2 * hp + hh
                for (m0, m1) in mov_groups:
                    def cons(ps, c0, c1, rbk, h=h, m0=m0):
                        S.op("act", lambda e: e.copy(out=KT_A[:, h, m0 + c0:m0 + c1], in_=ps), reads=[rbk, r_R1], writes=[r_KTA[h]])
                    fm_proj(lambda k, wv=wv, hh=hh: wv[:, k, hh * 128:(hh + 1) * 128], 16,
                            lambda k, c0, c1, m0=m0: h0T[:, k, m0 + c0:m0 + c1], m1 - m0, r_hTall + [rw], cons)


        def stage_ln2(i, j):
            gi = {"r_src": [r_h0[j]], "r_dst": [r_h0[j]], "r_bf": None}
            ln_tok(h0[:, j, :], 128, gi, out_f32=h0[:, j, :])
            r0 = TT * i + 128 * j
            S.dma("sp", "y%d" % j, lambda e, j=j, r0=r0: e.dma_start(out=y[r0:r0 + 128, :], in_=h0[:, j, :]), reads=[r_h0[j]], writes=[r_yj[j]])

        def tile(i):
            row0 = TT * i
            if i == 0:
                guard(r_R2)
                for w in range(10):
                    s1_subtile(0, w)
            load_lnp(0)
            S.dma("sp", "xh", lambda e: e.dma_start(out=xhs[:].rearrange("p k t -> p (k t)"), in_=xh[i, :, :]), writes=[r_xhs])
            S.dma("pool", "rna", lambda e: e.dma_start(out=rna[0][:], in_=rna_d[i, :, :]), writes=[r_rna[0]])

            if dbg and i == 0:
                S.dma('sp', 'dbg', lambda e: [e.dma_start(out=DBG['h0'][128 * j:128 * j + 128, :], in_=h0[:, j, :]) for j in range(4)], reads=r_h0, n_dma=4)
                S.dma('sp', 'dbg', lambda e: e.dma_start(out=DBG['h0T'][:, :], in_=h0T[:, :, :].rearrange('p a b -> p (a b)')), reads=r_hTall)
            if i == 0:
                guard(r_R1)
            if i == 0:
                for hp in range(4):
                    stage_KA(hp)
            for pc in range(4):
                s1_resid(i, pc)
                wv, rw = wload(w_in[8 + pc, :, :], 16, 256)
                for t in range(10):
                    npk = 128 if t < 9 else NMETA
                    bi = fm_state["main"] % 6
                    fm_state["main"] += 1
                    ps = banks[bi][0:npk, 0:256]

                    def mm(e, ps=ps, t=t, npk=npk, wv=wv):
                        last = None
                        for k in range(16):
                            last = e.matmul(ps, lhsT=h0T[:, k, 128 * t:128 * t + npk], rhs=wv[:, k, :], start=(k == 0), stop=(k == 15))
                        return last
                    S.op("pe", mm, reads=r_hTall + [rw], writes=[rb[bi]])
                    S.op("act", lambda e, ps=ps, t=t, npk=npk, pc=pc: e.copy(out=V_A[0:npk, t, 256 * pc:256 * pc + 256], in_=ps),
                         reads=[rb[bi], r_R1], writes=[r_VA[t]])
            tickstate["g"] = ln_feat_gen(xhs[:], lnT[:, 0:16], lnT[:, 16:32], h0h[:], [r_xhs], [r_h0h], 7)
            for hp in range(4):
                wv, rw = wload(w_in[hp, :, :], 16, 256)
                for hh in range(2):
                    h = 2 * hp + hh
                    tick(2)

                    def cons(ps, c0, c1, rbk, h=h):
                        S.op("act", lambda e: e.mul(out=QT_A[:, h, c0:c1], in_=ps, mul=SCALE), reads=[rbk, r_R1], writes=[r_QTA[h]])
                    fm_proj(lambda k, wv=wv, hh=hh: wv[:, k, hh * 128:(hh + 1) * 128], 16,
                            lambda k, c0, c1: h0T[:, k, Q0 + c0:Q0 + c1], NQ, [r_hTq, rw], cons)

            if dbg and i == 0:
                S.dma('sp', 'dbg', lambda e: [e.dma_start(out=DBG['qa'][:, :], in_=R1[:, 19584:23696]), e.dma_start(out=DBG['ka'][:, :], in_=R1[:, 0:9344]), e.dma_start(out=DBG['va'][:, :], in_=R1[:, 9344:19584])], reads=r_QTA + r_KTA + r_VA, n_dma=3)
            drain()
            guard(r_R2, "act")
            dlo, dhi = (0, 5) if i == 0 else ((-1, 4) if i == NTILES - 1 else (0, 4))
            rn = rna[i % 2]
            heads = []
            for h in range(8):
                kts = []
                for kt in range(9):
                    plo, phi = kt - 1 - dhi, kt - 1 - dlo
                    plo, phi = max(plo, -1), min(phi, 4)
                    if plo > phi:
                        continue
                    a = 0 if plo == -1 else 1 + 128 * plo
                    b = NQ if phi == 4 else 1 + 128 * (phi + 1)
                    t0 = h * 896 + (6 - kt) * 128 - 1
                    kts.append((kt, a, b,
                                (lambda c0, c1, t0=t0: Btab[:, t0 + c0:t0 + c1]),
                                (lrow[:, 128 * kt:128 * kt + 128], (lambda c0, c1, rn=rn: rn[:, c0:c1])),
                                None))
                heads.append(dict(kt_fn=(lambda kt, h=h: KT_A[:, h, 128 * kt:128 * kt + 128]),
                                  v_fn=(lambda kt, h=h: V_A[:, kt, 128 * h:128 * h + 128]),
                                  qT=QT_A[:, h, :], key_tiles=kts, meta_k=KT_A[:, h, NW:NWT],
                                  meta_v=V_A[0:NMETA, 9, 128 * h:128 * h + 128], out_ap=oaT[:, h, :],
                                  r_reads=[r_KTA[h], r_QTA[h], r_const, r_rna[i % 2], r_R1] + r_VA,
                                  r_out=[r_oa[h], r_R2], sink_ap=None))
            attn_phase(heads)

            guard(r_R1)
            wv, rw = wload(w_in[16, :, :], 16, 256)
            for g in range(2):
                for (m0, m1) in mov_groups:
                    def cons(ps, c0, c1, rbk, g=g, m0=m0):
                        S.op("act", lambda e: e.copy(out=KT_B[:, g, m0 + c0:m0 + c1], in_=ps), reads=[rbk, r_R1], writes=[r_KTB[g]])
                    fm_proj(lambda k, wv=wv, g=g: wv[:, k, g * 128:(g + 1) * 128], 16,
                            lambda k, c0, c1, m0=m0: h0T[:, k, m0 + c0:m0 + c1], m1 - m0, r_hTall + [rw], cons)
            wv, rw = wload(w_in[17, :, :], 16, 256)
            for t in range(10):
                npk = 128 if t < 9 else NMETA
                bi = fm_state["main"] % 6
                fm_state["main"] += 1
                ps = banks[bi][0:npk, 0:256]

                def mm(e, ps=ps, t=t, npk=npk, wv=wv):
                    last = None
                    for k in range(16):
                        last = e.matmul(ps, lhsT=h0T[:, k, 128 * t:128 * t + npk], rhs=wv[:, k, :], start=(k == 0), stop=(k == 15))
                    return last
                S.op("pe", mm, reads=r_hTall + [rw], writes=[rb[bi]])
                S.op("act", lambda e, ps=ps, t=t, npk=npk: e.copy(out=V_B[0:npk, t, :], in_=ps),
                     reads=[rb[bi], r_R1], writes=[r_VB[t]])
            for hp in range(4):
                wv, rw = wload(w_in[12 + hp, :, :], 16, 256)
                for hh in range(2):
                    h = 2 * hp + hh

                    def cons(ps, c0, c1, rbk, h=h):
                        S.op("act", lambda e: e.mul(out=QT_B[:, h, c0:c1], in_=ps, mul=SCALE), reads=[rbk, r_R1], writes=[r_QTB[h]])
                    fm_proj(lambda k, wv=wv, hh=hh: wv[:, k, hh * 128:(hh + 1) * 128], 16,
                            lambda k, c0, c1: h0T[:, k, Q0 + c0:Q0 + c1], NQ, [r_hTq, rw], cons)

            heads = []
            for hq in range(8):
                g = hq // 4
                kts = []
                for kb in range(1, 9):
                    a = max(0, 1 + 128 * (kb - 4))
                    b = min(NQ, 1 + 128 * (kb - 1))
                    if a >= b:
                        continue
                    t0 = hq * 384 + (4 - kb) * 128 - 1
                    kts.append((kb, a, b, (lambda c0, c1, t0=t0: BWtab[:, t0 + c0:t0 + c1]), None,
                                fsw[:, i * 9 + kb:i * 9 + kb + 1]))
                heads.append(dict(kt_fn=(lambda kb, g=g: KT_B[:, g, 128 * kb:128 * kb + 128]),
                                  v_fn=(lambda kb, g=g: V_B[:, kb, 128 * g:128 * g + 128]),
                                  qT=QT_B[:, hq, :], key_tiles=kts, meta_k=KT_B[:, g, NW:NWT],
                                  meta_v=V_B[0:NMETA, 9, 128 * g:128 * g + 128], out_ap=obT[:, hq, :],
                                  r_reads=[r_KTB[g], r_QTB[hq], r_const, r_R1] + r_VB,
                                  r_out=[r_ob[hq], r_R2], sink_ap=esink[:, hq:hq + 1]))
            attn_phase(heads)

            if dbg and i == 0:
                S.dma('sp', 'dbg', lambda e: [e.dma_start(out=DBG['oa'][:, :], in_=R2[:, 0:4112]), e.dma_start(out=DBG['ob'][:, :], in_=R2[:, 4112:8224])], reads=r_oa + r_ob, n_dma=2)
            guard(r_R1)
            SCRb = SCR[:, :].bitcast(BF16)
            sgt = {("ga", 0): SCRb[:, 0:514], ("ga", 1): SCRb[:, 514:1028], ("gb", 0): SCRb[:, 1028:1542], ("gb", 1): SCRb[:, 1542:2056]}
            r_sg = {k: Res("sg%s%d" % k) for k in sgt}
            t1 = SCR[:, 1032:1546]
            r_t1 = Res("t1")
            guard(r_SCR)
            for cp in range(8):
                for nm, wc0 in (("ga", 4608), ("gb", 6656)):
                    wv, rw = wload(w_in[wc0 // 256 + cp, :, :], 16, 256)
                    for cc in range(2):
                        dst, rd = sgt[(nm, cc)], r_sg[(nm, cc)]

                        def cons(ps, c0, c1, rbk, dst=dst, rd=rd):
                            S.op("act", lambda e: e.activation(out=dst[:, c0:c1], in_=ps, func=AF.Sigmoid), reads=[rbk, r_SCR], writes=[rd])
                        fm_proj(lambda k, wv=wv, cc=cc: wv[:, k, 128 * cc:128 * cc + 128], 16,
                                lambda k, c0, c1: h0T[:, k, Q0 + c0:Q0 + c1], NQ, [r_hTq, rw], cons, main_banks=(0, 1))
                wva, rwa = wload(w_pa[cp, :, :], 8, 256)
                wvb, rwb = wload(w_pb[cp, :, :], 8, 256)
                for cc in range(2):
                    c = 2 * cp + cc
                    sa, ra = sgt[("ga", cc)], r_sg[("ga", cc)]
                    sb_, rb_ = sgt[("gb", cc)], r_sg[("gb", cc)]

                    def consa(ps, c0, c1, rbk, sa=sa, ra=ra):
                        S.op("dve", lambda e: e.tensor_tensor(out=t1[:, c0:c1], in0=ps, in1=sa[:, c0:c1], op=ALU.mult),
                             reads=[rbk, ra, r_SCR], writes=[r_t1])
                    fm_proj(lambda k, wva=wva, cc=cc: wva[:, k, 128 * cc:128 * cc + 128], 8,
                            lambda k, c0, c1: oaT[:, k, c0:c1], NQ, r_oa + [rwa, r_R2], consa, main_banks=(2,), tail_banks=(4,))

                    def consb(ps, c0, c1, rbk, c=c, sb_=sb_, rb_=rb_):
                        S.op("dve", lambda e: e.tensor_tensor(out=ps, in0=ps, in1=sb_[:, c0:c1], op=ALU.mult),
                             reads=[rb_, r_SCR], writes=[rbk])
                        S.op("dve", lambda e: e.tensor_tensor(out=mergedT[:, c, c0:c1], in0=ps, in1=t1[:, c0:c1], op=ALU.add),
                             reads=[rbk, r_t1, r_R1, r_SCR], writes=[r_mg[c]])
                    fm_proj(lambda k, wvb=wvb, cc=cc: wvb[:, k, 128 * cc:128 * cc + 128], 8,
                            lambda k, c0, c1: obT[:, k, c0:c1], NQ, r_ob + [rwb, r_R2], consb, main_banks=(3,), tail_banks=(5,))
            guard(r_SCR)

            if dbg and i == 0:
                S.dma('sp', 'dbg', lambda e: e.dma_start(out=DBG['mg'][:, :], in_=R1[:, 0:8224]), reads=r_mg + [r_R1])
            load_lnp(1)
            guard(r_R2)
            for cb in range(4):
                pcs = []
                for q in range(2):
                    pcs.append(wload(w_out[2 * cb + q, :, :], 8, 512))
                    wv, rw = pcs[q]

                    def mm(e, q=q, wv=wv):
                        last = None
                        for j in range(4):
                            for kk in range(8):
                                k = 8 * q + kk
                                last = e.matmul(banks[j][:, :], lhsT=mergedT[:, k, 1 + 128 * j:129 + 128 * j], rhs=wv[:, kk, :],
                                                start=(k == 0), stop=(k == 15), skip_group_check=True)
                        return last
                    S.op("pe", mm, reads=r_mg + [rw, r_R1], writes=[rb[0], rb[1], rb[2], rb[3]])
                for j in range(4):
                    S.op("dve", lambda e, j=j, cb=cb: e.scalar_tensor_tensor(
                        out=h0[:, j, 512 * cb:512 * cb + 512], in0=h0[:, j, 512 * cb:512 * cb + 512], scalar=ALPHA,
                        in1=banks[j][:, :], op0=ALU.mult, op1=ALU.add), reads=[rb[j]], writes=[r_h0[j]])
                hps = banks[5]

                def hmm(e, cb=cb, pcs=pcs):
                    last = None
                    for cc in range(4):
                        for k in range(16):
                            wv = pcs[k // 8][0]
                            last = e.matmul(hps[:, 2 * cc:2 * cc + 2], lhsT=wv[:, k % 8, 128 * cc:128 * cc + 128],
                                            rhs=mergedT[:, k, 0:NQ:NQ - 1], start=(k == 0 and cc == 0), stop=(k == 15),
                                            skip_group_check=True)
                    return last
                S.op("pe", hmm, reads=r_mg + [pcs[0][1], pcs[1][1], r_R1], writes=[rb[5]])
                S.op("dve", lambda e, cb=cb: e.scalar_tensor_tensor(
                    out=zh[:, 4 * cb:4 * cb + 4, :], in0=h0h[:, 4 * cb:4 * cb + 4, :], scalar=ALPHA,
                    in1=hps[:, 0:8].rearrange("p (c t) -> p c t", c=4), op0=ALU.mult, op1=ALU.add),
                    reads=[rb[5], r_h0h], writes=[r_zh])
            tickstate["g"] = ln_feat_gen(zh[:], lnT[:, 32:48], lnT[:, 48:64], zh[:], [r_zh], [r_zh], 5)
            for j in range(4):
                hi = lnstate["n"] % 2
                hnb, r_hnx = hnbufs[hi], r_hnb[hi]
                gi = {"r_src": [r_h0[j]], "r_dst": [r_h0[j]], "r_bf": [r_hnx], "extra": [r_R2]}
                ln_tok(h0[:, j, :], 128, gi, out_f32=h0[:, j, :], out_bf=hnb[:, :])
                c0 = Q0 + 1 + 128 * j
                transpose_to(128, lambda k0, nk, c0=c0: h0T[:, k0:k0 + nk, c0:c0 + 128], [r_R2], [r_hTq], hnb=hnb, r_hnx=r_hnx)
            drain()

            def hcols(e):
                e.tensor_copy(out=h0T[:, :, Q0], in_=zh[:, :, 0])
                return e.tensor_scalar(out=h0T[:, :, Q0 + NQ - 1], in0=zh[:, :, 1], scalar1=flag[:, i:i + 1], scalar2=None, op0=ALU.mult)
            S.op("dve", hcols, reads=[r_zh, r_const], writes=[r_hTq])

            if dbg and i == 0:
                S.dma('sp', 'dbg', lambda e: [e.dma_start(out=DBG['h1'][128 * j:128 * j + 128, :], in_=h0[:, j, :]) for j in range(4)], reads=r_h0, n_dma=4)
            guard(r_R1)
            Araw2 = [SCR[:, 0:514], SCR[:, 514:1028]]
            cbuf6 = SCR[:, 1028:1540]
            vbuf6 = PT[0][:, 0:512]
            gbuf6 = PT[1][:, 0:512]
            r_Araw = [Res("Araw0"), Res("Araw1")]
            r_cbuf = Res("cbuf")
            guard(r_SCR)
            sched6 = {1: [("ln", 0)], 4: [("tr", 0)], 5: [("ln", 1)], 8: [("tr", 1)], 9: [("ln", 8)], 12: [("tr", 8)],
                      13: [("ln", 9)], 16: [("tr", 9)]}
            if i + 1 < ntiles:
                guard(r_R2)
            gpiece, vpiece = {}, {}

            def emit_gate(c):
                cp, cc = c // 2, c % 2
                if cc == 0:
                    gpiece[cp] = wload(w_fi[cp, :, :], 16, 256)
                wg, rwg = gpiece[cp]
                Ar, rA = Araw2[c % 2], r_Araw[c % 2]

                def consg(ps, c0, c1, rbk):
                    if c0 == 0:
                        S.op("act", lambda e: e.copy(out=Ar[:, c0:c1], in_=ps), reads=[rbk, r_SCR], writes=[rA])
                    else:
                        S.op("act", lambda e: e.copy(out=Ar[:, c0:c1], in_=ps), reads=[rbk, r_SCR], cowrites=[rA])
                fm_proj(lambda k: wg[:, k, cc * 128:cc * 128 + 128], 16,
                        lambda k, c0, c1: h0T[:, k, Q0 + c0:Q0 + c1], NQ, [r_hTq, rwg], consg)

            def emit_val(c):
                cp, cc = c // 2, c % 2
                wvv, rwv = vpiece[cp]

                def consv(ps, c0, c1, rbk):
                    lo, hi = max(c0, 1), min(c1, 513)
                    if c0 == 0:
                        S.op("act", lambda e: e.copy(out=vbuf6[:, lo - 1:hi - 1], in_=ps[:, lo - c0:hi - c0]), reads=[rbk], writes=[r_PT[0]])
                    else:
                        S.op("act", lambda e: e.copy(out=vbuf6[:, lo - 1:hi - 1], in_=ps[:, lo - c0:hi - c0]), reads=[rbk], cowrites=[r_PT[0]])
                fm_proj(lambda k: wvv[:, k, cc * 128:cc * 128 + 128], 16,
                        lambda k, c0, c1: h0T[:, k, Q0 + c0:Q0 + c1], NQ, [r_hTq, rwv], consv)

            emit_gate(0)
            for c in range(NCH):
                cp = c // 2
                if c % 2 == 0:
                    if i + 1 < ntiles:
                        for (act_, w) in sched6.get(cp, []):
                            if act_ == "ln":
                                s1_ln(i + 1, w, only_a=True)
                            else:
                                s1_tr(i + 1, w)
                    vpiece[cp] = wload(w_fi[NCH // 2 + cp, :, :], 16, 256)
                w0 = convw[:, 0 * NCH + c:0 * NCH + c + 1]
                w1 = convw[:, 1 * NCH + c:1 * NCH + c + 1]
                w2 = convw[:, 2 * NCH + c:2 * NCH + c + 1]
                bb = convw[:, 3 * NCH + c:3 * NCH + c + 1]
                Ar, rA = Araw2[c % 2], r_Araw[c % 2]
                S.op("act", lambda e, w1=w1, bb=bb, Ar=Ar: e.activation(out=cbuf6[:, :], in_=Ar[:, 1:513], func=AF.Identity, bias=bb, scale=w1),
                     reads=[rA, r_const, r_SCR], writes=[r_cbuf])
                S.op("dve", lambda e, w0=w0, Ar=Ar: e.scalar_tensor_tensor(out=cbuf6[:, :], in0=Ar[:, 0:512], scalar=w0, in1=cbuf6[:, :], op0=ALU.mult, op1=ALU.add),
                     reads=[rA, r_const, r_SCR], writes=[r_cbuf])
                S.op("dve", lambda e, w2=w2, Ar=Ar: e.scalar_tensor_tensor(out=cbuf6[:, :], in0=Ar[:, 2:514], scalar=w2, in1=cbuf6[:, :], op0=ALU.mult, op1=ALU.add),
                     reads=[rA, r_const, r_SCR], writes=[r_cbuf])
                emit_val(c)
                if c + 1 < NCH:
                    emit_gate(c + 1)
                S.op("act", lambda e: e.activation(out=gbuf6[:, :], in_=cbuf6[:, :], func=AF.Gelu_apprx_tanh), reads=[r_cbuf], writes=[r_PT[1]])
                S.op("dve", lambda e, c=c: e.tensor_tensor(out=uT[:, c, 1:513], in0=gbuf6[:, :], in1=vbuf6[:, :], op=ALU.mult),
                     reads=[r_PT[0], r_PT[1], r_R1], writes=[r_uT[c]])

            if dbg and i == 0:
                S.dma('sp', 'dbg', lambda e: e.dma_start(out=DBG['uT'][:, :], in_=R1[:, 0:22616]), reads=r_uT + [r_R1])
            load_lnp(2)
            sched7 = {0: [("ln", 2)], 1: [("ln", 7)], 3: [("tr", 2)], 4: [("ln", 3)], 6: [("tr", 7)], 7: [("ln", 4)],
                      9: [("tr", 3)], 10: [("ln", 5)], 12: [("tr", 4)], 13: [("ln", 6)], 15: [("tr", 5)], 18: [("tr", 6)]}
            for cb in range(4):
                ksz = [8, 8, 8, 8, 8, 4]
                for q in range(6):
                    if i + 1 < ntiles:
                        for (act_, w) in sched7.get(6 * cb + q, []):
                            if act_ == "ln":
                                s1_ln(i + 1, w)
                            else:
                                s1_tr(i + 1, w)
                    wv, rw = wload(w_fd[6 * cb + q, :, 0:ksz[q] * 512], ksz[q], 512)

                    def mm(e, q=q, wv=wv):
                        last = None
                        for j in range(4):
                            for kk in range(ksz[q]):
                                k = 8 * q + kk
                                last = e.matmul(banks[j][:, :], lhsT=uT[:, k, 1 + 128 * j:129 + 128 * j], rhs=wv[:, kk, :],
                                                start=(k == 0), stop=(k == NCH - 1), skip_group_check=True)
                        return last
                    S.op("pe", mm, reads=r_uT + [rw, r_R1], writes=[rb[0], rb[1], rb[2], rb[3]])
                for j in range(4):
                    S.op("dve", lambda e, j=j, cb=cb: e.scalar_tensor_tensor(
                        out=h0[:, j, 512 * cb:512 * cb + 512], in0=h0[:, j, 512 * cb:512 * cb + 512], scalar=ALPHA,
                        in1=banks[j][:, :], op0=ALU.mult, op1=ALU.add), reads=[rb[j]], writes=[r_h0[j]])
            if i + 1 < ntiles:
                guard(r_R1, "act")
                for hp in range(4):
                    stage_KA(hp)
                    stage_ln2(i, hp)
            else:
                for j in range(4):
                    stage_ln2(i, j)

        setup()
        for i in range(ntiles):
            tile(i)
        S.emit()
    return nc


def _static_tables():
    kp = np.arange(128) // 64
    kc = np.arange(128) % 64
    qp = kp.copy()
    qc = kc.copy()
    col_start = np.clip(qc - 8, 0, 48)
    col_ok = (kc[:, None] >= col_start[None, :]) & (kc[:, None] < col_start[None, :] + 16)
    dc = np.clip(kc[:, None] - qc[None, :], -15, 15) + 15
    dr_idx = np.zeros((7, 128, 128), np.int64)
    nab_m = np.zeros((7, 128, 128), np.float32)
    for b in range(7):
        d = 5 - b
        dr = 2 * (d - 2) + kp[:, None] - qp[None, :]
        ok = (np.abs(dr) <= 7) & col_ok
        dr_idx[b] = np.clip(dr, -7, 7) + 7
        nab_m[b] = np.where(ok, 0.0, NEG)
    slopes = 2.0 ** (-(np.arange(1, 9, dtype=np.float64)))
    swb = np.zeros((8, 3, 128, 128), np.float32)
    k = np.arange(128)
    for b in range(3):
        dd = 1 - b
        dist = np.abs(128 * dd + k[:, None] - k[None, :])
        for h in range(8):
            swb[h, b] = np.where(dist <= 128, -slopes[h] * dist, NEG)
    lrow = (np.arange(NW)[None, :] // 64 == np.arange(18)[:, None]).astype(np.float32)
    return dr_idx, dc, nab_m, swb, lrow


def _core_geom(c):
    if c < 4:
        return ("p", 0, TOK * c, 16384)
    return ("s", c - 4, 0, 4096)


def _core_tables(s, tseq):
    rows_seq = tseq // 64
    rna = np.full((NTILES, 18, NQ), NEG, np.float32)
    fsw = np.zeros((128, NTILES * 9), np.float32)
    flag = np.zeros((128, NTILES), np.float32)
    for i in range(NTILES):
        t0 = s + TT * i
        R0 = t0 // 64
        for qcol in range(NQ):
            if qcol == 0:
                rq = R0 - 1
            elif qcol == NQ - 1:
                rq = R0 + 8
            else:
                rq = R0 + (qcol - 1) // 64
            if rq < 0 or rq >= rows_seq:
                continue
            rs = min(max(rq - 4, 0), rows_seq - 8)
            for rr in range(18):
                rk = R0 + rr - 6
                if rs <= rk < rs + 8:
                    rna[i, rr, qcol] = 0.0
        for kb in range(9):
            blk = t0 // 128 + kb - 3
            if blk < 0 or blk >= tseq // 128:
                fsw[:, i * 9 + kb] = NEG
        flag[:, i] = 1.0 if t0 + TT < tseq else 0.0
    return rna, fsw, flag


def _pieces(w, nk, cols):
    C = w.shape[1]
    return np.ascontiguousarray(w.reshape(nk, 128, C // cols, cols).transpose(2, 1, 0, 3).reshape(C // cols, 128, nk * cols))


def _kpieces(w, ksz):
    out = np.zeros((4 * len(ksz), 128, 8 * 512), np.float32)
    for cb in range(4):
        k0 = 0
        for q, n in enumerate(ksz):
            blk = w[k0 * 128:(k0 + n) * 128, 512 * cb:512 * cb + 512].reshape(n, 128, 512).transpose(1, 0, 2)
            out[cb * len(ksz) + q, :, :n * 512] = blk.reshape(128, n * 512)
            k0 += n
    return out


def _prep(inputs):
    f = lambda a: np.ascontiguousarray(np.asarray(a, dtype=np.float32))
    xp = f(inputs["x_prompt"])[0]
    xsm = f(inputs["x_sample"])
    meta = f(inputs["meta_tokens"])
    dr_idx, dc, nab_m, swb, lrow = _static_tables()
    rpb = f(inputs["na_rpb"])[0]
    nab_g = rpb[:, dr_idx, dc[None, :, :]]
    nab_g = np.ascontiguousarray(nab_g.transpose(2, 0, 1, 3).reshape(128, 7168))
    nab_mm = np.ascontiguousarray(np.broadcast_to(nab_m[None], (8, 7, 128, 128)).transpose(2, 0, 1, 3).reshape(128, 7168))
    swb2 = np.ascontiguousarray(swb.transpose(2, 0, 1, 3).reshape(128, 3072))
    lnv = np.stack([f(inputs["ln_emb_g"]), f(inputs["ln_emb_b"]), f(inputs["ln1_g"])[0], f(inputs["ln1_b"])[0],
                    f(inputs["ln2_g"])[0], f(inputs["ln2_b"])[0]])
    lnT = np.ascontiguousarray(np.concatenate([v.reshape(16, 128).T for v in lnv[:4]], axis=1))
    cw = f(inputs["ffn_conv_w"])[0]
    cbias = f(inputs["ffn_conv_b"])[0]
    convw = np.ascontiguousarray(np.concatenate([cw[j].reshape(NCH, 128).T for j in range(3)] + [cbias.reshape(NCH, 128).T], axis=1))
    sink = np.ascontiguousarray(np.broadcast_to(f(inputs["sw_sink"])[0][None, :], (128, 8)))
    shared = {
        "meta": meta, "lnv": np.ascontiguousarray(lnv), "lnT": lnT, "convw": convw, "sink": sink,
        "nab_g": nab_g, "nab_m": nab_mm, "swb": swb2, "lrow": np.ascontiguousarray(lrow),
        "w_in": _pieces(f(inputs["w_in"])[0], 16, 256), "w_pa": _pieces(f(inputs["w_proj_na"])[0], 8, 256),
        "w_pb": _pieces(f(inputs["w_proj_sw"])[0], 8, 256), "w_out": _kpieces(f(inputs["w_out"])[0], [8, 8]),
        "w_fi": _pieces(f(inputs["w_ffn_in"])[0], 16, 256), "w_fd": _kpieces(f(inputs["w_ffn_down"])[0], [8, 8, 8, 8, 8, 4]),
    }
    in_maps = []
    for c in range(8):
        kind, b, s, tseq = _core_geom(c)
        xsrc = xp if kind == "p" else xsm[b]
        xe = np.zeros((XE_ROWS, D), np.float32)
        lo, hi = s - NB, s - NB + XE_ROWS
        a0, a1 = max(lo, 0), min(hi, tseq)
        xe[a0 - lo:a1 - lo] = xsrc[a0:a1]
        if s == 0:
            xe[NB - 1] = meta[NMETA - 1]
        xh = np.zeros((NTILES, 128, 16, 2), np.float32)
        for i in range(NTILES):
            for t, row in enumerate((NB + TT * i - 1, NB + TT * i + TT)):
                xh[i, :, :, t] = xe[row].reshape(16, 128).T
        rna, fsw, flag = _core_tables(s, tseq)
        m = dict(shared)
        m.update({"xe": xe, "xh": np.ascontiguousarray(xh.reshape(NTILES, 128, 32)), "rna": rna, "fsw": fsw, "flag": flag})
        in_maps.append(m)
    return in_maps


_NC_CACHE = {}


def kernel(**inputs):
    in_maps = _prep(inputs)
    if "nc" not in _NC_CACHE:
        _NC_CACHE["nc"] = build_nc()
    res = run_bass_kernel_spmd(_NC_CACHE["nc"], in_maps, core_ids=list(range(8)))
    ys = [np.asarray(r["y"], dtype=np.float32) for r in res.results]
    y_prompt = np.concatenate(ys[0:4], axis=0)[None]
    y_sample = np.stack(ys[4:8], axis=0)
    return (y_prompt, y_sample)
```

```python
import contextlib
import numpy as np
import concourse.bass as bass
import concourse.mybir as mybir
from concourse.bass_utils import run_bass_kernel_spmd

F32 = mybir.dt.float32
BF16 = mybir.dt.bfloat16
AF = mybir.ActivationFunctionType
ALU = mybir.AluOpType

D = 2048
NMETA = 16
DFF = 5632
NCH = DFF // 128
INC = 8704
TOK = 4096
TT = 512
NTILES = TOK // TT
NB = 384
NW = 1152
NWT = NW + NMETA
NQ = 514
Q0 = NB - 1
XE_ROWS = NB + TOK + 256
ALPHA = 2.0 ** 0.25
EPS = 1e-5
NEG = -30000.0
SCALE = 128.0 ** -0.5
NFILL = 0


class Res:
    __slots__ = ("name", "writers", "readers", "prev")

    def __init__(self, name):
        self.name = name
        self.writers = []
        self.readers = []
        self.prev = []


class Op:
    __slots__ = ("eng", "fn", "deps", "dma", "stream", "signal", "semval", "idx", "n_dma")

    def __init__(self, eng, fn, dma=False, stream=None, n_dma=1):
        self.eng = eng
        self.fn = fn
        self.deps = []
        self.dma = dma
        self.stream = stream
        self.signal = False
        self.semval = None
        self.idx = None
        self.n_dma = n_dma


class Sched:
    ENGS = ("pe", "act", "dve", "pool", "sp")

    def __init__(self, nc):
        self.nc = nc
        self.ops = []

    def op(self, eng, fn, reads=(), writes=(), dma=False, stream=None, n_dma=1, cowrites=()):
        o = Op(eng, fn, dma=dma, stream=stream, n_dma=n_dma)
        o.idx = len(self.ops)
        deps = set()
        writes = list(writes)
        co = []
        for r in cowrites:
            if r.readers or not r.writers:
                writes.append(r)
            else:
                co.append(r)
        for r in reads:
            deps.update(r.writers)
        for r in writes:
            deps.update(r.writers)
            deps.update(r.readers)
        for r in co:
            deps.update(r.prev)
        o.deps = sorted(deps, key=lambda d: d.idx)
        for r in reads:
            r.readers.append(o)
        for r in writes:
            r.prev = list(r.writers) + list(r.readers)
            r.writers = [o]
            r.readers = []
        for r in co:
            r.writers.append(o)
        self.ops.append(o)
        return o

    def dma(self, eng, stream, fn, reads=(), writes=(), n_dma=1):
        return self.op(eng, fn, reads=reads, writes=writes, dma=True, stream=stream, n_dma=n_dma)

    def emit(self):
        nc = self.nc
        for o in self.ops:
            if o.dma:
                o.signal = True
            for d in o.deps:
                if d.dma:
                    continue
                if d.eng == o.eng and d.eng == "pe" and not o.dma:
                    continue
                d.signal = True
        cnt = {e: 0 for e in self.ENGS}
        scnt = {}
        for o in self.ops:
            if o.dma:
                scnt[o.stream] = scnt.get(o.stream, 0) + 16 * o.n_dma
                o.semval = scnt[o.stream]
            elif o.signal:
                cnt[o.eng] += 1
                o.semval = cnt[o.eng]
        stream_names = sorted(scnt.keys())
        with contextlib.ExitStack() as es:
            esem = {e: es.enter_context(nc.semaphore("s_" + e)) for e in self.ENGS}
            ssem = {s: es.enter_context(nc.semaphore("d_" + s)) for s in stream_names}
            block = es.enter_context(nc.Block())
            by_eng = {e: [o for o in self.ops if o.eng == e] for e in self.ENGS}

            def run(eng_name, eng):
                waited = {}
                for o in by_eng[eng_name]:
                    need = {}
                    for d in o.deps:
                        if d.dma:
                            key = ("s", d.stream)
                        else:
                            if d.eng == eng_name and eng_name == "pe" and not o.dma:
                                continue
                            key = ("e", d.eng)
                        if d.semval > need.get(key, 0):
                            need[key] = d.semval
                    for key, v in need.items():
                        if waited.get(key, 0) >= v:
                            continue
                        waited[key] = v
                        sem = ssem[key[1]] if key[0] == "s" else esem[key[1]]
                        eng.wait_ge(sem, v)
                    ins = o.fn(eng)
                    if o.dma:
                        if not isinstance(ins, (list, tuple)):
                            ins = [ins]
                        assert len(ins) == o.n_dma, (len(ins), o.n_dma)
                        for i in ins:
                            i.then_inc(ssem[o.stream], 16)
                    elif o.signal:
                        ins.then_inc(esem[eng_name], 1)
                if eng_name == "sp":
                    for s in stream_names:
                        if waited.get(("s", s), 0) < scnt[s]:
                            eng.wait_ge(ssem[s], scnt[s])
                    for e in self.ENGS:
                        if e != "sp" and cnt[e] > 0:
                            eng.wait_ge(esem[e], cnt[e])

            @block.tensor
            def _(eng):
                run("pe", eng)

            @block.scalar
            def _(eng):
                run("act", eng)

            @block.vector
            def _(eng):
                run("dve", eng)

            @block.gpsimd
            def _(eng):
                run("pool", eng)

            @block.sync
            def _(eng):
                run("sp", eng)


def build_nc(ntiles=NTILES, dbg=False):
    nc = bass.Bass("TRN2", target_bir_lowering=False)
    DBG = {}
    if dbg:
        DBG['h0'] = nc.dram_tensor('d_h0', [512, D], F32, kind='ExternalOutput').ap()
        DBG['h1'] = nc.dram_tensor('d_h1', [512, D], F32, kind='ExternalOutput').ap()
        DBG['oa'] = nc.dram_tensor('d_oa', [128, 8 * NQ], BF16, kind='ExternalOutput').ap()
        DBG['ob'] = nc.dram_tensor('d_ob', [128, 8 * NQ], BF16, kind='ExternalOutput').ap()
        DBG['mg'] = nc.dram_tensor('d_mg', [128, 16 * NQ], BF16, kind='ExternalOutput').ap()
        DBG['uT'] = nc.dram_tensor('d_uT', [128, NCH * NQ], BF16, kind='ExternalOutput').ap()
        DBG['qa'] = nc.dram_tensor('d_qa', [128, 8 * NQ], BF16, kind='ExternalOutput').ap()
        DBG['ka'] = nc.dram_tensor('d_ka', [128, 8 * NWT], BF16, kind='ExternalOutput').ap()
        DBG['va'] = nc.dram_tensor('d_va', [128, 10 * 1024], BF16, kind='ExternalOutput').ap()
        DBG['h0T'] = nc.dram_tensor('d_h0T', [128, 16 * NWT], BF16, kind='ExternalOutput').ap()

    def din(name, shape):
        return nc.dram_tensor(name, list(shape), F32, kind="ExternalInput").ap()

    xe = din("xe", [XE_ROWS, D])
    xh = din("xh", [NTILES, 128, 32])
    meta = din("meta", [NMETA, D])
    lnv = din("lnv", [6, D])
    lnT_d = din("lnT", [128, 64])
    convw_d = din("convw", [128, 4 * NCH])
    sink_d = din("sink", [128, 8])
    nabg_d = din("nab_g", [128, 7168])
    nabm_d = din("nab_m", [128, 7168])
    swb_d = din("swb", [128, 3072])
    lrow_d = din("lrow", [18, NW])
    rna_d = din("rna", [NTILES, 18, NQ])
    fsw_d = din("fsw", [128, NTILES * 9])
    flag_d = din("flag", [128, NTILES])
    w_in = din("w_in", [INC // 256, 128, 4096])
    w_pa = din("w_pa", [8, 128, 2048])
    w_pb = din("w_pb", [8, 128, 2048])
    w_out = din("w_out", [8, 128, 4096])
    w_fi = din("w_fi", [2 * DFF // 256, 128, 4096])
    w_fd = din("w_fd", [24, 128, 4096])
    y = nc.dram_tensor("y", [TOK, D], F32, kind="ExternalOutput").ap()

    es = contextlib.ExitStack()
    with es:
        def sb(name, shape, dt):
            return es.enter_context(nc.sbuf_tensor("sb_" + name, list(shape), dt))

        h0 = sb("h0", [128, 4, D], F32)
        h0T = sb("h0T", [128, 16, NWT], BF16)
        R1 = sb("R1", [128, 23808], BF16)
        R2 = sb("R2", [128, 8224], BF16)
        wbuf = [sb("wb%d" % i, [128, 4096], BF16) for i in range(3)]
        LNp = sb("LNp", [128, 2, D], F32)
        Btab = sb("Btab", [128, 7168], BF16)
        BWtab = sb("BWtab", [128, 3072], BF16)
        lrow = sb("lrow", [18, NW], BF16)
        rna = [sb("rna0", [18, NQ], BF16)] * 2
        ident = sb("ident", [128, 128], BF16)
        ones_bf = sb("ones_bf", [128, 128], BF16)
        ones_f = sb("ones_f", [128, 128], F32)
        SCR = sb("SCR", [128, 2048], F32)
        PT = [sb("PT%d" % i, [128, NQ], BF16) for i in range(2)]
        lnT = sb("lnT", [128, 64], F32)
        convw = sb("convw", [128, 4 * NCH], F32)
        esink = sb("esink", [128, 8], F32)
        fsw = sb("fsw", [128, NTILES * 9], F32)
        flag = sb("flag", [128, NTILES], F32)
        small = sb("small", [128, 64], F32)
        xhs = sb("xhs", [128, 16, 2], F32)
        h0h = sb("h0h", [128, 16, 2], F32)
        zh = sb("zh", [128, 16, 2], F32)
        zh2 = sb("zh2", [128, 16, 2], F32)
        bnst = sb("bnst", [128, 48], F32)

        KT_A = R1[:, 0:9344].rearrange("p (h n) -> p h n", h=8)
        V_A = R1[:, 9344:19584].rearrange("p (t c) -> p t c", t=10)
        QT_A = R1[:, 19584:23696].rearrange("p (h n) -> p h n", h=8)
        KT_B = R1[:, 0:2336].rearrange("p (h n) -> p h n", h=2)
        V_B = R1[:, 2336:4896].rearrange("p (t c) -> p t c", t=10)
        QT_B = R1[:, 4896:9008].rearrange("p (h n) -> p h n", h=8)
        mergedT = R1[:, 0:8224].rearrange("p (c n) -> p c n", c=16)
        uT = R1[:, 0:22616].rearrange("p (c n) -> p c n", c=NCH)
        xs = R2[:, 0:4096].bitcast(F32)
        hn = R2[:, 4096:6144]
        hnbufs = [R2[:, 4096:6144], R2[:, 6144:8192]]
        xsbufs = [R2[:, 0:4096].bitcast(F32), SCR[:, 0:2048]]
        oaT = R2[:, 0:4112].rearrange("p (h n) -> p h n", h=8)
        obT = R2[:, 4112:8224].rearrange("p (h n) -> p h n", h=8)
        sga = SCR[:, 0:NQ]
        sgb = SCR[:, 514:514 + NQ]
        rD = SCR[:, 1028:1028 + NQ]
        Araw = SCR[:, 0:516]
        cbuf = SCR[:, 516:1028]
        gbuf = SCR[:, 1028:1540]

        banks = [es.enter_context(nc.psum_tensor("pb%d" % i, [128, 512], F32)) for i in range(8)]
        rb = [Res("bank%d" % i) for i in range(8)]

        S = Sched(nc)
        r_h0 = [Res("h0_%d" % j) for j in range(4)]
        r_hTl, r_hTq, r_hTh = Res("h0T_lo"), Res("h0T_q"), Res("h0T_hi")
        r_hTall = [r_hTl, r_hTq, r_hTh]
        r_R1 = Res("R1guard")
        r_R2 = Res("R2guard")
        r_xs, r_hn = Res("xs"), Res("hn")
        r_hnb = [Res("hnA"), Res("hnB")]
        r_bnb = [Res("bn0"), Res("bn1")]
        r_smb = [Res("sm0"), Res("sm1")]
        r_wb = [Res("wb%d" % i) for i in range(3)]
        r_LNp = Res("LNp")
        r_const = Res("const")
        r_rna = [Res("rna0")] * 2
        r_SCR = Res("SCR")
        r_Oc, r_rD = Res("Oc"), Res("rD")
        r_xsb = [Res("xsA"), r_SCR]
        r_PT = [Res("PT0"), Res("PT1")]
        r_small = Res("small")
        r_xhs, r_h0h, r_zh = Res("xhs"), Res("h0h"), Res("zh")
        r_bn = Res("bnst")
        r_KTA = [Res("KTA%d" % h) for h in range(8)]
        r_VA = [Res("VA%d" % t) for t in range(10)]
        r_QTA = [Res("QTA%d" % h) for h in range(8)]
        r_KTB = [Res("KTB%d" % h) for h in range(2)]
        r_VB = [Res("VB%d" % t) for t in range(10)]
        r_QTB = [Res("QTB%d" % h) for h in range(8)]
        r_oa = [Res("oa%d" % h) for h in range(8)]
        r_ob = [Res("ob%d" % h) for h in range(8)]
        r_mg = [Res("mg%d" % c) for c in range(16)]
        r_uT = [Res("uT%d" % c) for c in range(NCH)]
        r_y = Res("y")
        r_yj = [Res("y%d" % j) for j in range(4)]

        wstate = {"n": 0}

        def wload(src2, kk, cols):
            i = wstate["n"] % 3
            wstate["n"] += 1
            flat = wbuf[i][:, 0:kk * cols]
            view = flat.rearrange("p (k c) -> p k c", k=kk)
            S.dma("pool", "wb%d" % i, lambda e, flat=flat, src2=src2: e.dma_start(out=flat, in_=src2),
                  writes=[r_wb[i]])
            return view, r_wb[i]

        def wcols(w, c0, ncols, k0=0, nk=None):
            v = w.rearrange("(k p) c -> p k c", p=128)
            if nk is None:
                nk = v.shape[1] - k0
            return v[:, k0:k0 + nk, c0:c0 + ncols]

        def setup():
            S.dma("sp", "c0", lambda e: [e.dma_start(out=lnT[:], in_=lnT_d[:, :]),
                                         e.dma_start(out=convw[:], in_=convw_d[:, :]),
                                         e.dma_start(out=esink[:], in_=sink_d[:, :]),
                                         e.dma_start(out=fsw[:], in_=fsw_d[:, :]),
                                         e.dma_start(out=flag[:], in_=flag_d[:, :])],
                  writes=[r_const], n_dma=5)
            S.dma("pool", "c1", lambda e: [e.dma_start(out=BWtab[:], in_=swb_d[:, :]),
                                           e.dma_start(out=lrow[:], in_=lrow_d[:, :])],
                  writes=[r_const], n_dma=2)
            stg_g = h0[:, :, :].rearrange("p a f -> p (a f)")[:, 0:7168]
            stg_m = R1[:, 0:14336].bitcast(F32)
            S.dma("sp", "c2", lambda e: [e.dma_start(out=stg_g, in_=nabg_d[:, :]),
                                         e.dma_start(out=stg_m, in_=nabm_d[:, :])],
                  writes=[r_h0[0], r_R1], n_dma=2)
            S.op("dve", lambda e: e.tensor_tensor(out=Btab[:], in0=stg_g, in1=stg_m, op=ALU.add),
                 reads=[r_h0[0], r_R1], writes=[r_const])

            def mk_ones(e):
                e.memset(ones_f[:], 1.0)
                return e.memset(SCR[:, 0:128], 1.0)
            S.op("pool", mk_ones, writes=[r_SCR, r_const])
            S.op("pool", lambda e: e.affine_select(out=SCR[:, 0:128], in_=SCR[:, 0:128], pattern=[[-1, 128]],
                                                   compare_op=ALU.is_equal, fill=0.0, base=0, channel_multiplier=1),
                 reads=[r_SCR], writes=[r_SCR])

            def mk_consts(e):
                e.tensor_copy(out=ident[:], in_=SCR[:, 0:128])
                e.tensor_copy(out=ones_bf[:], in_=ones_f[:])
                e.memset(small[:, 0:1], EPS)
                return e.memset(small[:, 1:2], 0.0)
            S.op("dve", mk_consts, reads=[r_SCR], writes=[r_const, r_small])
            S.op("act", lambda e: e.activation(out=esink[:], in_=esink[:], func=AF.Exp),
                 reads=[r_const], writes=[r_const])

        eps_ap = small[:, 0:1]

        lnstate = {"n": 0}

        def ln_tok(src, np_, gi, out_f32=None, out_bf=None):
            ss = lnstate["n"] % 2
            lnstate["n"] += 1
            sc = 8 + 16 * ss
            bo = 24 * ss
            rsm, rbn = r_smb[ss], r_bnb[ss]
            extra = gi.get("extra", [])

            def stats(e):
                last = None
                for c in range(4):
                    last = e.bn_stats(out=bnst[0:np_, bo + 6 * c:bo + 6 * c + 6], in_=src[:, c * 512:(c + 1) * 512])
                return last
            S.op("dve", stats, reads=gi["r_src"], writes=[rbn])
            tick()
            S.op("dve", lambda e: e.bn_aggr(out=small[0:np_, sc:sc + 2], in_=bnst[0:np_, bo:bo + 24]), reads=[rbn], writes=[rsm])
            tick()
            S.op("act", lambda e: e.activation(out=small[0:np_, sc + 2:sc + 3], in_=small[0:np_, sc + 1:sc + 2], func=AF.Ln,
                                               bias=eps_ap[0:np_, :], scale=1.0), reads=[rsm, r_const], writes=[rsm])
            tick()
            S.op("act", lambda e: e.activation(out=small[0:np_, sc + 2:sc + 3], in_=small[0:np_, sc + 2:sc + 3], func=AF.Exp, scale=-0.5),
                 reads=[rsm], writes=[rsm])
            tick()
            if gi.get("feat"):
                S.op("dve", lambda e: e.scalar_tensor_tensor(out=small[0:np_, sc + 3:sc + 4], in0=small[0:np_, sc:sc + 1], scalar=-1.0,
                                                             in1=small[0:np_, sc + 2:sc + 3], op0=ALU.mult, op1=ALU.mult),
                     reads=[rsm], writes=[rsm])
                tick()
                S.op("act", lambda e: e.activation(out=out_bf, in_=src, func=AF.Identity, bias=small[0:np_, sc + 3:sc + 4],
                                                   scale=small[0:np_, sc + 2:sc + 3]),
                     reads=[rsm] + gi["r_src"] + extra, writes=gi["r_bf"])
                tick()
                return
            S.op("dve", lambda e: e.scalar_tensor_tensor(out=src, in0=src, scalar=small[0:np_, sc:sc + 1], in1=LNp[0:np_, 0, :],
                                                         op0=ALU.subtract, op1=ALU.mult),
                 reads=[rsm, r_LNp] + gi["r_src"], writes=gi["r_src"])
            tick()
            if out_f32 is not None:
                S.op("dve", lambda e: e.scalar_tensor_tensor(out=out_f32, in0=src, scalar=small[0:np_, sc + 2:sc + 3], in1=LNp[0:np_, 1, :],
                                                             op0=ALU.mult, op1=ALU.add),
                     reads=[rsm, r_LNp] + gi["r_src"], writes=gi["r_dst"])
                tick()
                if out_bf is not None:
                    S.op("act", lambda e: e.copy(out=out_bf, in_=out_f32), reads=gi["r_dst"] + extra, writes=gi["r_bf"])
                    tick()
            else:
                S.op("dve", lambda e: e.scalar_tensor_tensor(out=out_bf, in0=src, scalar=small[0:np_, sc + 2:sc + 3], in1=LNp[0:np_, 1, :],
                                                             op0=ALU.mult, op1=ALU.add),
                     reads=[rsm, r_LNp] + gi["r_src"] + extra, writes=gi["r_bf"])
                tick()

        def load_lnp(i):
            S.dma("sp", "lnp", lambda e: [e.dma_start(out=LNp[:, 0:1, :], in_=lnv[2 * i:2 * i + 1, :].partition_broadcast(128)),
                                          e.dma_start(out=LNp[:, 1:2, :], in_=lnv[2 * i + 1:2 * i + 2, :].partition_broadcast(128))],
                  writes=[r_LNp], n_dma=2)

        tstate = {"n": 0}

        def transpose_to(np_, dst_fn, reads, writes, hnb=None, r_hnx=None, affine=None):
            for half in range(2):
                bi = 6 + (tstate["n"] % 2)
                tstate["n"] += 1
                pT = banks[bi][:, :].bitcast(BF16)

                def tr(e, half=half, pT=pT):
                    last = None
                    for j in range(8):
                        k = half * 8 + j
                        last = e.transpose(out=pT[:, j * 128:j * 128 + np_], in_=(hn if hnb is None else hnb)[0:np_, k * 128:(k + 1) * 128],
                                           identity=ident[0:np_, 0:np_])
                    return last
                S.op("pe", tr, reads=[r_hn if r_hnx is None else r_hnx, r_const], writes=[rb[bi]])
                if affine is None:
                    src = pT[:, :].rearrange("p (a b) -> p a b", a=8)[:, :, 0:np_]
                    S.op("act", lambda e, half=half, src=src: e.copy(out=dst_fn(half * 8, 8), in_=src),
                         reads=[rb[bi]] + reads, writes=writes)
                else:
                    g0, b0 = affine

                    def ev_act(e, half=half, pT=pT):
                        last = None
                        for j in range(8):
                            k = half * 8 + j
                            last = e.activation(out=dst_fn(k, 1)[:, 0, :], in_=pT[:, j * 128:j * 128 + np_], func=AF.Identity,
                                                bias=lnT[:, b0 + k:b0 + k + 1], scale=lnT[:, g0 + k:g0 + k + 1])
                        return last

                    def ev_dve(e, half=half, pT=pT):
                        last = None
                        for j in range(8):
                            k = half * 8 + j
                            last = e.tensor_scalar(out=dst_fn(k, 1)[:, 0, :], in0=pT[:, j * 128:j * 128 + np_],
                                                   scalar1=lnT[:, g0 + k:g0 + k + 1], scalar2=lnT[:, b0 + k:b0 + k + 1],
                                                   op0=ALU.mult, op1=ALU.add)
                        return last
                    if half == 0:
                        S.op("act", ev_act, reads=[rb[bi], r_const] + reads, writes=writes)
                    else:
                        S.op("dve", ev_dve, reads=[rb[bi], r_const] + reads, cowrites=writes)

        def ln_feat_gen(x3, gcol, bcol, out3, r_in, r_out, hb_bank):
            S.op("dve", lambda e: e.tensor_tensor(out=zh2[:], in0=x3, in1=x3, op=ALU.mult), reads=r_in, writes=[r_zh])
            yield
            ps = banks[hb_bank]

            def mm(e):
                last = None
                for k in range(16):
                    last = e.matmul(ps[:, 0:2], lhsT=ones_f[:], rhs=x3[:, k, :], start=(k == 0), stop=False)
                for k in range(16):
                    last = e.matmul(ps[:, 2:4], lhsT=ones_f[:], rhs=zh2[:, k, :], start=False, stop=(k == 15))
                return last
            S.op("pe", mm, reads=r_in + [r_zh, r_const], writes=[rb[hb_bank]])
            yield
            S.op("dve", lambda e: e.tensor_scalar(out=small[:, 16:20], in0=ps[:, 0:4], scalar1=1.0 / D, scalar2=None, op0=ALU.mult),
                 reads=[rb[hb_bank]], writes=[r_small])
            yield
            S.op("dve", lambda e: e.tensor_tensor(out=small[:, 20:22], in0=small[:, 16:18], in1=small[:, 16:18], op=ALU.mult),
                 reads=[r_small], writes=[r_small])
            yield
            S.op("dve", lambda e: e.tensor_tensor(out=small[:, 18:20], in0=small[:, 18:20], in1=small[:, 20:22], op=ALU.subtract),
                 reads=[r_small], writes=[r_small])
            yield
            S.op("act", lambda e: e.activation(out=small[:, 18:20], in_=small[:, 18:20], func=AF.Ln, bias=eps_ap, scale=1.0),
                 reads=[r_small, r_const], writes=[r_small])
            yield
            S.op("act", lambda e: e.activation(out=small[:, 18:20], in_=small[:, 18:20], func=AF.Exp, scale=-0.5),
                 reads=[r_small], writes=[r_small])
            yield

            def n1(e):
                e.tensor_scalar(out=zh2[:, :, 0], in0=x3[:, :, 0], scalar1=small[:, 16:17], scalar2=small[:, 18:19], op0=ALU.subtract, op1=ALU.mult)
                return e.tensor_scalar(out=zh2[:, :, 1], in0=x3[:, :, 1], scalar1=small[:, 17:18], scalar2=small[:, 19:20], op0=ALU.subtract, op1=ALU.mult)
            S.op("dve", n1, reads=r_in + [r_small, r_zh], writes=[r_zh])
            yield

            def n2(e):
                e.tensor_tensor(out=zh2[:, :, 0], in0=zh2[:, :, 0], in1=gcol, op=ALU.mult)
                return e.tensor_tensor(out=zh2[:, :, 1], in0=zh2[:, :, 1], in1=gcol, op=ALU.mult)
            S.op("dve", n2, reads=[r_zh, r_const], writes=[r_zh])
            yield

            def n3(e):
                e.tensor_tensor(out=out3[:, :, 0], in0=zh2[:, :, 0], in1=bcol, op=ALU.add)
                return e.tensor_tensor(out=out3[:, :, 1], in0=zh2[:, :, 1], in1=bcol, op=ALU.add)
            S.op("dve", n3, reads=[r_zh, r_const], writes=r_out + [r_zh])
            yield

        def ln_feat(*args):
            for _ in ln_feat_gen(*args):
                pass

        tickstate = {"g": None}

        def tick(n=1):
            for _ in range(n):
                g = tickstate["g"]
                if g is None:
                    return
                try:
                    next(g)
                except StopIteration:
                    tickstate["g"] = None

        def drain():
            while tickstate["g"] is not None:
                tick()

        fm_state = {"main": 0, "tail": 0}

        def fm_proj(lhs_fn, nk, rhs_fn, ncols, reads, consume, main_banks=None, tail_banks=None):
            if ncols == NQ:
                pieces = [(0, 257), (257, NQ)]
            else:
                pieces = []
                c = 0
                while c < ncols:
                    n = min(512, ncols - c)
                    pieces.append((c, c + n))
                    c += n
            for (c0, c1) in pieces:
                n = c1 - c0
                bi = fm_state["main"] % 6
                fm_state["main"] += 1
                ps = banks[bi][:, 0:n]

                def mm(e, ps=ps, c0=c0, c1=c1):
                    last = None
                    for k in range(nk):
                        last = e.matmul(ps, lhsT=lhs_fn(k), rhs=rhs_fn(k, c0, c1), start=(k == 0), stop=(k == nk - 1))
                    return last
                S.op("pe", mm, reads=reads, writes=[rb[bi]])
                consume(ps, c0, c1, rb[bi])

        def attn_phase(heads):
            Om, Dm = banks[0], banks[1]
            Ot = banks[2][:, 0:2]
            Dt = banks[7][:, 0:2]
            Oc = SCR[:, 0:NQ]
            steps = []
            for H in heads:
                items = [("meta", 0, NQ, None, None, None)] + list(H["key_tiles"])
                for ii, itm in enumerate(items):
                    steps.append((H, itm, ii == 0, ii == len(items) - 1))

            def segs_of(a, b):
                sg = []
                if a < 512:
                    sg.append((a, min(b, 512), False))
                if b > 512:
                    sg.append((max(a, 512), b, True))
                return sg

            def emit_S(si):
                H, (kt, a, b, tabf, mask, bias_ap), _, _ = steps[si]
                segs = segs_of(a, b)
                Sm, St = banks[3 + si % 2], banks[5 + si % 2]
                res_w = [rb[3 + si % 2], rb[5 + si % 2]]
                qT, kt_fn, meta_k = H["qT"], H["kt_fn"], H["meta_k"]

                def smm(e):
                    last = None
                    for (c0, c1, tail) in segs:
                        npo = NMETA if kt == "meta" else 128
                        dst = St[0:npo, c0 - 512:c1 - 512] if tail else Sm[0:npo, c0:c1]
                        if kt == "meta":
                            last = e.matmul(dst, lhsT=meta_k, rhs=qT[:, c0:c1], start=True, stop=True)
                        else:
                            nm = 1 + (tabf is not None) + (mask is not None)
                            j = 0
                            last = e.matmul(dst, lhsT=kt_fn(kt), rhs=qT[:, c0:c1], start=True, stop=(j == nm - 1))
                            if tabf is not None:
                                j += 1
                                last = e.matmul(dst, lhsT=ident[:], rhs=tabf(c0, c1), start=False, stop=(j == nm - 1))
                            if mask is not None:
                                j += 1
                                last = e.matmul(dst, lhsT=mask[0], rhs=mask[1](c0, c1), start=False, stop=(j == nm - 1))
                    return last
                S.op("pe", smm, reads=H["r_reads"], writes=res_w)

            def emit_EXP(si):
                H, (kt, a, b, tabf, mask, bias_ap), _, _ = steps[si]
                segs = segs_of(a, b)
                Sm, St = banks[3 + si % 2], banks[5 + si % 2]
                nkp = NMETA if kt == "meta" else 128
                pt_i = si % 2

                def ex(e):
                    last = None
                    for (c0, c1, tail) in segs:
                        src = St[0:nkp, c0 - 512:c1 - 512] if tail else Sm[0:nkp, c0:c1]
                        if bias_ap is None:
                            last = e.activation(out=PT[pt_i][0:nkp, c0:c1], in_=src, func=AF.Exp)
                        else:
                            last = e.activation(out=PT[pt_i][0:nkp, c0:c1], in_=src, func=AF.Exp, bias=bias_ap[0:nkp, :], scale=1.0)
                    return last
                S.op("act", ex, reads=[rb[3 + si % 2], rb[5 + si % 2], r_const], writes=[r_PT[pt_i]])

            def emit_PV(si):
                H, (kt, a, b, tabf, mask, bias_ap), first, last_item = steps[si]
                segs = segs_of(a, b)
                nkp = NMETA if kt == "meta" else 128
                pt_i = si % 2
                vv = H["meta_v"] if kt == "meta" else H["v_fn"](kt)

                if NFILL:
                    def filler(e):
                        last = None
                        for _ in range(NFILL):
                            last = e.matmul(banks[7][:, :], lhsT=ident[:], rhs=Btab[:, 0:512], start=True, stop=True, skip_group_check=True)
                        return last
                    S.op("pe", filler, reads=[r_const], writes=[rb[7]])

                def pv(e):
                    last = None
                    for (c0, c1, tail) in segs:
                        od = Ot[:, c0 - 512:c1 - 512] if tail else Om[:, c0:c1]
                        dd = Dt[:, c0 - 512:c1 - 512] if tail else Dm[:, c0:c1]
                        e.matmul(od, lhsT=vv, rhs=PT[pt_i][0:nkp, c0:c1], start=first, stop=False, skip_group_check=True)
                        last = e.matmul(dd, lhsT=ones_bf[0:nkp, :], rhs=PT[pt_i][0:nkp, c0:c1], start=first,
                                        stop=False, skip_group_check=True)
                    return last
                S.op("pe", pv, reads=[r_PT[pt_i], r_const] + H["r_reads"], writes=[rb[0], rb[1], rb[2], rb[7]])
                if last_item:
                    sink_ap, out_ap = H["sink_ap"], H["out_ap"]

                    def oc(e):
                        e.copy(out=Oc[:, 0:512], in_=Om[:, :])
                        return e.copy(out=Oc[:, 512:514], in_=Ot)
                    S.op("act", oc, reads=[rb[0], rb[2], r_SCR], writes=[r_Oc])
                    if sink_ap is not None:
                        def f1(e):
                            e.tensor_scalar(out=rD[:, 0:512], in0=Dm[:, :], scalar1=sink_ap, scalar2=None, op0=ALU.add)
                            return e.tensor_scalar(out=rD[:, 512:514], in0=Dt, scalar1=sink_ap, scalar2=None, op0=ALU.add)
                        S.op("dve", f1, reads=[rb[1], rb[7], r_const, r_SCR], writes=[r_rD])
                        S.op("dve", lambda e: e.reciprocal(out=rD[:, :], in_=rD[:, :]), reads=[r_rD], writes=[r_rD])
                    else:
                        def f1(e):
                            e.reciprocal(out=rD[:, 0:512], in_=Dm[:, :])
                            return e.reciprocal(out=rD[:, 512:514], in_=Dt)
                        S.op("dve", f1, reads=[rb[1], rb[7], r_const, r_SCR], writes=[r_rD])
                    S.op("dve", lambda e: e.tensor_tensor(out=out_ap[:, :], in0=Oc[:, :], in1=rD[:, :], op=ALU.mult),
                         reads=[r_Oc, r_rD, r_SCR], writes=H["r_out"])

            s_done = set()

            def ensure_S(k):
                if k < len(steps) and k not in s_done:
                    s_done.add(k)
                    emit_S(k)

            ensure_S(0)
            for si in range(len(steps)):
                ensure_S(si + 1)
                emit_EXP(si)
                if steps[si][2]:
                    ensure_S(si + 2)
                emit_PV(si)

        def guard(res, eng="dve"):
            if eng == "act":
                S.op("act", lambda e: e.copy(out=small[:, 3:4], in_=small[:, 1:2]), reads=[r_const], writes=[res])
            else:
                S.op("dve", lambda e: e.memset(small[:, 2:3], 0.0), writes=[res])

        s1state = {"h": 0, "x": 0}

        s1buf = {}

        def s1_ln(i, w, only_a=False):
            row0 = TT * i
            np_ = 128 if w < 9 else NMETA
            src_rows = xe[row0 + 128 * w: row0 + 128 * w + 128, :] if w < 9 else meta[:, :]
            hi = s1state["h"] % 2
            s1state["h"] += 1
            hnb, r_hnx = hnbufs[hi], r_hnb[hi]
            xi = 0 if only_a else (s1state["x"] % 2)
            s1state["x"] += 1
            xsb, r_x = xsbufs[xi], r_xsb[xi]
            S.dma("sp", "xs%d" % xi, lambda e, np_=np_, src_rows=src_rows, xsb=xsb: e.dma_start(out=xsb[0:np_, :], in_=src_rows),
                  reads=[r_R2], writes=[r_x])
            gi = {"r_src": [r_x], "r_dst": None, "r_bf": [r_hnx], "extra": [r_R2], "feat": True}
            ln_tok(xsb[0:np_, :], np_, gi, out_bf=hnb[0:np_, :])
            s1buf[(i, w)] = (hnb, r_hnx, np_)

        def s1_tr(i, w):
            hnb, r_hnx, np_ = s1buf.pop((i, w))
            c0 = 128 * w
            dstf = (lambda k0, nk, c0=c0, np_=np_: h0T[:, k0:k0 + nk, c0:c0 + np_])
            wr = {0: [r_hTl], 1: [r_hTl], 2: [r_hTl, r_hTq], 3: [r_hTq], 4: [r_hTq], 5: [r_hTq], 6: [r_hTq],
                  7: [r_hTq, r_hTh], 8: [r_hTh], 9: [r_hTh]}[w]
            transpose_to(np_, dstf, [r_R2], wr, hnb=hnb, r_hnx=r_hnx, affine=(0, 16))

        def s1_subtile(i, w):
            s1_ln(i, w)
            s1_tr(i, w)

        def s1_resid(i, j):
            row0 = TT * i + 128 * (3 + j)
            S.dma("sp", "xh0_%d" % j, lambda e: e.dma_start(out=h0[:, j, :], in_=xe[row0:row0 + 128, :]), writes=[r_h0[j]])
            gi = {"r_src": [r_h0[j]], "r_dst": [r_h0[j]], "r_bf": None}
            ln_tok(h0[:, j, :], 128, gi, out_f32=h0[:, j, :])

        mov_groups = [(0, 512), (512, 1024), (1024, NWT)]

        def stage_KA(hp):
            wv, rw = wload(w_in[4 + hp, :, :], 16, 256)
            for hh in range(2):
                h = 2 * hp + hh
                for (m0, m1) in mov_groups:
                    def cons(ps, c0, c1, rbk, h=h, m0=m0):
                        S.op("act", lambda e: e.copy(out=KT_A[:, h, m0 + c0:m0 + c1], in_=ps), reads=[rbk, r_R1], writes=[r_KTA[h]])
                    fm_proj(lambda k, wv=wv, hh=hh: wv[:, k, hh * 128:(hh + 1) * 128], 16,
                            lambda k, c0, c1, m0=m0: h0T[:, k, m0 + c0:m0 + c1], m1 - m0, r_hTall + [rw], cons)


        def stage_ln2(i, j):
            gi = {"r_src": [r_h0[j]], "r_dst": [r_h0[j]], "r_bf": None}
            ln_tok(h0[:, j, :], 128, gi, out_f32=h0[:, j, :])
            r0 = TT * i + 128 * j
            S.dma("sp", "y%d" % j, lambda e, j=j, r0=r0: e.dma_start(out=y[r0:r0 + 128, :], in_=h0[:, j, :]), reads=[r_h0[j]], writes=[r_yj[j]])

        def tile(i):
            row0 = TT * i
            if i == 0:
                guard(r_R2)
                for w in range(10):
                    s1_subtile(0, w)
            load_lnp(0)
            S.dma("sp", "xh", lambda e: e.dma_start(out=xhs[:].rearrange("p k t -> p (k t)"), in_=xh[i, :, :]), writes=[r_xhs])
            S.dma("pool", "rna", lambda e: e.dma_start(out=rna[0][:], in_=rna_d[i, :, :]), writes=[r_rna[0]])

            if dbg and i == 0:
                S.dma('sp', 'dbg', lambda e: [e.dma_start(out=DBG['h0'][128 * j:128 * j + 128, :], in_=h0[:, j, :]) for j in range(4)], reads=r_h0, n_dma=4)
                S.dma('sp', 'dbg', lambda e: e.dma_start(out=DBG['h0T'][:, :], in_=h0T[:, :, :].rearrange('p a b -> p (a b)')), reads=r_hTall)
            if i == 0:
                guard(r_R1)
            if i == 0:
                for hp in range(4):
                    stage_KA(hp)
            for pc in range(4):
                wv, rw = wload(w_in[8 + pc, :, :], 16, 256)
                for t in range(10):
                    npk = 128 if t < 9 else NMETA
                    bi = fm_state["main"] % 6
                    fm_state["main"] += 1
                    ps = banks[bi][0:npk, 0:256]

                    def mm(e, ps=ps, t=t, npk=npk, wv=wv):
                        last = None
                        for k in range(16):
                            last = e.matmul(ps, lhsT=h0T[:, k, 128 * t:128 * t + npk], rhs=wv[:, k, :], start=(k == 0), stop=(k == 15))
                        return last
                    S.op("pe", mm, reads=r_hTall + [rw], writes=[rb[bi]])
                    S.op("act", lambda e, ps=ps, t=t, npk=npk, pc=pc: e.copy(out=V_A[0:npk, t, 256 * pc:256 * pc + 256], in_=ps),
                         reads=[rb[bi], r_R1], writes=[r_VA[t]])
                    if t == 9:
                        s1_resid(i, pc)
            tickstate["g"] = ln_feat_gen(xhs[:], lnT[:, 0:16], lnT[:, 16:32], h0h[:], [r_xhs], [r_h0h], 7)
            for hp in range(4):
                wv, rw = wload(w_in[hp, :, :], 16, 256)
                for hh in range(2):
                    h = 2 * hp + hh
                    tick(2)

                    def cons(ps, c0, c1, rbk, h=h):
                        S.op("act", lambda e: e.mul(out=QT_A[:, h, c0:c1], in_=ps, mul=SCALE), reads=[rbk, r_R1], writes=[r_QTA[h]])
                    fm_proj(lambda k, wv=wv, hh=hh: wv[:, k, hh * 128:(hh + 1) * 128], 16,
                            lambda k, c0, c1: h0T[:, k, Q0 + c0:Q0 + c1], NQ, [r_hTq, rw], cons)

            if dbg and i == 0:
                S.dma('sp', 'dbg', lambda e: [e.dma_start(out=DBG['qa'][:, :], in_=R1[:, 19584:23696]), e.dma_start(out=DBG['ka'][:, :], in_=R1[:, 0:9344]), e.dma_start(out=DBG['va'][:, :], in_=R1[:, 9344:19584])], reads=r_QTA + r_KTA + r_VA, n_dma=3)
            drain()
            guard(r_R2, "act")
            dlo, dhi = (0, 5) if i == 0 else ((-1, 4) if i == NTILES - 1 else (0, 4))
            rn = rna[i % 2]
            heads = []
            for h in range(8):
                kts = []
                for kt in range(9):
                    plo, phi = kt - 1 - dhi, kt - 1 - dlo
                    plo, phi = max(plo, -1), min(phi, 4)
                    if plo > phi:
                        continue
                    a = 0 if plo == -1 else 1 + 128 * plo
                    b = NQ if phi == 4 else 1 + 128 * (phi + 1)
                    t0 = h * 896 + (6 - kt) * 128 - 1
                    kts.append((kt, a, b,
                                (lambda c0, c1, t0=t0: Btab[:, t0 + c0:t0 + c1]),
                                (lrow[:, 128 * kt:128 * kt + 128], (lambda c0, c1, rn=rn: rn[:, c0:c1])),
                                None))
                heads.append(dict(kt_fn=(lambda kt, h=h: KT_A[:, h, 128 * kt:128 * kt + 128]),
                                  v_fn=(lambda kt, h=h: V_A[:, kt, 128 * h:128 * h + 128]),
                                  qT=QT_A[:, h, :], key_tiles=kts, meta_k=KT_A[:, h, NW:NWT],
                                  meta_v=V_A[0:NMETA, 9, 128 * h:128 * h + 128], out_ap=oaT[:, h, :],
                                  r_reads=[r_KTA[h], r_QTA[h], r_const, r_rna[i % 2], r_R1] + r_VA,
                                  r_out=[r_oa[h], r_R2], sink_ap=None))
            attn_phase(heads)

            guard(r_R1)
            wv, rw = wload(w_in[16, :, :], 16, 256)
            for g in range(2):
                for (m0, m1) in mov_groups:
                    def cons(ps, c0, c1, rbk, g=g, m0=m0):
                        S.op("act", lambda e: e.copy(out=KT_B[:, g, m0 + c0:m0 + c1], in_=ps), reads=[rbk, r_R1], writes=[r_KTB[g]])
                    fm_proj(lambda k, wv=wv, g=g: wv[:, k, g * 128:(g + 1) * 128], 16,
                            lambda k, c0, c1, m0=m0: h0T[:, k, m0 + c0:m0 + c1], m1 - m0, r_hTall + [rw], cons)
            wv, rw = wload(w_in[17, :, :], 16, 256)
            for t in range(10):
                npk = 128 if t < 9 else NMETA
                bi = fm_state["main"] % 6
                fm_state["main"] += 1
                ps = banks[bi][0:npk, 0:256]

                def mm(e, ps=ps, t=t, npk=npk, wv=wv):
                    last = None
                    for k in range(16):
                        last = e.matmul(ps, lhsT=h0T[:, k, 128 * t:128 * t + npk], rhs=wv[:, k, :], start=(k == 0), stop=(k == 15))
                    return last
                S.op("pe", mm, reads=r_hTall + [rw], writes=[rb[bi]])
                S.op("act", lambda e, ps=ps, t=t, npk=npk: e.copy(out=V_B[0:npk, t, :], in_=ps),
                     reads=[rb[bi], r_R1], writes=[r_VB[t]])
            for hp in range(4):
                wv, rw = wload(w_in[12 + hp, :, :], 16, 256)
                for hh in range(2):
                    h = 2 * hp + hh

                    def cons(ps, c0, c1, rbk, h=h):
                        S.op("act", lambda e: e.mul(out=QT_B[:, h, c0:c1], in_=ps, mul=SCALE), reads=[rbk, r_R1], writes=[r_QTB[h]])
                    fm_proj(lambda k, wv=wv, hh=hh: wv[:, k, hh * 128:(hh + 1) * 128], 16,
                            lambda k, c0, c1: h0T[:, k, Q0 + c0:Q0 + c1], NQ, [r_hTq, rw], cons)

            heads = []
            for hq in range(8):
                g = hq // 4
                kts = []
                for kb in range(1, 9):
                    a = max(0, 1 + 128 * (kb - 4))
                    b = min(NQ, 1 + 128 * (kb - 1))
                    if a >= b:
                        continue
                    t0 = hq * 384 + (4 - kb) * 128 - 1
                    kts.append((kb, a, b, (lambda c0, c1, t0=t0: BWtab[:, t0 + c0:t0 + c1]), None,
                                fsw[:, i * 9 + kb:i * 9 + kb + 1]))
                heads.append(dict(kt_fn=(lambda kb, g=g: KT_B[:, g, 128 * kb:128 * kb + 128]),
                                  v_fn=(lambda kb, g=g: V_B[:, kb, 128 * g:128 * g + 128]),
                                  qT=QT_B[:, hq, :], key_tiles=kts, meta_k=KT_B[:, g, NW:NWT],
                                  meta_v=V_B[0:NMETA, 9, 128 * g:128 * g + 128], out_ap=obT[:, hq, :],
                                  r_reads=[r_KTB[g], r_QTB[hq], r_const, r_R1] + r_VB,
                                  r_out=[r_ob[hq], r_R2], sink_ap=esink[:, hq:hq + 1]))
            attn_phase(heads)

            if dbg and i == 0:
                S.dma('sp', 'dbg', lambda e: [e.dma_start(out=DBG['oa'][:, :], in_=R2[:, 0:4112]), e.dma_start(out=DBG['ob'][:, :], in_=R2[:, 4112:8224])], reads=r_oa + r_ob, n_dma=2)
            guard(r_R1)
            SCRb = SCR[:, :].bitcast(BF16)
            sgt = {("ga", 0): SCRb[:, 0:514], ("ga", 1): SCRb[:, 514:1028], ("gb", 0): SCRb[:, 1028:1542], ("gb", 1): SCRb[:, 1542:2056]}
            r_sg = {k: Res("sg%s%d" % k) for k in sgt}
            t1 = SCR[:, 1032:1546]
            r_t1 = Res("t1")
            guard(r_SCR)
            for cp in range(8):
                for nm, wc0 in (("ga", 4608), ("gb", 6656)):
                    wv, rw = wload(w_in[wc0 // 256 + cp, :, :], 16, 256)
                    for cc in range(2):
                        dst, rd = sgt[(nm, cc)], r_sg[(nm, cc)]

                        def cons(ps, c0, c1, rbk, dst=dst, rd=rd):
                            S.op("act", lambda e: e.activation(out=dst[:, c0:c1], in_=ps, func=AF.Sigmoid), reads=[rbk, r_SCR], writes=[rd])
                        fm_proj(lambda k, wv=wv, cc=cc: wv[:, k, 128 * cc:128 * cc + 128], 16,
                                lambda k, c0, c1: h0T[:, k, Q0 + c0:Q0 + c1], NQ, [r_hTq, rw], cons, main_banks=(0, 1))
                wva, rwa = wload(w_pa[cp, :, :], 8, 256)
                wvb, rwb = wload(w_pb[cp, :, :], 8, 256)
                for cc in range(2):
                    c = 2 * cp + cc
                    sa, ra = sgt[("ga", cc)], r_sg[("ga", cc)]
                    sb_, rb_ = sgt[("gb", cc)], r_sg[("gb", cc)]

                    def consa(ps, c0, c1, rbk, sa=sa, ra=ra):
                        S.op("dve", lambda e: e.tensor_tensor(out=t1[:, c0:c1], in0=ps, in1=sa[:, c0:c1], op=ALU.mult),
                             reads=[rbk, ra, r_SCR], writes=[r_t1])
                    fm_proj(lambda k, wva=wva, cc=cc: wva[:, k, 128 * cc:128 * cc + 128], 8,
                            lambda k, c0, c1: oaT[:, k, c0:c1], NQ, r_oa + [rwa, r_R2], consa, main_banks=(2,), tail_banks=(4,))

                    def consb(ps, c0, c1, rbk, c=c, sb_=sb_, rb_=rb_):
                        S.op("dve", lambda e: e.tensor_tensor(out=ps, in0=ps, in1=sb_[:, c0:c1], op=ALU.mult),
                             reads=[rb_, r_SCR], writes=[rbk])
                        S.op("dve", lambda e: e.tensor_tensor(out=mergedT[:, c, c0:c1], in0=ps, in1=t1[:, c0:c1], op=ALU.add),
                             reads=[rbk, r_t1, r_R1, r_SCR], writes=[r_mg[c]])
                    fm_proj(lambda k, wvb=wvb, cc=cc: wvb[:, k, 128 * cc:128 * cc + 128], 8,
                            lambda k, c0, c1: obT[:, k, c0:c1], NQ, r_ob + [rwb, r_R2], consb, main_banks=(3,), tail_banks=(5,))
            guard(r_SCR)

            if dbg and i == 0:
                S.dma('sp', 'dbg', lambda e: e.dma_start(out=DBG['mg'][:, :], in_=R1[:, 0:8224]), reads=r_mg + [r_R1])
            load_lnp(1)
            guard(r_R2)
            for cb in range(4):
                pcs = []
                for q in range(2):
                    pcs.append(wload(w_out[2 * cb + q, :, :], 8, 512))
                    wv, rw = pcs[q]

                    def mm(e, q=q, wv=wv):
                        last = None
                        for j in range(4):
                            for kk in range(8):
                                k = 8 * q + kk
                                last = e.matmul(banks[j][:, :], lhsT=mergedT[:, k, 1 + 128 * j:129 + 128 * j], rhs=wv[:, kk, :],
                                                start=(k == 0), stop=(k == 15), skip_group_check=True)
                        return last
                    S.op("pe", mm, reads=r_mg + [rw, r_R1], writes=[rb[0], rb[1], rb[2], rb[3]])
                for j in range(4):
                    S.op("dve", lambda e, j=j, cb=cb: e.scalar_tensor_tensor(
                        out=h0[:, j, 512 * cb:512 * cb + 512], in0=h0[:, j, 512 * cb:512 * cb + 512], scalar=ALPHA,
                        in1=banks[j][:, :], op0=ALU.mult, op1=ALU.add), reads=[rb[j]], writes=[r_h0[j]])
                hps = banks[5]

                def hmm(e, cb=cb, pcs=pcs):
                    last = None
                    for cc in range(4):
                        for k in range(16):
                            wv = pcs[k // 8][0]
                            last = e.matmul(hps[:, 2 * cc:2 * cc + 2], lhsT=wv[:, k % 8, 128 * cc:128 * cc + 128],
                                            rhs=mergedT[:, k, 0:NQ:NQ - 1], start=(k == 0 and cc == 0), stop=(k == 15),
                                            skip_group_check=True)
                    return last
                S.op("pe", hmm, reads=r_mg + [pcs[0][1], pcs[1][1], r_R1], writes=[rb[5]])
                S.op("dve", lambda e, cb=cb: e.scalar_tensor_tensor(
                    out=zh[:, 4 * cb:4 * cb + 4, :], in0=h0h[:, 4 * cb:4 * cb + 4, :], scalar=ALPHA,
                    in1=hps[:, 0:8].rearrange("p (c t) -> p c t", c=4), op0=ALU.mult, op1=ALU.add),
                    reads=[rb[5], r_h0h], writes=[r_zh])
            tickstate["g"] = ln_feat_gen(zh[:], lnT[:, 32:48], lnT[:, 48:64], zh[:], [r_zh], [r_zh], 5)
            for j in range(4):
                hi = lnstate["n"] % 2
                hnb, r_hnx = hnbufs[hi], r_hnb[hi]
                gi = {"r_src": [r_h0[j]], "r_dst": [r_h0[j]], "r_bf": [r_hnx], "extra": [r_R2]}
                ln_tok(h0[:, j, :], 128, gi, out_f32=h0[:, j, :], out_bf=hnb[:, :])
                c0 = Q0 + 1 + 128 * j
                transpose_to(128, lambda k0, nk, c0=c0: h0T[:, k0:k0 + nk, c0:c0 + 128], [r_R2], [r_hTq], hnb=hnb, r_hnx=r_hnx)
            drain()

            def hcols(e):
                e.tensor_copy(out=h0T[:, :, Q0], in_=zh[:, :, 0])
                return e.tensor_scalar(out=h0T[:, :, Q0 + NQ - 1], in0=zh[:, :, 1], scalar1=flag[:, i:i + 1], scalar2=None, op0=ALU.mult)
            S.op("dve", hcols, reads=[r_zh, r_const], writes=[r_hTq])

            if dbg and i == 0:
                S.dma('sp', 'dbg', lambda e: [e.dma_start(out=DBG['h1'][128 * j:128 * j + 128, :], in_=h0[:, j, :]) for j in range(4)], reads=r_h0, n_dma=4)
            guard(r_R1)
            Araw2 = [SCR[:, 0:514], SCR[:, 514:1028]]
            cbuf6 = SCR[:, 1028:1540]
            vbuf6 = PT[0][:, 0:512]
            gbuf6 = PT[1][:, 0:512]
            r_Araw = [Res("Araw0"), Res("Araw1")]
            r_cbuf = Res("cbuf")
            guard(r_SCR)
            sched6 = {1: [("ln", 0)], 4: [("tr", 0)], 5: [("ln", 1)], 8: [("tr", 1)], 9: [("ln", 8)], 12: [("tr", 8)],
                      13: [("ln", 9)], 16: [("tr", 9)]}
            if i + 1 < ntiles:
                guard(r_R2)
            gpiece, vpiece = {}, {}

            def emit_gate(c):
                cp, cc = c // 2, c % 2
                if cc == 0:
                    gpiece[cp] = wload(w_fi[cp, :, :], 16, 256)
                wg, rwg = gpiece[cp]
                Ar, rA = Araw2[c % 2], r_Araw[c % 2]

                def consg(ps, c0, c1, rbk):
                    if c0 == 0:
                        S.op("act", lambda e: e.copy(out=Ar[:, c0:c1], in_=ps), reads=[rbk, r_SCR], writes=[rA])
                    else:
                        S.op("act", lambda e: e.copy(out=Ar[:, c0:c1], in_=ps), reads=[rbk, r_SCR], cowrites=[rA])
                fm_proj(lambda k: wg[:, k, cc * 128:cc * 128 + 128], 16,
                        lambda k, c0, c1: h0T[:, k, Q0 + c0:Q0 + c1], NQ, [r_hTq, rwg], consg)

            def emit_val(c):
                cp, cc = c // 2, c % 2
                wvv, rwv = vpiece[cp]

                def consv(ps, c0, c1, rbk):
                    lo, hi = max(c0, 1), min(c1, 513)
                    if c0 == 0:
                        S.op("act", lambda e: e.copy(out=vbuf6[:, lo - 1:hi - 1], in_=ps[:, lo - c0:hi - c0]), reads=[rbk], writes=[r_PT[0]])
                    else:
                        S.op("act", lambda e: e.copy(out=vbuf6[:, lo - 1:hi - 1], in_=ps[:, lo - c0:hi - c0]), reads=[rbk], cowrites=[r_PT[0]])
                fm_proj(lambda k: wvv[:, k, cc * 128:cc * 128 + 128], 16,
                        lambda k, c0, c1: h0T[:, k, Q0 + c0:Q0 + c1], NQ, [r_hTq, rwv], consv)

            emit_gate(0)
            for c in range(NCH):
                cp = c // 2
                if c % 2 == 0:
                    if i + 1 < ntiles:
                        for (act_, w) in sched6.get(cp, []):
                            if act_ == "ln":
                                s1_ln(i + 1, w, only_a=True)
                            else:
                                s1_tr(i + 1, w)
                    vpiece[cp] = wload(w_fi[NCH // 2 + cp, :, :], 16, 256)
                w0 = convw[:, 0 * NCH + c:0 * NCH + c + 1]
                w1 = convw[:, 1 * NCH + c:1 * NCH + c + 1]
                w2 = convw[:, 2 * NCH + c:2 * NCH + c + 1]
                bb = convw[:, 3 * NCH + c:3 * NCH + c + 1]
                Ar, rA = Araw2[c % 2], r_Araw[c % 2]
                S.op("act", lambda e, w1=w1, bb=bb, Ar=Ar: e.activation(out=cbuf6[:, :], in_=Ar[:, 1:513], func=AF.Identity, bias=bb, scale=w1),
                     reads=[rA, r_const, r_SCR], writes=[r_cbuf])
                S.op("dve", lambda e, w0=w0, Ar=Ar: e.scalar_tensor_tensor(out=cbuf6[:, :], in0=Ar[:, 0:512], scalar=w0, in1=cbuf6[:, :], op0=ALU.mult, op1=ALU.add),
                     reads=[rA, r_const, r_SCR], writes=[r_cbuf])
                S.op("dve", lambda e, w2=w2, Ar=Ar: e.scalar_tensor_tensor(out=cbuf6[:, :], in0=Ar[:, 2:514], scalar=w2, in1=cbuf6[:, :], op0=ALU.mult, op1=ALU.add),
                     reads=[rA, r_const, r_SCR], writes=[r_cbuf])
                emit_val(c)
                if c + 1 < NCH:
                    emit_gate(c + 1)
                S.op("act", lambda e: e.activation(out=gbuf6[:, :], in_=cbuf6[:, :], func=AF.Gelu_apprx_tanh), reads=[r_cbuf], writes=[r_PT[1]])
                S.op("dve", lambda e, c=c: e.tensor_tensor(out=uT[:, c, 1:513], in0=gbuf6[:, :], in1=vbuf6[:, :], op=ALU.mult),
                     reads=[r_PT[0], r_PT[1], r_R1], writes=[r_uT[c]])

            if dbg and i == 0:
                S.dma('sp', 'dbg', lambda e: e.dma_start(out=DBG['uT'][:, :], in_=R1[:, 0:22616]), reads=r_uT + [r_R1])
            load_lnp(2)
            sched7 = {0: [("ln", 2)], 1: [("ln", 7)], 3: [("tr", 2)], 4: [("ln", 3)], 6: [("tr", 7)], 7: [("ln", 4)],
                      9: [("tr", 3)], 10: [("ln", 5)], 12: [("tr", 4)], 13: [("ln", 6)], 15: [("tr", 5)], 18: [("tr", 6)]}
            for cb in range(4):
                ksz = [8, 8, 8, 8, 8, 4]
                for q in range(6):
                    if i + 1 < ntiles:
                        for (act_, w) in sched7.get(6 * cb + q, []):
                            if act_ == "ln":
                                s1_ln(i + 1, w)
                            else:
                                s1_tr(i + 1, w)
                    wv, rw = wload(w_fd[6 * cb + q, :, 0:ksz[q] * 512], ksz[q], 512)

                    def mm(e, q=q, wv=wv):
                        last = None
                        for j in range(4):
                            for kk in range(ksz[q]):
                                k = 8 * q + kk
                                last = e.matmul(banks[j][:, :], lhsT=uT[:, k, 1 + 128 * j:129 + 128 * j], rhs=wv[:, kk, :],
                                                start=(k == 0), stop=(k == NCH - 1), skip_group_check=True)
                        return last
                    S.op("pe", mm, reads=r_uT + [rw, r_R1], writes=[rb[0], rb[1], rb[2], rb[3]])
                for j in range(4):
                    S.op("dve", lambda e, j=j, cb=cb: e.scalar_tensor_tensor(
                        out=h0[:, j, 512 * cb:512 * cb + 512], in0=h0[:, j, 512 * cb:512 * cb + 512], scalar=ALPHA,
                        in1=banks[j][:, :], op0=ALU.mult, op1=ALU.add), reads=[rb[j]], writes=[r_h0[j]])
            if i + 1 < ntiles:
                guard(r_R1, "act")
                for hp in range(4):
                    stage_KA(hp)
                    stage_ln2(i, hp)
            else:
                for j in range(4):
                    stage_ln2(i, j)

        setup()
        for i in range(ntiles):
            tile(i)
        S.emit()
    return nc


def _static_tables():
    kp = np.arange(128) // 64
    kc = np.arange(128) % 64
    qp = kp.copy()
    qc = kc.copy()
    col_start = np.clip(qc - 8, 0, 48)
    col_ok = (kc[:, None] >= col_start[None, :]) & (kc[:, None] < col_start[None, :] + 16)
    dc = np.clip(kc[:, None] - qc[None, :], -15, 15) + 15
    dr_idx = np.zeros((7, 128, 128), np.int64)
    nab_m = np.zeros((7, 128, 128), np.float32)
    for b in range(7):
        d = 5 - b
        dr = 2 * (d - 2) + kp[:, None] - qp[None, :]
        ok = (np.abs(dr) <= 7) & col_ok
        dr_idx[b] = np.clip(dr, -7, 7) + 7
        nab_m[b] = np.where(ok, 0.0, NEG)
    slopes = 2.0 ** (-(np.arange(1, 9, dtype=np.float64)))
    swb = np.zeros((8, 3, 128, 128), np.float32)
    k = np.arange(128)
    for b in range(3):
        dd = 1 - b
        dist = np.abs(128 * dd + k[:, None] - k[None, :])
        for h in range(8):
            swb[h, b] = np.where(dist <= 128, -slopes[h] * dist, NEG)
    lrow = (np.arange(NW)[None, :] // 64 == np.arange(18)[:, None]).astype(np.float32)
    return dr_idx, dc, nab_m, swb, lrow


def _core_geom(c):
    if c < 4:
        return ("p", 0, TOK * c, 16384)
    return ("s", c - 4, 0, 4096)


def _core_tables(s, tseq):
    rows_seq = tseq // 64
    rna = np.full((NTILES, 18, NQ), NEG, np.float32)
    fsw = np.zeros((128, NTILES * 9), np.float32)
    flag = np.zeros((128, NTILES), np.float32)
    for i in range(NTILES):
        t0 = s + TT * i
        R0 = t0 // 64
        for qcol in range(NQ):
            if qcol == 0:
                rq = R0 - 1
            elif qcol == NQ - 1:
                rq = R0 + 8
            else:
                rq = R0 + (qcol - 1) // 64
            if rq < 0 or rq >= rows_seq:
                continue
            rs = min(max(rq - 4, 0), rows_seq - 8)
            for rr in range(18):
                rk = R0 + rr - 6
                if rs <= rk < rs + 8:
                    rna[i, rr, qcol] = 0.0
        for kb in range(9):
            blk = t0 // 128 + kb - 3
            if blk < 0 or blk >= tseq // 128:
                fsw[:, i * 9 + kb] = NEG
        flag[:, i] = 1.0 if t0 + TT < tseq else 0.0
    return rna, fsw, flag


def _pieces(w, nk, cols):
    C = w.shape[1]
    return np.ascontiguousarray(w.reshape(nk, 128, C // cols, cols).transpose(2, 1, 0, 3).reshape(C // cols, 128, nk * cols))


def _kpieces(w, ksz):
    out = np.zeros((4 * len(ksz), 128, 8 * 512), np.float32)
    for cb in range(4):
        k0 = 0
        for q, n in enumerate(ksz):
            blk = w[k0 * 128:(k0 + n) * 128, 512 * cb:512 * cb + 512].reshape(n, 128, 512).transpose(1, 0, 2)
            out[cb * len(ksz) + q, :, :n * 512] = blk.reshape(128, n * 512)
            k0 += n
    return out


def _prep(inputs):
    f = lambda a: np.ascontiguousarray(np.asarray(a, dtype=np.float32))
    xp = f(inputs["x_prompt"])[0]
    xsm = f(inputs["x_sample"])
    meta = f(inputs["meta_tokens"])
    dr_idx, dc, nab_m, swb, lrow = _static_tables()
    rpb = f(inputs["na_rpb"])[0]
    nab_g = rpb[:, dr_idx, dc[None, :, :]]
    nab_g = np.ascontiguousarray(nab_g.transpose(2, 0, 1, 3).reshape(128, 7168))
    nab_mm = np.ascontiguousarray(np.broadcast_to(nab_m[None], (8, 7, 128, 128)).transpose(2, 0, 1, 3).reshape(128, 7168))
    swb2 = np.ascontiguousarray(swb.transpose(2, 0, 1, 3).reshape(128, 3072))
    lnv = np.stack([f(inputs["ln_emb_g"]), f(inputs["ln_emb_b"]), f(inputs["ln1_g"])[0], f(inputs["ln1_b"])[0],
                    f(inputs["ln2_g"])[0], f(inputs["ln2_b"])[0]])
    lnT = np.ascontiguousarray(np.concatenate([v.reshape(16, 128).T for v in lnv[:4]], axis=1))
    cw = f(inputs["ffn_conv_w"])[0]
    cbias = f(inputs["ffn_conv_b"])[0]
    convw = np.ascontiguousarray(np.concatenate([cw[j].reshape(NCH, 128).T for j in range(3)] + [cbias.reshape(NCH, 128).T], axis=1))
    sink = np.ascontiguousarray(np.broadcast_to(f(inputs["sw_sink"])[0][None, :], (128, 8)))
    shared = {
        "meta": meta, "lnv": np.ascontiguousarray(lnv), "lnT": lnT, "convw": convw, "sink": sink,
        "nab_g": nab_g, "nab_m": nab_mm, "swb": swb2, "lrow": np.ascontiguousarray(lrow),
        "w_in": _pieces(f(inputs["w_in"])[0], 16, 256), "w_pa": _pieces(f(inputs["w_proj_na"])[0], 8, 256),
        "w_pb": _pieces(f(inputs["w_proj_sw"])[0], 8, 256), "w_out": _kpieces(f(inputs["w_out"])[0], [8, 8]),
        "w_fi": _pieces(f(inputs["w_ffn_in"])[0], 16, 256), "w_fd": _kpieces(f(inputs["w_ffn_down"])[0], [8, 8, 8, 8, 8, 4]),
    }
    in_maps = []
    for c in range(8):
        kind, b, s, tseq = _core_geom(c)
        xsrc = xp if kind == "p" else xsm[b]
        xe = np.zeros((XE_ROWS, D), np.float32)
        lo, hi = s - NB, s - NB + XE_ROWS
        a0, a1 = max(lo, 0), min(hi, tseq)
        xe[a0 - lo:a1 - lo] = xsrc[a0:a1]
        if s == 0:
            xe[NB - 1] = meta[NMETA - 1]
        xh = np.zeros((NTILES, 128, 16, 2), np.float32)
        for i in range(NTILES):
            for t, row in enumerate((NB + TT * i - 1, NB + TT * i + TT)):
                xh[i, :, :, t] = xe[row].reshape(16, 128).T
        rna, fsw, flag = _core_tables(s, tseq)
        m = dict(shared)
        m.update({"xe": xe, "xh": np.ascontiguousarray(xh.reshape(NTILES, 128, 32)), "rna": rna, "fsw": fsw, "flag": flag})
        in_maps.append(m)
    return in_maps


_NC_CACHE = {}


def kernel(**inputs):
    in_maps = _prep(inputs)
    if "nc" not in _NC_CACHE:
        _NC_CACHE["nc"] = build_nc()
    res = run_bass_kernel_spmd(_NC_CACHE["nc"], in_maps, core_ids=list(range(8)))
    ys = [np.asarray(r["y"], dtype=np.float32) for r in res.results]
    y_prompt = np.concatenate(ys[0:4], axis=0)[None]
    y_sample = np.stack(ys[4:8], axis=0)
    return (y_prompt, y_sample)
```
